# Optimizing a Trainium2 kernel written in Bass

```python
import math
import jax, jax.numpy as jnp
from jax import lax
import numpy as np

D_MODEL = 1024
BATCH = 2
SEQ = 16384
DEPTH = 4

N_MIXERS = 4
D_FF = 2816
N_MOD = 9
EPS = 1e-6
Q_BLOCK = 128
GRID_W = 64

POOL_WINDOWS = (2, 4, 8, 16)
N_POOL = len(POOL_WINDOWS)
POOL_GROUP = D_MODEL // N_POOL

DIFF_HEADS = 8
DIFF_HEAD_DIM = D_MODEL // DIFF_HEADS // 2
DIFF_V_DIM = 2 * DIFF_HEAD_DIM
ROPE_THETA = 500000.0
ROT_DIM = DIFF_HEAD_DIM // 4

GQA_HEADS = 8
GQA_KV_HEADS = 2
GQA_HEAD_DIM = D_MODEL // GQA_HEADS
GQA_GROUP = GQA_HEADS // GQA_KV_HEADS
GQA_Q_DIM = GQA_HEADS * GQA_HEAD_DIM
GQA_KV_DIM = GQA_KV_HEADS * GQA_HEAD_DIM
AXIAL_THETA = 10000.0
AXIAL_DIM = GQA_HEAD_DIM // 2

CONV_WIDTH = 3

kernel_name = "hybrid_interleaved_adaln_encoder"


def rms_norm(x, g):
    xf = x.astype(jnp.float32)
    y = xf * lax.rsqrt(jnp.mean(xf * xf, axis=-1, keepdims=True) + EPS)
    return (y * g.astype(jnp.float32)).astype(x.dtype)


def modulate(h, shift, scale):
    return h * (1 + scale) + shift


def swiglu(h, w_gu, w_down):
    g, u = jnp.split(h @ w_gu, 2, axis=-1)
    return (jax.nn.silu(g) * u) @ w_down


def rope_tables(pos, dim, theta):
    inv = 1.0 / (theta ** (jnp.arange(0, dim, 2, dtype=jnp.float32) / dim))
    ang = pos.astype(jnp.float32)[:, None] * inv[None, :]
    return jnp.cos(ang), jnp.sin(ang)


def rope(x, cos, sin):
    half = x.shape[-1] // 2
    xf = x.astype(jnp.float32)
    x1, x2 = xf[..., :half], xf[..., half:]
    c = cos[None, :, None, :]
    s = sin[None, :, None, :]
    return jnp.concatenate([x1 * c - x2 * s, x2 * c + x1 * s], axis=-1).astype(x.dtype)


def sweep_query_blocks(fn, q):
    b, s = q.shape[:2]
    nblk = s // Q_BLOCK
    qb = jnp.moveaxis(q.reshape((b, nblk, Q_BLOCK) + q.shape[2:]), 1, 0)
    out = jnp.moveaxis(lax.map(fn, qb), 0, 1)
    return out.reshape((b, s) + out.shape[3:])


def pool_mixer(h, pool_w, pool_scale):
    b, s, d = h.shape
    cs = jnp.concatenate([jnp.zeros((b, 1, d), jnp.float32),
                          jnp.cumsum(h.astype(jnp.float32), axis=1)], axis=1)
    t = jnp.arange(s)
    diffs = []
    for g, win in enumerate(POOL_WINDOWS):
        lo = jnp.clip(t - win // 2, 0, s)
        hi = jnp.clip(t + win // 2, 0, s)
        csg = cs[..., g * POOL_GROUP:(g + 1) * POOL_GROUP]
        mean = (jnp.take(csg, hi, axis=1) - jnp.take(csg, lo, axis=1)) / (hi - lo).astype(jnp.float32)[None, :, None]
        diffs.append(mean.astype(h.dtype) - h[..., g * POOL_GROUP:(g + 1) * POOL_GROUP])
    dgrp = jnp.stack(diffs, axis=2)
    y = jnp.einsum('bsgc,gce->bsge', dgrp, pool_w).reshape(b, s, d)
    return y * pool_scale


def diff_attention(h, w_qkv, lam, subln_g, w_o, cos, sin, layer_idx):
    b, s, d = h.shape
    q, k, v = jnp.split(h @ w_qkv, 3, axis=-1)
    q = q.reshape(b, s, 2 * DIFF_HEADS, DIFF_HEAD_DIM)
    k = k.reshape(b, s, 2 * DIFF_HEADS, DIFF_HEAD_DIM)
    v = v.reshape(b, s, DIFF_HEADS, DIFF_V_DIM)
    q = jnp.concatenate([rope(q[..., :ROT_DIM], cos, sin), q[..., ROT_DIM:]], axis=-1)
    k = jnp.concatenate([rope(k[..., :ROT_DIM], cos, sin), k[..., ROT_DIM:]], axis=-1)
    q = (q * DIFF_HEAD_DIM ** -0.5).reshape(b, s, DIFF_HEADS, 2, DIFF_HEAD_DIM)
    k = k.reshape(b, s, DIFF_HEADS, 2, DIFF_HEAD_DIM)
    lam_init = 0.8 - 0.6 * math.exp(-0.3 * layer_idx)
    lf = lam.astype(jnp.float32)
    lam_full = jnp.exp(jnp.sum(lf[0] * lf[1])) - jnp.exp(jnp.sum(lf[2] * lf[3])) + lam_init

    def block(qb):
        sc = jnp.einsum('bqhcd,bkhcd->bhcqk', qb, k, preferred_element_type=jnp.float32)
        p = jax.nn.softmax(sc, axis=-1)
        a = p[:, :, 0] - lam_full * p[:, :, 1]
        return jnp.einsum('bhqk,bkhe->bqhe', a.astype(v.dtype), v)

    o = sweep_query_blocks(block, q)
    o = rms_norm(o, subln_g) * (1 - lam_init)
    return o.reshape(b, s, d) @ w_o


def gqa_axial_attention(h, w_qkv, q_norm_g, k_norm_g, w_o, cos_r, sin_r, cos_c, sin_c):
    b, s, d = h.shape
    qkv = h @ w_qkv
    q = qkv[..., :GQA_Q_DIM].reshape(b, s, GQA_HEADS, GQA_HEAD_DIM)
    k = qkv[..., GQA_Q_DIM:GQA_Q_DIM + GQA_KV_DIM].reshape(b, s, GQA_KV_HEADS, GQA_HEAD_DIM)
    v = qkv[..., GQA_Q_DIM + GQA_KV_DIM:].reshape(b, s, GQA_KV_HEADS, GQA_HEAD_DIM)
    q = rms_norm(q, q_norm_g)
    k = rms_norm(k, k_norm_g)

    def axial(t):
        return jnp.concatenate([rope(t[..., :AXIAL_DIM], cos_r, sin_r),
                                rope(t[..., AXIAL_DIM:], cos_c, sin_c)], axis=-1)

    q = (axial(q) * GQA_HEAD_DIM ** -0.5).reshape(b, s, GQA_KV_HEADS, GQA_GROUP, GQA_HEAD_DIM)
    k = axial(k)

    def block(qb):
        sc = jnp.einsum('bqgrd,bkgd->bgrqk', qb, k, preferred_element_type=jnp.float32)
        p = jax.nn.softmax(sc, axis=-1)
        return jnp.einsum('bgrqk,bkgd->bqgrd', p.astype(v.dtype), v)

    o = sweep_query_blocks(block, q)
    return o.reshape(b, s, d) @ w_o


def short_conv_mixer(h, w_in, w_conv, w_out):
    d = h.shape[-1]
    gb, gc, u = jnp.split(h @ w_in, 3, axis=-1)
    z = gc * u
    zc = lax.conv_general_dilated(z, w_conv[:, None, :], window_strides=(1,),
                                  padding=[((CONV_WIDTH - 1) // 2, (CONV_WIDTH - 1) // 2)],
                                  dimension_numbers=('NWC', 'WIO', 'NWC'),
                                  feature_group_count=d)
    return (gb * zc) @ w_out


def setup_inputs(seed: int = 0) -> dict:
    key = jax.random.key(seed)
    ks = jax.random.split(key, 24)

    def nrm(k, shape, scale):
        return jax.random.normal(k, shape, jnp.float32) * scale

    D = D_MODEL
    return {
        "x": nrm(ks[0], (BATCH, SEQ, D), 1.0),
        "c": nrm(ks[1], (BATCH, D), 1.0),
        "mod_w": nrm(ks[2], (DEPTH, D, N_MOD * D), 0.5 * D ** -0.5),
        "mod_b": nrm(ks[3], (DEPTH, N_MOD * D), 0.02),
        "norm_g": 1.0 + nrm(ks[4], (DEPTH, 3, D), 0.1),
        "ffn_w_gu": nrm(ks[5], (DEPTH, 2, D, 2 * D_FF), D ** -0.5),
        "ffn_w_down": nrm(ks[6], (DEPTH, 2, D_FF, D), D_FF ** -0.5),
        "pool_w": nrm(ks[7], (N_POOL, POOL_GROUP, POOL_GROUP), POOL_GROUP ** -0.5),
        "pool_scale": 1.0 + nrm(ks[8], (D,), 0.1),
        "diff_w_qkv": nrm(ks[9], (D, 3 * D), D ** -0.5),
        "diff_lambda": nrm(ks[10], (4, DIFF_HEAD_DIM), 0.1),
        "diff_subln_g": 1.0 + nrm(ks[11], (DIFF_V_DIM,), 0.1),
        "diff_w_o": nrm(ks[12], (D, D), D ** -0.5),
        "gqa_w_qkv": nrm(ks[13], (D, GQA_Q_DIM + 2 * GQA_KV_DIM), D ** -0.5),
        "gqa_q_norm_g": 1.0 + nrm(ks[14], (GQA_HEAD_DIM,), 0.1),
        "gqa_k_norm_g": 1.0 + nrm(ks[15], (GQA_HEAD_DIM,), 0.1),
        "gqa_w_o": nrm(ks[16], (D, D), D ** -0.5),
        "conv_w_in": nrm(ks[17], (D, 3 * D), D ** -0.5),
        "conv_w": nrm(ks[18], (CONV_WIDTH, D), CONV_WIDTH ** -0.5),
        "conv_w_out": nrm(ks[19], (D, D), D ** -0.5),
        "final_g": 1.0 + nrm(ks[20], (D,), 0.1),
    }


def reference(x, c, mod_w, mod_b, norm_g, ffn_w_gu, ffn_w_down, pool_w, pool_scale,
              diff_w_qkv, diff_lambda, diff_subln_g, diff_w_o,
              gqa_w_qkv, gqa_q_norm_g, gqa_k_norm_g, gqa_w_o,
              conv_w_in, conv_w, conv_w_out, final_g):
    s = x.shape[1]
    t = jnp.arange(s)
    cos1, sin1 = rope_tables(t, ROT_DIM, ROPE_THETA)
    rows = s // GRID_W
    row_pos = jnp.broadcast_to(jnp.arange(rows)[:, None], (rows, GRID_W)).reshape(-1)
    col_pos = jnp.broadcast_to(jnp.arange(GRID_W)[None, :], (rows, GRID_W)).reshape(-1)
    cos_r, sin_r = rope_tables(row_pos, AXIAL_DIM, AXIAL_THETA)
    cos_c, sin_c = rope_tables(col_pos, AXIAL_DIM, AXIAL_THETA)

    c_act = jax.nn.silu(c)
    for i in range(DEPTH):
        mod = (c_act @ mod_w[i] + mod_b[i])[:, None, :]
        sh0, sc0, g0, sh1, sc1, g1, sh2, sc2, g2 = jnp.split(mod, N_MOD, axis=-1)

        h = modulate(rms_norm(x, norm_g[i, 0]), sh0, sc0)
        x = x + 0.5 * g0 * swiglu(h, ffn_w_gu[i, 0], ffn_w_down[i, 0])

        h = modulate(rms_norm(x, norm_g[i, 1]), sh1, sc1)
        kind = i % N_MIXERS
        if kind == 0:
            y = pool_mixer(h, pool_w, pool_scale)
        elif kind == 1:
            y = diff_attention(h, diff_w_qkv, diff_lambda, diff_subln_g, diff_w_o, cos1, sin1, i)
        elif kind == 2:
            y = gqa_axial_attention(h, gqa_w_qkv, gqa_q_norm_g, gqa_k_norm_g, gqa_w_o,
                                    cos_r, sin_r, cos_c, sin_c)
        else:
            y = short_conv_mixer(h, conv_w_in, conv_w, conv_w_out)
        x = x + g1 * y

        h = modulate(rms_norm(x, norm_g[i, 2]), sh2, sc2)
        x = x + 0.5 * g2 * swiglu(h, ffn_w_gu[i, 1], ffn_w_down[i, 1])

    return rms_norm(x, final_g)
```

```python
import numpy as np
from contextlib import ExitStack
import concourse.bass as bass
import concourse.mybir as mybir
from concourse.bass_utils import run_bass_kernel_spmd

F32 = mybir.dt.float32
BF16 = mybir.dt.bfloat16
ALU = mybir.AluOpType
AF = mybir.ActivationFunctionType

D = 1024
DFF = 2816
NFC = DFF // 128
NL = 4
SEQ = 16384
NCORE = 8
TT = 512
EPS = 1e-6


class Buf:
    __slots__ = ("w", "r", "name")

    def __init__(self, name=""):
        self.w = None
        self.r = {}
        self.name = name


class Prog:
    CENG = ["pe", "dve", "act", "pool"]
    DMAQ = ["sp", "pool"]
    R = 6

    def __init__(self, nc, stack, same_engine_sync=True):
        self.nc = nc
        self.same = same_engine_sync
        self.streams = {e: [] for e in ["pe", "dve", "act", "pool", "sp"]}
        self.ncomp = {e: 0 for e in self.CENG}
        self.comp_ops = {e: [] for e in self.CENG}
        self.ndma = {q: 0 for q in self.DMAQ}
        self.sem = {e: stack.enter_context(nc.semaphore("s_" + e)) for e in self.CENG}
        self.dsem = {q: [stack.enter_context(nc.semaphore(f"d_{q}{i}")) for i in range(self.R)]
                     for q in self.DMAQ}
        self.all_dma = []
        self.base_dep = None
        self.csem = []
        self.stack = stack

    def _deps(self, reads, writes):
        deps = set()
        if self.base_dep is not None:
            deps.add(self.base_dep)
        for b in reads:
            if b.w is not None:
                deps.add(b.w)
        for b in writes:
            if b.w is not None:
                deps.add(b.w)
            for k, v in b.r.items():
                if isinstance(k, tuple):
                    deps.add(k)
                else:
                    deps.add(("c", k, v))
        return deps

    def _mark(self, tok, reads, writes):
        for b in reads:
            if tok[0] == "c":
                b.r[tok[1]] = tok[2]
            else:
                b.r[tok] = True
        for b in writes:
            b.w = tok
            b.r = {}

    def op(self, eng, fn, reads=(), writes=()):
        self.total_ops = getattr(self, "total_ops", 0) + 1
        if self.total_ops > getattr(self, "limit", 10 ** 9):
            return None
        idx = self.ncomp[eng]
        self.ncomp[eng] += 1
        deps = self._deps(reads, writes)
        tok = ("c", eng, idx)
        o = dict(kind="c", fn=fn, deps=deps, inc=False, idx=idx)
        self.comp_ops[eng].append(o)
        self.streams[eng].append(o)
        self._mark(tok, reads, writes)
        return tok

    def dma(self, q, fn, reads=(), writes=()):
        j = self.ndma[q]
        self.ndma[q] += 1
        deps = self._deps(reads, writes)
        tok = ("d", q, j)
        o = dict(kind="d", fn=fn, deps=deps, q=q, j=j)
        self.streams[q].append(o)
        self.all_dma.append(tok)
        self._mark(tok, reads, writes)
        return tok

    def cc(self, fn, reads=(), writes=()):
        k = len(self.csem)
        self.csem.append(self.stack.enter_context(self.nc.semaphore(f"cc{k}")))
        deps = self._deps(reads, writes)
        tok = ("s", k)
        o = dict(kind="s", fn=fn, deps=deps, k=k)
        self.streams["pool"].append(o)
        self.all_dma.append(tok)
        self._mark(tok, reads, writes)
        return tok

    def finalize(self):
        for e, ops in self.streams.items():
            for o in ops:
                for d in o["deps"]:
                    if d[0] == "c":
                        if d[1] == e and o["kind"] == "c" and (not self.same or e == "pe"):
                            continue
                        self.comp_ops[d[1]][d[2]]["inc"] = True
        self.val = {}
        for e in self.CENG:
            c = 0
            vals = []
            for o in self.comp_ops[e]:
                if o["inc"]:
                    c += 1
                vals.append(c)
            self.val[e] = vals

    def emit(self, eng, h):
        seen = {}
        R = self.R
        nwait = 0
        for o in self.streams[eng]:
            needs = {}
            for d in o["deps"]:
                if d[0] == "c":
                    if d[1] == eng and o["kind"] == "c" and (not self.same or eng == "pe"):
                        continue
                    s = self.sem[d[1]]
                    v = self.val[d[1]][d[2]]
                    key = ("c", d[1])
                elif d[0] == "s":
                    s = self.csem[d[1]]
                    v = 1
                    key = ("s", d[1])
                else:
                    s = self.dsem[d[1]][d[2] % R]
                    v = 16 * (d[2] // R + 1)
                    key = ("d", d[1], d[2] % R)
                if seen.get(key, 0) >= v:
                    continue
                if key not in needs or needs[key][1] < v:
                    needs[key] = (s, v)
            if o["kind"] == "d":
                j = o["j"]
                if j >= R:
                    key = ("d", o["q"], j % R)
                    v = 16 * (j // R)
                    if seen.get(key, 0) < v and (key not in needs or needs[key][1] < v):
                        needs[key] = (self.dsem[o["q"]][j % R], v)
            for key, (s, v) in needs.items():
                h.wait_ge(s, v)
                seen[key] = v
                nwait += 1
            ins = o["fn"](h)
            if o["kind"] == "c":
                if o["inc"]:
                    ins.then_inc(self.sem[eng], 1)
            elif o["kind"] == "s":
                ins.then_inc(self.csem[o["k"]], 1)
            else:
                ins.then_inc(self.dsem[o["q"]][o["j"] % R], 16)
        return nwait

    def final_waits(self, eng, h, toks, q):
        R = self.R
        for d in toks:
            if d[0] == "d" and d[1] == q:
                h.wait_ge(self.dsem[d[1]][d[2] % R], 16 * (d[2] // R + 1))


def const_layout():
    off = {}
    n = 0

    def add(name, w):
        nonlocal n
        off[name] = n
        n += w
    add("c", 8)
    for l in range(NL):
        add(f"modb{l}", 72)
        add(f"ng{l}", 24)
    add("final_g", 8)
    add("pool_scale", 8)
    add("conv_w", 24)
    add("gqa_qg", 1)
    add("gqa_kg", 1)
    add("subln_g", 1)
    add("gqa_qg_sw", 1)
    add("gqa_kg_sw", 1)
    add("maskL", 1)
    add("maskR", 1)
    add("selL", 4)
    add("selR", 4)
    return off, n


COFF, NCONST = const_layout()


def col128(v):
    v = np.asarray(v, np.float32).reshape(-1, 128)
    return np.ascontiguousarray(v.T)


def build_consts(inp, b):
    cs = np.zeros((128, NCONST), np.float32)
    cs[:, COFF["c"]:COFF["c"] + 8] = col128(inp["c"][b])
    for l in range(NL):
        cs[:, COFF[f"modb{l}"]:COFF[f"modb{l}"] + 72] = col128(inp["mod_b"][l])
        cs[:, COFF[f"ng{l}"]:COFF[f"ng{l}"] + 24] = col128(inp["norm_g"][l].reshape(-1))
    cs[:, COFF["final_g"]:COFF["final_g"] + 8] = col128(inp["final_g"])
    cs[:, COFF["pool_scale"]:COFF["pool_scale"] + 8] = col128(inp["pool_scale"])
    cs[:, COFF["conv_w"]:COFF["conv_w"] + 24] = col128(inp["conv_w"].reshape(-1))
    cs[:, COFF["gqa_qg"]] = inp["gqa_q_norm_g"]
    cs[:, COFF["gqa_kg"]] = inp["gqa_k_norm_g"]
    cs[:, COFF["subln_g"]] = inp["diff_subln_g"]
    return cs


def relayout_wgu(w):
    w = np.asarray(w, np.float32)
    g = w[:, :DFF].reshape(8, 128, 11, 256)
    u = w[:, DFF:].reshape(8, 128, 11, 256)
    r = np.concatenate([g, u], axis=3)
    r = r.transpose(2, 1, 0, 3)
    return np.ascontiguousarray(r).reshape(11, 128, 8 * 512)


class Region:
    def __init__(self, t, n):
        self.t, self.n, self.o = t, n, 0

    def reset(self):
        self.o = 0

    def take(self, *shape):
        sz = int(np.prod(shape))
        assert self.o + sz <= self.n, ("region overflow", self.o, sz, self.n)
        ap = self.t[:, self.o:self.o + sz]
        self.o += sz
        if len(shape) == 2:
            ap = ap.rearrange("p (a b) -> p a b", a=shape[0])
        return ap


FREG = 5120
BREG = 27648


class Builder:
    def __init__(self, cfg):
        self.cfg = cfg
        self.NT = cfg["ntok"]
        self.H = cfg.get("halo", 0)
        self.NTE = self.NT + 2 * self.H
        self.NK = cfg.get("nkeys", 0)
        self.nc = bass.Bass("TRN2", target_bir_lowering=False)
        self.stack = ExitStack()
        self.P = Prog(self.nc, self.stack, same_engine_sync=cfg.get("same", True))
        self.rr = {}
        self.P.limit = cfg.get("limit", 10 ** 9)
        self.outs = []
        self.fused = cfg.get("fused", False)
        self.cpbt = cfg.get("cpbt", 1)
        self.groups = cfg.get("groups", [[0]])
        self.scr = {}
        self.nhalo = 0
        self.tiles = [(self.H + t0, min(TT, self.NT - t0)) for t0 in range(0, self.NT, TT)]

    def din(self, name, shape, dt=F32):
        return self.nc.dram_tensor(name, list(shape), dt, kind="ExternalInput").ap()

    def dout(self, name, shape, dt=F32):
        return self.nc.dram_tensor(name, list(shape), dt, kind="ExternalOutput").ap()

    def sb(self, name, shape, dt=F32):
        return self.stack.enter_context(self.nc.sbuf_tensor(name, list(shape), dt))

    def dbg(self, name, ap, shape, dt, reads):
        if not self.cfg.get("dbg"):
            return
        o = self.dout("dbg_" + name, shape, dt)
        self.outs.append(self.P.dma("sp", lambda h: h.dma_start(out=o, in_=ap), reads=reads))

    def ring(self, key, n):
        i = self.rr.get(key, 0)
        self.rr[key] = i + 1
        return i % n

    def new_stage(self):
        self.barrier()
        self.FR.reset()
        self.BR.reset()
        self.pb = [Buf(f"psum{i}") for i in range(8)]

    def barrier(self):
        P = self.P
        b = Buf("bar")
        for e in Prog.CENG:
            if P.ncomp[e] > 0:
                b.r[e] = P.ncomp[e] - 1
        for t in P.all_dma:
            b.r[t] = True
        P.all_dma = []
        bt = self.bar_t
        tok = P.op("dve", lambda h: h.memset(bt[:, 0:1], 0.0), writes=[b])
        P.base_dep = tok

    def xbuf_of(self, c0, w):
        bs = []
        for i, (t0, tw) in enumerate(self.tiles):
            if c0 < t0 + tw and c0 + w > t0:
                bs.append(self.xbuf[i])
        if c0 < self.H or c0 + w > self.H + self.NT:
            bs.append(self.xhalo)
        return bs

    def build(self):
        cfg = self.cfg
        nc, P, NT, H, NTE = self.nc, self.P, self.NT, self.H, self.NTE
        stages = cfg["stages"]
        xT_d = self.din("xT", [D, NTE])
        consts_d = self.din("consts", [128, NCONST])

        self.xT = self.sb("xT_sb", [128, 8, NTE])
        self.xbuf = [Buf(f"x{t}") for t in range(len(self.tiles))]
        self.xhalo = Buf("xhalo")
        self.consts = self.sb("consts_sb", [128, NCONST])
        self.cbuf = Buf("consts")
        self.modtab = self.sb("modtab", [128, 480])
        self.modv = self.modtab[:, 0:288]
        self.modA = self.modtab[:, 288:384]
        self.modG = self.modtab[:, 384:480]
        self.mbuf = Buf("mod")
        self.ones = self.sb("ones", [128, 128], BF16)
        self.onesb = Buf("ones")
        self.bar_t = self.sb("bar", [128, 2])
        self.epsb = self.sb("epsb", [128, 1])
        self.FR = Region(self.sb("freg", [128, FREG], F32), FREG)
        self.BR = Region(self.sb("breg", [128, BREG], BF16), BREG)
        self.psum = [self.stack.enter_context(nc.psum_tensor(f"pb{i}", [128, 512], F32))
                     for i in range(8)]
        self.pb = [Buf(f"psum{i}") for i in range(8)]

        P.dma("sp", lambda h: h.dma_start(out=self.consts[:], in_=consts_d[:, :]), writes=[self.cbuf])
        xv = xT_d.rearrange("(c p) n -> p c n", p=128)
        for i, (c0, w) in enumerate(self.tiles):
            a, b = c0, c0 + w
            wr = [self.xbuf[i]]
            if i == 0:
                a = 0
                wr.append(self.xhalo)
            if i == len(self.tiles) - 1:
                b = NTE
                wr.append(self.xhalo)
            P.dma("sp", lambda h, a=a, b=b: h.dma_start(out=self.xT[:, :, a:b], in_=xv[:, :, a:b]),
                  writes=wr)
        P.op("dve", lambda h: h.memset(self.ones[:], 1.0), writes=[self.onesb])
        P.op("dve", lambda h: h.memset(self.epsb[:], EPS), writes=[self.onesb])

        if any(s[0] == "mod" for s in stages):
            modw_d = self.din("modw", [NL, D, 9 * D])
            self.stage_mod(modw_d)
            mt_o = self.dout("modtab_out", [128, 480])
            self.outs.append(P.dma("sp", lambda h: h.dma_start(out=mt_o[:, :], in_=self.modtab[:]),
                                   reads=[self.mbuf]))
        else:
            mt_i = self.din("modtab_in", [128, 480])
            P.dma("sp", lambda h: h.dma_start(out=self.modtab[:], in_=mt_i[:, :]), writes=[self.mbuf])

        dr = {}
        for st in stages:
            kind = st[0]
            if kind == "ffn":
                if "wgu" not in dr:
                    nf = sum(1 for s in stages if s[0] == "ffn")
                    dr["wgu"] = self.din("wgu", [nf, 11, 128, 8 * 512])
                    dr["wdn"] = self.din("wdn", [nf, DFF, D])
                    dr["nf"] = 0
                self.stage_ffn(st[1], st[2], dr["wgu"], dr["wdn"], dr["nf"])
                dr["nf"] += 1
            elif kind == "final":
                self.stage_final()
            elif kind == "pool":
                self.stage_pool(st[1])
            elif kind == "conv":
                self.stage_conv(st[1])
            elif kind == "qkv":
                self.stage_qkv(st[1], st[2])
            elif kind == "attn":
                self.stage_attn(st[1], st[2])
            elif kind == "halo":
                self.stage_halo()
            elif kind == "gather":
                self.stage_gather(st[1])

        xo_d = self.dout("xout", [D, NT])
        ov = xo_d.rearrange("(c p) n -> p c n", p=128)
        for i, (c0, w) in enumerate(self.tiles):
            self.outs.append(P.dma("sp", lambda h, c0=c0, w=w: h.dma_start(
                out=ov[:, :, c0 - H:c0 - H + w], in_=self.xT[:, :, c0:c0 + w]), reads=[self.xbuf[i]]))
        self.emit_all(self.outs)
        return nc

    def emit_all(self, outs):
        P, nc = self.P, self.nc
        P.finalize()
        self.nwaits = {}
        with nc.Block() as block:
            @block.tensor
            def _(e):
                self.nwaits["pe"] = P.emit("pe", e)

            @block.vector
            def _(e):
                self.nwaits["dve"] = P.emit("dve", e)

            @block.scalar
            def _(e):
                self.nwaits["act"] = P.emit("act", e)

            @block.gpsimd
            def _(e):
                self.nwaits["pool"] = P.emit("pool", e)
                P.final_waits("pool", e, outs, "pool")

            @block.sync
            def _(e):
                self.nwaits["sp"] = P.emit("sp", e)
                P.final_waits("sp", e, outs, "sp")
        self.stack.close()

    def stage_mod(self, modw_d):
        P, nc = self.P, self.nc
        self.new_stage()
        cact = self.sb("cact", [128, 8], BF16)[:]
        cb = Buf("cact")
        wm = [self.BR.take(8, 512) for i in range(2)]
        wmb = [Buf(f"wm{i}") for i in range(2)]
        mps, mpb = self.psum[0], self.pb[0]
        co = COFF["c"]
        P.op("act", lambda h: h.activation(out=cact, in_=self.consts[:, co:co + 8], func=AF.Silu),
             reads=[self.cbuf], writes=[cb])
        for l in range(NL):
            mv = modw_d[l].rearrange("(c p) n -> p c n", p=128)
            for piece in range(18):
                s = self.ring("wm", 2)
                P.dma("pool", lambda h, s=s, piece=piece, mv=mv: h.dma_start(
                    out=wm[s], in_=mv[:, :, piece * 512:(piece + 1) * 512]), writes=[wmb[s]])
                for jj in range(4):
                    j = piece * 4 + jj
                    for c in range(8):
                        P.op("pe", lambda h, s=s, jj=jj, c=c, j=j: h.matmul(
                            mps[:, j:j + 1], wm[s][:, c, jj * 128:(jj + 1) * 128], cact[:, c:c + 1],
                            start=(c == 0), stop=(c == 7)),
                            reads=[wmb[s], cb], writes=[mpb])
            bo = COFF[f"modb{l}"]
            P.op("dve", lambda h, l=l, bo=bo: h.tensor_tensor(
                out=self.modv[:, l * 72:(l + 1) * 72], in0=mps[:, 0:72],
                in1=self.consts[:, bo:bo + 72], op=ALU.add),
                reads=[mpb, self.cbuf], writes=[self.mbuf])
            go = COFF[f"ng{l}"]
            for k in range(3):
                sc = self.modv[:, l * 72 + (3 * k + 1) * 8: l * 72 + (3 * k + 2) * 8]
                P.op("dve", lambda h, l=l, k=k, sc=sc, go=go: h.scalar_tensor_tensor(
                    out=self.modA[:, l * 24 + k * 8: l * 24 + (k + 1) * 8], in0=sc, scalar=1.0,
                    in1=self.consts[:, go + k * 8: go + (k + 1) * 8], op0=ALU.add, op1=ALU.mult),
                    reads=[self.mbuf, self.cbuf], writes=[self.mbuf])
                gt = self.modv[:, l * 72 + (3 * k + 2) * 8: l * 72 + (3 * k + 3) * 8]
                P.op("dve", lambda h, l=l, k=k, gt=gt: h.tensor_scalar(
                    out=self.modG[:, l * 24 + k * 8: l * 24 + (k + 1) * 8], in0=gt,
                    scalar1=(1.0 if k == 1 else 0.5), scalar2=None, op0=ALU.mult),
                    reads=[self.mbuf], writes=[self.mbuf])

    def alloc_norm_scratch(self, ssbank, tmp=True):
        S = {}
        S["sq"] = [self.BR.take(TT) for i in range(2)]
        S["sqb"] = [Buf() for _ in range(2)]
        S["ssp"] = self.psum[ssbank]
        S["sspb"] = self.pb[ssbank]
        S["rstd"] = self.FR.take(TT)
        S["rstdb"] = Buf()
        if tmp:
            S["tmp"] = [self.FR.take(TT) for i in range(2)]
            S["tmpb"] = [Buf() for _ in range(2)]
        return S

    def rstd_cols(self, c0, w, S):
        P = self.P
        xb = self.xbuf_of(c0, w)
        for c in range(8):
            s = self.ring("sq", 2)
            P.op("act", lambda h, c=c, s=s: h.activation(out=S["sq"][s][:, :w], in_=self.xT[:, c, c0:c0 + w],
                                                         func=AF.Square),
                 reads=xb, writes=[S["sqb"][s]])
            P.op("pe", lambda h, c=c, s=s: h.matmul(S["ssp"][:, :w], self.ones[:], S["sq"][s][:, :w],
                                                    start=(c == 0), stop=(c == 7)),
                 reads=[S["sqb"][s], self.onesb], writes=[S["sspb"]])
        P.op("act", lambda h: h.activation(out=S["rstd"][:, :w], in_=S["ssp"][:, :w], func=AF.Sqrt,
                                           bias=self.epsb[:], scale=1.0 / D),
             reads=[S["sspb"], self.onesb], writes=[S["rstdb"]])
        P.op("dve", lambda h: h.reciprocal(out=S["rstd"][:, :w], in_=S["rstd"][:, :w]),
             reads=[S["rstdb"]], writes=[S["rstdb"]])

    def modulate_cols(self, l, k, c0, w, S, dst, dstb):
        P = self.P
        A = lambda c: self.modA[:, l * 24 + k * 8 + c: l * 24 + k * 8 + c + 1]
        SH = lambda c: self.modv[:, l * 72 + 3 * k * 8 + c: l * 72 + 3 * k * 8 + c + 1]
        xb = self.xbuf_of(c0, w)
        for c in range(8):
            s = self.ring("tmp", 2)
            P.op("dve", lambda h, c=c, s=s: h.scalar_tensor_tensor(
                out=S["tmp"][s][:, :w], in0=self.xT[:, c, c0:c0 + w], scalar=A(c), in1=S["rstd"][:, :w],
                op0=ALU.mult, op1=ALU.mult),
                reads=xb + [S["rstdb"], self.mbuf], writes=[S["tmpb"][s]])
            P.op("act", lambda h, c=c, s=s: h.activation(
                out=dst(c), in_=S["tmp"][s][:, :w], func=AF.Identity, bias=SH(c), scale=1.0),
                reads=[S["tmpb"][s], self.mbuf], writes=[dstb])

    def stage_ffn(self, l, which, wgu_d, wdn_d, fi):
        P, nc = self.P, self.nc
        self.new_stage()
        k = which
        S = self.alloc_norm_scratch(4)
        hT = self.BR.take(8, TT)
        hb = Buf("hT")
        aT = self.BR.take(NFC, TT)
        ab = [Buf(f"a{f}") for f in range(NFC)]
        wg = [self.BR.take(8, 512) for i in range(2)]
        wgb = [Buf(f"wg{i}") for i in range(2)]
        wd = [self.BR.take(2, 512) for i in range(3)]
        wdb = [Buf(f"wd{i}") for i in range(3)]
        sg = [self.FR.take(TT) for i in range(2)]
        sgb = [Buf(f"sg{i}") for i in range(2)]
        G = lambda c: self.modG[:, l * 24 + k * 8 + c: l * 24 + k * 8 + c + 1]
        wdv = wdn_d[fi].rearrange("(f p) n -> p f n", p=128)
        for ti, (c0, w) in enumerate(self.tiles):
            xs = lambda c, c0=c0, w=w: self.xT[:, c, c0:c0 + w]
            self.rstd_cols(c0, w, S)
            self.modulate_cols(l, k, c0, w, S, lambda c, w=w: hT[:, c, :w], hb)
            for G2 in range(11):
                ws = self.ring("wg", 2)
                P.dma("pool", lambda h, ws=ws, G2=G2: h.dma_start(
                    out=wg[ws].rearrange("p c n -> p (c n)"), in_=wgu_d[fi, G2]), writes=[wgb[ws]])
                for ff in range(2):
                    f = G2 * 2 + ff
                    pi = self.ring("gu", 2)
                    gp, gb = self.psum[pi], self.pb[pi]
                    up, ub = self.psum[2 + pi], self.pb[2 + pi]
                    for c in range(8):
                        P.op("pe", lambda h, c=c, ws=ws, ff=ff, gp=gp, w=w: h.matmul(
                            gp[:, :w], wg[ws][:, c, ff * 128:(ff + 1) * 128], hT[:, c, :w],
                            start=(c == 0), stop=(c == 7)),
                            reads=[wgb[ws], hb], writes=[gb])
                    for c in range(8):
                        P.op("pe", lambda h, c=c, ws=ws, ff=ff, up=up, w=w: h.matmul(
                            up[:, :w], wg[ws][:, c, 256 + ff * 128:256 + (ff + 1) * 128], hT[:, c, :w],
                            start=(c == 0), stop=(c == 7)),
                            reads=[wgb[ws], hb], writes=[ub])
                    P.op("act", lambda h, pi=pi, gp=gp, w=w: h.activation(out=sg[pi][:, :w], in_=gp[:, :w],
                                                                          func=AF.Silu),
                         reads=[gb], writes=[sgb[pi]])
                    P.op("dve", lambda h, pi=pi, f=f, up=up, w=w: h.tensor_tensor(
                        out=aT[:, f, :w], in0=sg[pi][:, :w], in1=up[:, :w], op=ALU.mult),
                        reads=[sgb[pi], ub], writes=[ab[f]])
            for half in range(2):
                for fp in range(11):
                    ws = self.ring("wd", 3)
                    P.dma("pool", lambda h, ws=ws, fp=fp, half=half: h.dma_start(
                        out=wd[ws], in_=wdv[:, 2 * fp:2 * fp + 2, half * 512:(half + 1) * 512]),
                        writes=[wdb[ws]])
                    for ff in range(2):
                        f = fp * 2 + ff
                        for dq in range(4):
                            P.op("pe", lambda h, ws=ws, ff=ff, f=f, dq=dq, w=w: h.matmul(
                                self.psum[4 + dq][:, :w], wd[ws][:, ff, dq * 128:(dq + 1) * 128], aT[:, f, :w],
                                start=(f == 0), stop=(f == NFC - 1)),
                                reads=[wdb[ws], ab[f]], writes=[self.pb[4 + dq]])
                for dq in range(4):
                    dc = half * 4 + dq
                    P.op("dve", lambda h, dc=dc, dq=dq, xs=xs, w=w: h.scalar_tensor_tensor(
                        out=xs(dc), in0=self.psum[4 + dq][:, :w], scalar=G(dc), in1=xs(dc),
                        op0=ALU.mult, op1=ALU.add),
                        reads=[self.pb[4 + dq], self.mbuf], writes=[self.xbuf[ti]])

    def stage_final(self):
        P = self.P
        self.new_stage()
        S = self.alloc_norm_scratch(0, tmp=False)
        go = COFF["final_g"]
        for ti, (c0, w) in enumerate(self.tiles):
            xs = lambda c, c0=c0, w=w: self.xT[:, c, c0:c0 + w]
            self.rstd_cols(c0, w, S)
            for c in range(8):
                P.op("dve", lambda h, c=c, xs=xs, w=w: h.scalar_tensor_tensor(
                    out=xs(c), in0=xs(c), scalar=self.consts[:, go + c:go + c + 1], in1=S["rstd"][:, :w],
                    op0=ALU.mult, op1=ALU.mult),
                    reads=[S["rstdb"], self.cbuf], writes=[self.xbuf[ti]])


    def stage_pool(self, l):
        P, H, NT = self.P, self.H, self.NT
        assert H == 8
        self.new_stage()
        k = 1
        pw_d = self.din("pool_w", [4, 256, 256])
        inv_d = self.din("pool_inv", [128, 4, self.NTE])
        S = self.alloc_norm_scratch(0, tmp=False)
        hx = self.FR.take(2, TT)
        sA = self.FR.take(2, TT)
        sB = self.FR.take(2, TT)
        invt = self.FR.take(TT)
        hxb, sAb, sBb, invb = Buf(), Buf(), Buf(), Buf()
        dg = [self.BR.take(8, TT) for _ in range(2)]
        dgb = [Buf() for _ in range(2)]
        pw = self.BR.take(8, 256)
        pwb = Buf()
        gsv = self.sb("gsv", [128, 8])
        gsb = Buf()
        P.dma("pool", lambda h: h.dma_start(out=pw, in_=pw_d.rearrange("g (cc p) e -> p (g cc) e", p=128)),
              writes=[pwb])
        po = COFF["pool_scale"]
        P.op("dve", lambda h: h.tensor_tensor(out=gsv[:], in0=self.modG[:, l * 24 + 8: l * 24 + 16],
                                              in1=self.consts[:, po:po + 8], op=ALU.mult),
             reads=[self.mbuf, self.cbuf], writes=[gsb])
        A = lambda c: self.modA[:, l * 24 + k * 8 + c: l * 24 + k * 8 + c + 1]
        SH = lambda c: self.modv[:, l * 72 + 3 * k * 8 + c: l * 72 + 3 * k * 8 + c + 1]
        mL = self.consts[:, COFF["maskL"]:COFF["maskL"] + 1]
        mR = self.consts[:, COFF["maskR"]:COFF["maskR"] + 1]
        OW = self.cfg.get("pool_ow", 496)
        otiles = [(H + o, min(OW, NT - o)) for o in range(0, NT, OW)]

        def finish(i):
            o0, ow = otiles[i]
            d = dg[i % 2]
            for g in range(4):
                for ec in range(2):
                    dc = 2 * g + ec
                    pi = 1 + self.ring("py", 4)
                    for cc in range(2):
                        P.op("pe", lambda h, g=g, ec=ec, cc=cc, pi=pi, d=d, ow=ow: h.matmul(
                            self.psum[pi][:, :ow], pw[:, g * 2 + cc, ec * 128:(ec + 1) * 128], d[:, g * 2 + cc, :ow],
                            start=(cc == 0), stop=(cc == 1)),
                            reads=[pwb, dgb[i % 2]], writes=[self.pb[pi]])
                    P.op("dve", lambda h, dc=dc, pi=pi, o0=o0, ow=ow: h.scalar_tensor_tensor(
                        out=self.xT[:, dc, o0:o0 + ow], in0=self.psum[pi][:, :ow], scalar=gsv[:, dc:dc + 1],
                        in1=self.xT[:, dc, o0:o0 + ow], op0=ALU.mult, op1=ALU.add),
                        reads=[self.pb[pi], gsb], writes=self.xbuf_of(o0, ow))

        for i, (o0, ow) in enumerate(otiles):
            e0, ew = o0 - 8, ow + 16
            xb = self.xbuf_of(e0, ew)
            self.rstd_cols(e0, ew, S)
            d = dg[i % 2]
            for g in range(4):
                P.dma("sp", lambda h, g=g, e0=e0, ew=ew: h.dma_start(out=invt[:, :ew], in_=inv_d[:, g, e0:e0 + ew]),
                      writes=[invb])
                for cc in range(2):
                    c = 2 * g + cc
                    P.op("dve", lambda h, c=c, cc=cc, e0=e0, ew=ew: h.scalar_tensor_tensor(
                        out=hx[:, cc, :ew], in0=self.xT[:, c, e0:e0 + ew], scalar=A(c), in1=S["rstd"][:, :ew],
                        op0=ALU.mult, op1=ALU.mult), reads=xb + [S["rstdb"], self.mbuf], writes=[hxb])
                    P.op("act", lambda h, c=c, cc=cc, ew=ew: h.activation(
                        out=hx[:, cc, :ew], in_=hx[:, cc, :ew], func=AF.Identity, bias=SH(c), scale=1.0),
                        reads=[hxb, self.mbuf], writes=[hxb])
                if i == 0:
                    P.op("dve", lambda h: h.tensor_scalar(out=hx[:, :, 0:8], in0=hx[:, :, 0:8], scalar1=mL,
                                                          scalar2=None, op0=ALU.mult),
                         reads=[hxb, self.cbuf], writes=[hxb])
                if i == len(otiles) - 1:
                    P.op("dve", lambda h, ew=ew: h.tensor_scalar(out=hx[:, :, ew - 8:ew], in0=hx[:, :, ew - 8:ew],
                                                                 scalar1=mR, scalar2=None, op0=ALU.mult),
                         reads=[hxb, self.cbuf], writes=[hxb])
                P.op("dve", lambda h, ew=ew: h.tensor_tensor(out=sA[:, :, 1:ew], in0=hx[:, :, 0:ew - 1],
                                                             in1=hx[:, :, 1:ew], op=ALU.add),
                     reads=[hxb], writes=[sAb])
                fin, finb = sA, sAb
                if g >= 1:
                    P.op("dve", lambda h, ew=ew: h.tensor_tensor(out=sB[:, :, 2:ew - 1], in0=sA[:, :, 1:ew - 2],
                                                                 in1=sA[:, :, 3:ew], op=ALU.add),
                         reads=[sAb], writes=[sBb])
                    fin, finb = sB, sBb
                if g >= 2:
                    P.op("dve", lambda h, ew=ew: h.tensor_tensor(out=sA[:, :, 4:ew - 3], in0=sB[:, :, 2:ew - 5],
                                                                 in1=sB[:, :, 6:ew - 1], op=ALU.add),
                         reads=[sBb], writes=[sAb])
                    fin, finb = sA, sAb
                if g >= 3:
                    P.op("dve", lambda h, ew=ew: h.tensor_tensor(out=sB[:, :, 8:ew - 7], in0=sA[:, :, 4:ew - 11],
                                                                 in1=sA[:, :, 12:ew - 3], op=ALU.add),
                         reads=[sAb], writes=[sBb])
                    fin, finb = sB, sBb
                for cc in range(2):
                    P.op("dve", lambda h, cc=cc, fin=fin, ow=ow: h.tensor_tensor(
                        out=fin[:, cc, 8:8 + ow], in0=fin[:, cc, 8:8 + ow], in1=invt[:, 8:8 + ow], op=ALU.mult),
                        reads=[finb, invb], writes=[finb])
                    P.op("dve", lambda h, cc=cc, fin=fin, ow=ow, g=g, d=d: h.tensor_tensor(
                        out=d[:, g * 2 + cc, :ow], in0=fin[:, cc, 8:8 + ow], in1=hx[:, cc, 8:8 + ow],
                        op=ALU.subtract),
                        reads=[finb, hxb], writes=[dgb[i % 2]])
            if i >= 1:
                finish(i - 1)
        finish(len(otiles) - 1)

    def stage_conv(self, l):
        P, H, NT = self.P, self.H, self.NT
        assert H >= 1
        self.new_stage()
        k = 1
        win_d = self.din("conv_win", [8, 128, 8 * 384])
        wout_d = self.din("conv_wout", [D, D])
        S = self.alloc_norm_scratch(0)
        hT = self.BR.take(8, TT)
        hb = Buf()
        wi = [self.BR.take(8, 384) for _ in range(2)]
        wib = [Buf() for _ in range(2)]
        mT = [self.BR.take(8, TT) for _ in range(2)]
        mTb = [Buf() for _ in range(2)]
        wo = self.BR.take(8, D)
        wob = Buf()
        t1 = self.FR.take(TT)
        z = self.FR.take(TT)
        zc = self.FR.take(TT)
        t1b, zb, zcb = Buf(), Buf(), Buf()
        P.dma("pool", lambda h: h.dma_start(out=wo, in_=wout_d.rearrange("(c p) n -> p c n", p=128)),
              writes=[wob])
        G = lambda c: self.modG[:, l * 24 + k * 8 + c: l * 24 + k * 8 + c + 1]
        cw = lambda kk, dc: self.consts[:, COFF["conv_w"] + kk * 8 + dc: COFF["conv_w"] + kk * 8 + dc + 1]
        mL = self.consts[:, COFF["maskL"]:COFF["maskL"] + 1]
        mR = self.consts[:, COFF["maskR"]:COFF["maskR"] + 1]
        OW = 510
        otiles = [(H + o, min(OW, NT - o)) for o in range(0, NT, OW)]

        def finish(i):
            o0, ow = otiles[i]
            m = mT[i % 2]
            for oc in range(8):
                pi = 4 + self.ring("cy", 4)
                for dc in range(8):
                    P.op("pe", lambda h, oc=oc, dc=dc, pi=pi, m=m, ow=ow: h.matmul(
                        self.psum[pi][:, :ow], wo[:, dc, oc * 128:(oc + 1) * 128], m[:, dc, :ow],
                        start=(dc == 0), stop=(dc == 7)),
                        reads=[wob, mTb[i % 2]], writes=[self.pb[pi]])
                P.op("dve", lambda h, oc=oc, pi=pi, o0=o0, ow=ow: h.scalar_tensor_tensor(
                    out=self.xT[:, oc, o0:o0 + ow], in0=self.psum[pi][:, :ow], scalar=G(oc),
                    in1=self.xT[:, oc, o0:o0 + ow], op0=ALU.mult, op1=ALU.add),
                    reads=[self.pb[pi], self.mbuf], writes=self.xbuf_of(o0, ow))

        for i, (o0, ow) in enumerate(otiles):
            e0, ew = o0 - 1, ow + 2
            self.rstd_cols(e0, ew, S)
            self.modulate_cols(l, k, e0, ew, S, lambda c, ew=ew: hT[:, c, :ew], hb)
            if i >= 1:
                finish(i - 1)
            m = mT[i % 2]
            for dc in range(8):
                ws = self.ring("wi", 2)
                P.dma("pool", lambda h, ws=ws, dc=dc: h.dma_start(
                    out=wi[ws].rearrange("p c n -> p (c n)"), in_=win_d[dc]), writes=[wib[ws]])
                for part in range(3):
                    for c in range(8):
                        P.op("pe", lambda h, ws=ws, part=part, c=c, ew=ew: h.matmul(
                            self.psum[1 + part][:, :ew], wi[ws][:, c, part * 128:(part + 1) * 128], hT[:, c, :ew],
                            start=(c == 0), stop=(c == 7)),
                            reads=[wib[ws], hb], writes=[self.pb[1 + part]])
                P.op("act", lambda h, ew=ew: h.activation(out=t1[:, :ew], in_=self.psum[2][:, :ew], func=AF.Copy),
                     reads=[self.pb[2]], writes=[t1b])
                P.op("dve", lambda h, ew=ew: h.tensor_tensor(out=z[:, :ew], in0=t1[:, :ew], in1=self.psum[3][:, :ew],
                                                             op=ALU.mult),
                     reads=[t1b, self.pb[3]], writes=[zb])
                if i == 0:
                    P.op("dve", lambda h: h.tensor_scalar(out=z[:, 0:1], in0=z[:, 0:1], scalar1=mL, scalar2=None,
                                                          op0=ALU.mult), reads=[zb, self.cbuf], writes=[zb])
                if i == len(otiles) - 1:
                    P.op("dve", lambda h, ew=ew: h.tensor_scalar(out=z[:, ew - 1:ew], in0=z[:, ew - 1:ew], scalar1=mR,
                                                                 scalar2=None, op0=ALU.mult),
                         reads=[zb, self.cbuf], writes=[zb])
                P.op("dve", lambda h, dc=dc, ow=ow: h.tensor_scalar(out=zc[:, :ow], in0=z[:, 0:ow], scalar1=cw(0, dc),
                                                                    scalar2=None, op0=ALU.mult),
                     reads=[zb, self.cbuf], writes=[zcb])
                for kk in (1, 2):
                    P.op("dve", lambda h, dc=dc, ow=ow, kk=kk: h.scalar_tensor_tensor(
                        out=zc[:, :ow], in0=z[:, kk:kk + ow], scalar=cw(kk, dc), in1=zc[:, :ow],
                        op0=ALU.mult, op1=ALU.add), reads=[zb, zcb, self.cbuf], writes=[zcb])
                P.op("dve", lambda h, dc=dc, ow=ow, m=m: h.tensor_tensor(
                    out=m[:, dc, :ow], in0=zc[:, :ow], in1=self.psum[1][:, 1:1 + ow], op=ALU.mult),
                    reads=[zcb, self.pb[1]], writes=[mTb[i % 2]])
        finish(len(otiles) - 1)

    def stage_qkv(self, kind, l):
        P, H, NT = self.P, self.H, self.NT
        self.new_stage()
        k = 1
        gqa = (kind == "gqa")
        KF = 256 if gqa else 1024
        VF = 256 if gqa else 1024
        nblk = 6 if gqa else 10
        nqb, nkb = 4, (1 if gqa else 4)
        sfx = ("_" + kind) if self.fused else ""
        w_d = self.din("wqkv" + sfx, [nblk, 128, 8 * 512])
        rc_d = self.din("ropeC" + sfx, [128, NT])
        rs_d = self.din("ropeS" + sfx, [128, NT])
        KBL = min(2048, NT)
        NBL = NT // KBL
        CPB = KBL // 128
        VH = VF // 128
        if self.fused:
            q_o = self.nc.dram_tensor("q_" + kind, [1024, NT], BF16).ap()
            nkh = KF // 128
            k_ts = [self.nc.dram_tensor(f"kb_{kind}{j}", [128, NT], BF16) for j in range(nkh)]
            v_ts = [self.nc.dram_tensor(f"vb_{kind}{j}", [NBL * 128, CPB * 128], BF16) for j in range(VH)]
            v_o4 = [t.ap().rearrange("(b p) (c e) -> b p c e", b=NBL, p=128, c=CPB, e=128) for t in v_ts]
            k_o = None
            self.scr[kind] = dict(q=q_o, k_ts=k_ts, v_ts=v_ts, NBL=NBL, CPB=CPB)
        else:
            q_o = self.dout("q_out", [1024, NT], BF16)
            k_o = self.dout("k_out", [KF, NT], BF16)
            v_o = self.dout("v_out", [NT, VF], BF16)
        S = self.alloc_norm_scratch(0)
        hT = self.BR.take(8, TT)
        hb = Buf()
        wq = [self.BR.take(8, 512) for _ in range(2)]
        wqb = [Buf() for _ in range(2)]
        stg = [self.BR.take(TT) for _ in range(4)]
        stgb = [Buf() for _ in range(4)]
        sqq = self.BR.take(TT)
        sqqb = Buf()
        Ct = self.FR.take(TT)
        St = self.FR.take(TT)
        ctb = Buf()
        t1 = self.FR.take(TT)
        t2 = self.FR.take(TT)
        rq = self.FR.take(TT)
        t1b, t2b, rqb = Buf(), Buf(), Buf()
        gcol = {"q": (COFF["gqa_qg"], COFF["gqa_qg_sw"]), "k": (COFF["gqa_kg"], COFF["gqa_kg_sw"])}
        for ti, (c0, w) in enumerate(self.tiles):
            tok0 = c0 - H
            self.rstd_cols(c0, w, S)
            self.modulate_cols(l, k, c0, w, S, lambda c, w=w: hT[:, c, :w], hb)
            P.dma("sp", lambda h, tok0=tok0, w=w: h.dma_start(out=Ct[:, :w], in_=rc_d[:, tok0:tok0 + w]), writes=[ctb])
            P.dma("sp", lambda h, tok0=tok0, w=w: h.dma_start(out=St[:, :w], in_=rs_d[:, tok0:tok0 + w]), writes=[ctb])
            for b in range(nblk):
                ws = self.ring("wq", 2)
                P.dma("pool", lambda h, ws=ws, b=b: h.dma_start(
                    out=wq[ws].rearrange("p c n -> p (c n)"), in_=w_d[b]), writes=[wqb[ws]])
                if b < nqb + nkb:
                    which = "q" if b < nqb else "k"
                    dest = q_o if which == "q" else k_o
                    for ff in range(2):
                        fc = (b if which == "q" else b - nqb) * 2 + ff
                        pr = self.ring("qp", 2)
                        qp, qpb = self.psum[1 + pr], self.pb[1 + pr]
                        qs, qsb = self.psum[3 + pr], self.pb[3 + pr]
                        for c in range(8):
                            P.op("pe", lambda h, ws=ws, ff=ff, c=c, qp=qp, w=w: h.matmul(
                                qp[:, :w], wq[ws][:, c, ff * 128:(ff + 1) * 128], hT[:, c, :w],
                                start=(c == 0), stop=(c == 7)), reads=[wqb[ws], hb], writes=[qpb])
                        for c in range(8):
                            P.op("pe", lambda h, ws=ws, ff=ff, c=c, qs=qs, w=w: h.matmul(
                                qs[:, :w], wq[ws][:, c, 256 + ff * 128:256 + (ff + 1) * 128], hT[:, c, :w],
                                start=(c == 0), stop=(c == 7)), reads=[wqb[ws], hb], writes=[qsb])
                        si = self.ring("stg", 4)
                        if not gqa:
                            P.op("dve", lambda h, qp=qp, w=w: h.tensor_tensor(out=t1[:, :w], in0=qp[:, :w], in1=Ct[:, :w],
                                                                               op=ALU.mult),
                                 reads=[qpb, ctb], writes=[t1b])
                            P.op("dve", lambda h, qs=qs, w=w: h.tensor_tensor(out=t2[:, :w], in0=qs[:, :w], in1=St[:, :w],
                                                                               op=ALU.mult),
                                 reads=[qsb, ctb], writes=[t2b])
                            P.op("dve", lambda h, si=si, w=w: h.tensor_tensor(out=stg[si][:, :w], in0=t1[:, :w],
                                                                               in1=t2[:, :w], op=ALU.add),
                                 reads=[t1b, t2b], writes=[stgb[si]])
                        else:
                            g0, g1 = gcol[which]
                            P.op("act", lambda h, qp=qp, w=w: h.activation(out=sqq[:, :w], in_=qp[:, :w], func=AF.Square),
                                 reads=[qpb], writes=[sqqb])
                            P.op("pe", lambda h, w=w: h.matmul(self.psum[5][:, :w], self.ones[:], sqq[:, :w],
                                                               start=True, stop=True),
                                 reads=[sqqb, self.onesb], writes=[self.pb[5]])
                            P.op("act", lambda h, w=w: h.activation(out=rq[:, :w], in_=self.psum[5][:, :w], func=AF.Sqrt,
                                                                    bias=self.epsb[:], scale=1.0 / 128),
                                 reads=[self.pb[5], self.onesb], writes=[rqb])
                            P.op("dve", lambda h, w=w: h.reciprocal(out=rq[:, :w], in_=rq[:, :w]),
                                 reads=[rqb], writes=[rqb])
                            P.op("dve", lambda h, qp=qp, w=w, g0=g0: h.scalar_tensor_tensor(
                                out=t1[:, :w], in0=qp[:, :w], scalar=self.consts[:, g0:g0 + 1], in1=Ct[:, :w],
                                op0=ALU.mult, op1=ALU.mult), reads=[qpb, ctb, self.cbuf], writes=[t1b])
                            P.op("dve", lambda h, qs=qs, w=w, g1=g1: h.scalar_tensor_tensor(
                                out=t2[:, :w], in0=qs[:, :w], scalar=self.consts[:, g1:g1 + 1], in1=St[:, :w],
                                op0=ALU.mult, op1=ALU.mult), reads=[qsb, ctb, self.cbuf], writes=[t2b])
                            P.op("dve", lambda h, w=w: h.tensor_tensor(out=t1[:, :w], in0=t1[:, :w], in1=t2[:, :w],
                                                                       op=ALU.add),
                                 reads=[t1b, t2b], writes=[t1b])
                            P.op("dve", lambda h, si=si, w=w: h.tensor_tensor(out=stg[si][:, :w], in0=t1[:, :w],
                                                                               in1=rq[:, :w], op=ALU.mult),
                                 reads=[t1b, rqb], writes=[stgb[si]])
                        if self.fused and which == "k":
                            dap = k_ts[fc].ap()[:, tok0:tok0 + w]
                        else:
                            dap = dest[fc * 128:(fc + 1) * 128, tok0:tok0 + w]
                        self.outs.append(P.dma("sp", lambda h, si=si, dap=dap, w=w: h.dma_start(
                            out=dap, in_=stg[si][:, :w]), reads=[stgb[si]]))
                else:
                    vb = b - nqb - nkb
                    vw = 256 if gqa else 512
                    for tc in range(w // 128):
                        pr = 6 + self.ring("vp", 2)
                        for c in range(8):
                            P.op("pe", lambda h, ws=ws, c=c, pr=pr, tc=tc, vw=vw: h.matmul(
                                self.psum[pr][:, :vw], hT[:, c, tc * 128:(tc + 1) * 128], wq[ws][:, c, 0:vw],
                                start=(c == 0), stop=(c == 7)), reads=[wqb[ws], hb], writes=[self.pb[pr]])
                        si = self.ring("stg", 4)
                        P.op("act", lambda h, si=si, pr=pr, vw=vw: h.activation(out=stg[si][:, :vw], in_=self.psum[pr][:, :vw],
                                                                                func=AF.Copy),
                             reads=[self.pb[pr]], writes=[stgb[si]])
                        if self.fused:
                            tk = tok0 + tc * 128
                            bl, cl = tk // KBL, (tk % KBL) // 128
                            nh = vw // 128
                            h0 = vb * 4
                            for hh in range(nh):
                                self.outs.append(P.dma("sp", lambda h, si=si, bl=bl, cl=cl, hh=hh, h0=h0: h.dma_start(
                                    out=v_o4[h0 + hh][bl, :, cl, :], in_=stg[si][:, hh * 128:(hh + 1) * 128]),
                                    reads=[stgb[si]]))
                        else:
                            self.outs.append(P.dma("sp", lambda h, si=si, tok0=tok0, tc=tc, vb=vb, vw=vw: h.dma_start(
                                out=v_o[tok0 + tc * 128:tok0 + (tc + 1) * 128, vb * 512:vb * 512 + vw], in_=stg[si][:, :vw]),
                                reads=[stgb[si]]))

    def stage_gather(self, kind):
        P = self.P
        self.new_stage()
        sc = self.scr[kind]
        sc["kgb"], sc["vgb"] = Buf(), Buf()
        NT = self.NT
        if self.cpbt == 1:
            sc["kg"] = [t.ap() for t in sc["k_ts"]]
            sc["vg"] = [t.ap() for t in sc["v_ts"]]
            return
        R4 = self.cpbt
        vr, vc = sc["NBL"] * 128, sc["CPB"] * 128
        sc["kg"], sc["vg"] = [], []
        for j, t in enumerate(sc["k_ts"]):
            g = self.nc.dram_tensor(f"kg_{kind}{j}", [R4 * 128, NT], BF16)
            P.cc(lambda h, t=t, g=g: h.collective_compute("AllGather", ALU.bypass, replica_groups=self.groups,
                                                          ins=[t.ap().opt()], outs=[g.ap().opt()]),
                 writes=[sc["kgb"]])
            sc["kg"].append(g.ap())
        for j, t in enumerate(sc["v_ts"]):
            g = self.nc.dram_tensor(f"vg_{kind}{j}", [R4 * vr, vc], BF16)
            P.cc(lambda h, t=t, g=g: h.collective_compute("AllGather", ALU.bypass, replica_groups=self.groups,
                                                          ins=[t.ap().opt()], outs=[g.ap().opt()]),
                 writes=[sc["vgb"]])
            sc["vg"].append(g.ap())

    def stage_halo(self):
        P, H, NT, NTE = self.P, self.H, self.NT, self.NTE
        self.new_stage()
        i = self.nhalo
        self.nhalo += 1
        allx = self.xbuf + [self.xhalo]
        if self.cpbt == 1:
            P.op("dve", lambda h: h.memset(self.xT[:, :, 0:H], 0.0), writes=[self.xhalo])
            P.op("dve", lambda h: h.memset(self.xT[:, :, H + NT:NTE], 0.0), writes=[self.xhalo])
            return
        R4 = self.cpbt
        hb_t = self.nc.dram_tensor(f"hb{i}", [D, 2 * H], F32)
        hg_t = self.nc.dram_tensor(f"hg{i}", [R4 * D, 2 * H], F32)
        hbv = hb_t.ap().rearrange("(c p) n -> p c n", p=128)
        hbuf = Buf()
        P.dma("sp", lambda h: h.dma_start(out=hbv[:, :, 0:H], in_=self.xT[:, :, H:2 * H]), reads=allx, writes=[])
        P.dma("sp", lambda h: h.dma_start(out=hbv[:, :, H:2 * H], in_=self.xT[:, :, NT:NT + H]), reads=allx, writes=[])
        self.barrier()
        P.cc(lambda h: h.collective_compute("AllGather", ALU.bypass, replica_groups=self.groups,
                                            ins=[hb_t.ap().opt()], outs=[hg_t.ap().opt()]), writes=[hbuf])
        hs = self.FR.take(R4 * 8, 2 * H)
        hsb = Buf()
        P.dma("sp", lambda h: h.dma_start(out=hs, in_=hg_t.ap().rearrange("(rc p) n -> p rc n", p=128)),
              reads=[hbuf], writes=[hsb])
        sl, sr = COFF["selL"], COFF["selR"]
        for r in range(R4):
            src_l = hs[:, r * 8:(r + 1) * 8, H:2 * H]
            src_r = hs[:, r * 8:(r + 1) * 8, 0:H]
            dl = self.xT[:, :, 0:H]
            drr = self.xT[:, :, H + NT:NTE]
            if r == 0:
                P.op("dve", lambda h, src_l=src_l, dl=dl: h.tensor_scalar(
                    out=dl, in0=src_l, scalar1=self.consts[:, sl:sl + 1], scalar2=None, op0=ALU.mult),
                    reads=[hsb, self.cbuf], writes=[self.xhalo])
                P.op("dve", lambda h, src_r=src_r, drr=drr: h.tensor_scalar(
                    out=drr, in0=src_r, scalar1=self.consts[:, sr:sr + 1], scalar2=None, op0=ALU.mult),
                    reads=[hsb, self.cbuf], writes=[self.xhalo])
            else:
                P.op("dve", lambda h, src_l=src_l, dl=dl, r=r: h.scalar_tensor_tensor(
                    out=dl, in0=src_l, scalar=self.consts[:, sl + r:sl + r + 1], in1=dl, op0=ALU.mult, op1=ALU.add),
                    reads=[hsb, self.cbuf], writes=[self.xhalo])
                P.op("dve", lambda h, src_r=src_r, drr=drr, r=r: h.scalar_tensor_tensor(
                    out=drr, in0=src_r, scalar=self.consts[:, sr + r:sr + r + 1], in1=drr, op0=ALU.mult, op1=ALU.add),
                    reads=[hsb, self.cbuf], writes=[self.xhalo])

    def stage_attn(self, kind, l):
        P, H, NT, NK = self.P, self.H, self.NT, self.NK
        self.new_stage()
        gqa = (kind == "gqa")
        KB = min(2048, NT if self.fused else NK)
        NBLK = NK // KB
        CPB = KB // 128
        KF = 256 if gqa else 1024
        VH = 2 if gqa else 8
        if self.fused:
            sc = self.scr[kind]
            q_i = sc["q"]
            kg, vg = sc["kg"], sc["vg"]
            kgb, vgb = sc["kgb"], sc["vgb"]
            NBL = NT // KB
            wo_d = self.din("wo_" + kind, [D, D])

            def ksrc(kv, blk):
                r, hf = blk // NBL, blk % NBL
                return kg[kv][r * 128:(r + 1) * 128, hf * KB:(hf + 1) * KB]

            def vsrc(kv, blk):
                return vg[kv][blk * 128:(blk + 1) * 128, :]
        else:
            q_i = self.din("q_in", [1024, NT], BF16)
            k_i = self.din("k_in", [KF, NK], BF16)
            v_i = self.din("v_in", [VH, NBLK, 128, CPB * 128], BF16)
            wo_d = self.din("wo", [D, D])
            kgb, vgb = Buf(), Buf()

            def ksrc(kv, blk):
                return k_i[kv * 128:(kv + 1) * 128, blk * KB:(blk + 1) * KB]

            def vsrc(kv, blk):
                return v_i[kv, blk]
        qt = [self.BR.take(TT) for _ in range(2)]
        qtb = [Buf() for _ in range(2)]
        kt = [self.BR.take(KB) for _ in range(2)]
        ktb = [Buf() for _ in range(2)]
        vt = [self.BR.take(CPB, 128) for _ in range(2)]
        vtb = [Buf() for _ in range(2)]
        pt = [self.BR.take(TT) for _ in range(4)]
        ptb = [Buf() for _ in range(4)]
        oT = self.BR.take(8, TT)
        oTb = Buf()
        wo = self.BR.take(8, D)
        wob = Buf()
        sq = self.BR.take(TT)
        sqb = Buf()
        r1 = self.FR.take(TT)
        r2 = self.FR.take(TT)
        t1 = self.FR.take(TT)
        t2 = self.FR.take(TT)
        r1b, r2b, t1b, t2b = Buf(), Buf(), Buf(), Buf()
        P.dma("pool", lambda h: h.dma_start(out=wo, in_=wo_d.rearrange("(c p) n -> p c n", p=128)), writes=[wob])
        G = lambda c: self.modG[:, l * 24 + 8 + c: l * 24 + 8 + c + 1]
        scale = (128.0 if gqa else 64.0) ** -0.5
        if not gqa:
            lam_init = 0.8 - 0.6 * float(np.exp(-0.3 * l))
            lam_d = self.din("lam", [128, 256])
            self.lam_done = True
            lamt = self.FR.take(256)
            lt = self.sb("lamtmp", [128, 8])
            lb = Buf()
            P.dma("sp", lambda h: h.dma_start(out=lamt, in_=lam_d[:, :]), writes=[lb])
            P.op("dve", lambda h: h.tensor_tensor(out=lamt[:, 0:64], in0=lamt[:, 0:64], in1=lamt[:, 64:128], op=ALU.mult),
                 reads=[lb], writes=[lb])
            P.op("dve", lambda h: h.tensor_tensor(out=lamt[:, 128:192], in0=lamt[:, 128:192], in1=lamt[:, 192:256],
                                                  op=ALU.mult), reads=[lb], writes=[lb])
            P.op("dve", lambda h: h.reduce_sum(out=lt[:, 0:1], in_=lamt[:, 0:64], axis=mybir.AxisListType.X),
                 reads=[lb], writes=[lb])
            P.op("dve", lambda h: h.reduce_sum(out=lt[:, 1:2], in_=lamt[:, 128:192], axis=mybir.AxisListType.X),
                 reads=[lb], writes=[lb])
            P.op("act", lambda h: h.activation(out=lt[:, 2:4], in_=lt[:, 0:2], func=AF.Exp), reads=[lb], writes=[lb])
            P.op("dve", lambda h: h.tensor_tensor(out=lt[:, 4:5], in0=lt[:, 3:4], in1=lt[:, 2:3], op=ALU.subtract),
                 reads=[lb], writes=[lb])
            P.op("dve", lambda h: h.tensor_scalar(out=lt[:, 4:5], in0=lt[:, 4:5], scalar1=-lam_init, scalar2=None,
                                                  op0=ALU.add), reads=[lb], writes=[lb])
            so = COFF["subln_g"]
            P.op("dve", lambda h: h.tensor_scalar(out=lt[:, 5:6], in0=self.consts[:, so:so + 1],
                                                  scalar1=(1.0 - lam_init), scalar2=None, op0=ALU.mult),
                 reads=[lb, self.cbuf], writes=[lb])
            neglam = lt[:, 4:5]
            gsub = lt[:, 5:6]
        ncomp = 1 if gqa else 2
        for ti, (c0, w) in enumerate(self.tiles):
            tok0 = c0 - H
            for u in range(8):
                qi = self.ring("qt", 2)
                P.dma("sp", lambda h, qi=qi, u=u, tok0=tok0, w=w: h.dma_start(
                    out=qt[qi][:, :w], in_=q_i[u * 128:(u + 1) * 128, tok0:tok0 + w]), writes=[qtb[qi]])
                kv = (u // 4) if gqa else u
                if gqa:
                    ob = 2 + 2 * self.ring("ob", 2)
                    Ob = [ob]
                    Zb = [ob + 1]
                else:
                    Ob = [2, 4]
                    Zb = [3, 5]
                for blk in range(NBLK):
                    ki = self.ring("kt", 2)
                    P.dma("sp", lambda h, ki=ki, kv=kv, blk=blk: h.dma_start(
                        out=kt[ki], in_=ksrc(kv, blk)), reads=[kgb], writes=[ktb[ki]])
                    P.dma("sp", lambda h, ki=ki, kv=kv, blk=blk: h.dma_start(
                        out=vt[ki].rearrange("p c e -> p (c e)"), in_=vsrc(kv, blk)), reads=[vgb], writes=[vtb[ki]])
                    for kc in range(CPB):
                        first = (blk == 0 and kc == 0)
                        last = (blk == NBLK - 1 and kc == CPB - 1)
                        for comp in range(ncomp):
                            r0, r1_ = (0, 128) if gqa else (comp * 64, comp * 64 + 64)
                            si = self.ring("S", 2)
                            P.op("pe", lambda h, ki=ki, qi=qi, kc=kc, r0=r0, r1_=r1_, si=si, w=w: h.matmul(
                                self.psum[si][:, :w], kt[ki][r0:r1_, kc * 128:(kc + 1) * 128], qt[qi][r0:r1_, :w],
                                start=True, stop=True), reads=[ktb[ki], qtb[qi]], writes=[self.pb[si]])
                            pi = self.ring("pt", 4)
                            P.op("act", lambda h, si=si, pi=pi, w=w: h.activation(
                                out=pt[pi][:, :w], in_=self.psum[si][:, :w], func=AF.Exp, scale=scale),
                                reads=[self.pb[si]], writes=[ptb[pi]])
                            P.op("pe", lambda h, ki=ki, kc=kc, pi=pi, comp=comp, w=w, first=first, last=last, Ob=Ob: h.matmul(
                                self.psum[Ob[comp]][:, :w], vt[ki][:, kc, :], pt[pi][:, :w], start=first, stop=last),
                                reads=[vtb[ki], ptb[pi]], writes=[self.pb[Ob[comp]]])
                            P.op("pe", lambda h, pi=pi, comp=comp, w=w, first=first, last=last, Zb=Zb: h.matmul(
                                self.psum[Zb[comp]][:, :w], self.ones[:], pt[pi][:, :w], start=first, stop=last),
                                reads=[self.onesb, ptb[pi]], writes=[self.pb[Zb[comp]]])
                P.op("dve", lambda h, w=w, Zb=Zb: h.reciprocal(out=r1[:, :w], in_=self.psum[Zb[0]][:, :w]),
                     reads=[self.pb[Zb[0]]], writes=[r1b])
                if gqa:
                    P.op("dve", lambda h, w=w, u=u, Ob=Ob: h.tensor_tensor(out=oT[:, u, :w], in0=self.psum[Ob[0]][:, :w],
                                                                           in1=r1[:, :w], op=ALU.mult),
                         reads=[self.pb[Ob[0]], r1b], writes=[oTb])
                else:
                    P.op("dve", lambda h, w=w: h.reciprocal(out=r2[:, :w], in_=self.psum[5][:, :w]),
                         reads=[self.pb[5]], writes=[r2b])
                    P.op("dve", lambda h, w=w: h.tensor_tensor(out=t1[:, :w], in0=self.psum[2][:, :w], in1=r1[:, :w],
                                                               op=ALU.mult), reads=[self.pb[2], r1b], writes=[t1b])
                    P.op("dve", lambda h, w=w: h.tensor_tensor(out=t2[:, :w], in0=self.psum[4][:, :w], in1=r2[:, :w],
                                                               op=ALU.mult), reads=[self.pb[4], r2b], writes=[t2b])
                    P.op("dve", lambda h, w=w: h.scalar_tensor_tensor(out=t1[:, :w], in0=t2[:, :w], scalar=neglam,
                                                                      in1=t1[:, :w], op0=ALU.mult, op1=ALU.add),
                         reads=[t1b, t2b, lb], writes=[t1b])
                    P.op("act", lambda h, w=w: h.activation(out=sq[:, :w], in_=t1[:, :w], func=AF.Square),
                         reads=[t1b], writes=[sqb])
                    P.op("pe", lambda h, w=w: h.matmul(self.psum[6][:, :w], self.ones[:], sq[:, :w], start=True, stop=True),
                         reads=[sqb, self.onesb], writes=[self.pb[6]])
                    P.op("act", lambda h, w=w: h.activation(out=r2[:, :w], in_=self.psum[6][:, :w], func=AF.Sqrt,
                                                            bias=self.epsb[:], scale=1.0 / 128),
                         reads=[self.pb[6], self.onesb], writes=[r2b])
                    P.op("dve", lambda h, w=w: h.reciprocal(out=r2[:, :w], in_=r2[:, :w]), reads=[r2b], writes=[r2b])
                    P.op("dve", lambda h, w=w, u=u: h.scalar_tensor_tensor(out=oT[:, u, :w], in0=t1[:, :w], scalar=gsub,
                                                                           in1=r2[:, :w], op0=ALU.mult, op1=ALU.mult),
                         reads=[t1b, r2b, lb], writes=[oTb])
            for dc in range(8):
                pi = 6 + self.ring("oy", 2)
                for u in range(8):
                    P.op("pe", lambda h, dc=dc, u=u, pi=pi, w=w: h.matmul(
                        self.psum[pi][:, :w], wo[:, u, dc * 128:(dc + 1) * 128], oT[:, u, :w],
                        start=(u == 0), stop=(u == 7)), reads=[wob, oTb], writes=[self.pb[pi]])
                P.op("dve", lambda h, dc=dc, pi=pi, c0=c0, w=w: h.scalar_tensor_tensor(
                    out=self.xT[:, dc, c0:c0 + w], in0=self.psum[pi][:, :w], scalar=G(dc),
                    in1=self.xT[:, dc, c0:c0 + w], op0=ALU.mult, op1=ALU.add),
                    reads=[self.pb[pi], self.mbuf], writes=[self.xbuf[ti]])


def blockify(w):
    n = w.shape[1]
    return np.ascontiguousarray(w.reshape(8, 128, n).transpose(1, 0, 2)).reshape(128, 8 * n)


def rope_np(pos, dim, theta):
    inv = (1.0 / (np.float32(theta) ** (np.arange(0, dim, 2, dtype=np.float32) / np.float32(dim)))).astype(np.float32)
    ang = pos.astype(np.float32)[:, None] * inv[None, :]
    return np.cos(ang).astype(np.float32), np.sin(ang).astype(np.float32)


def sigma_diff():
    s = np.arange(64)
    s[0:8] = np.arange(8, 16)
    s[8:16] = np.arange(0, 8)
    return s


def sigma_gqa():
    s = np.arange(128)
    s[0:32] = np.arange(32, 64)
    s[32:64] = np.arange(0, 32)
    s[64:96] = np.arange(96, 128)
    s[96:128] = np.arange(64, 96)
    return s


def rope_tabs(kind, pos):
    n = len(pos)
    if kind == "diff":
        cos, sin = rope_np(pos, 16, 500000.0)
        C = np.ones((64, n), np.float32)
        S = np.zeros((64, n), np.float32)
        C[0:8] = cos.T
        C[8:16] = cos.T
        S[0:8] = -sin.T
        S[8:16] = sin.T
        return np.ascontiguousarray(np.tile(C, (2, 1))), np.ascontiguousarray(np.tile(S, (2, 1)))
    cr, sr = rope_np(pos // 64, 64, 10000.0)
    cc, sc = rope_np(pos % 64, 64, 10000.0)
    C = np.concatenate([cr.T, cr.T, cc.T, cc.T], 0)
    S = np.concatenate([-sr.T, sr.T, -sc.T, sc.T], 0)
    return np.ascontiguousarray(C), np.ascontiguousarray(S)


def qkv_blocks(kind, inp):
    if kind == "diff":
        w = inp["diff_w_qkv"]
        q, k, v = w[:, :1024], w[:, 1024:2048], w[:, 2048:3072]
        sg = sigma_diff()
        hd = 64
    else:
        w = inp["gqa_w_qkv"]
        q, k, v = w[:, :1024], w[:, 1024:1280], w[:, 1280:1536]
        sg = sigma_gqa()
        hd = 128

    def sw(m):
        n = m.shape[1]
        idx = (np.arange(n) // hd) * hd + sg[np.arange(n) % hd]
        return m[:, idx]
    blocks = []
    for m in (q, k):
        ms = sw(m)
        for b in range(m.shape[1] // 256):
            blocks.append(blockify(np.concatenate([m[:, b * 256:(b + 1) * 256], ms[:, b * 256:(b + 1) * 256]], 1)))
    if kind == "diff":
        for b in range(2):
            blocks.append(blockify(np.ascontiguousarray(v[:, b * 512:(b + 1) * 512])))
    else:
        blocks.append(blockify(np.concatenate([v, np.zeros((1024, 256), np.float32)], 1)))
    return np.stack(blocks)


LAUNCHES = [
    dict(name="A", halo=0, stages=[("mod",), ("ffn", 0, 0)]),
    dict(name="B", halo=8, stages=[("pool", 0), ("ffn", 0, 2), ("ffn", 1, 0), ("qkv", "diff", 1)]),
    dict(name="C", halo=0, stages=[("attn", "diff", 1), ("ffn", 1, 2), ("ffn", 2, 0), ("qkv", "gqa", 2)]),
    dict(name="D", halo=0, stages=[("attn", "gqa", 2), ("ffn", 2, 2), ("ffn", 3, 0)]),
    dict(name="E", halo=1, stages=[("conv", 3), ("ffn", 3, 2), ("final",)]),
]


def run_pipeline(inp, B, S, CPBT, launches=LAUNCHES, same=True, trace=False, stop_after=None):
    inp = {k: np.asarray(v) for k, v in inp.items()}
    ncore = B * CPBT
    NT = S // CPBT
    xcur = [np.ascontiguousarray(inp["x"][c // CPBT, (c % CPBT) * NT:((c % CPBT) + 1) * NT, :].T)
            for c in range(ncore)]
    modtab = None
    qkv = None
    times = []
    for L in launches:
        H = L["halo"]
        stages = L["stages"]
        cfg = dict(ntok=NT, halo=H, stages=stages, nkeys=S, same=same, dbg=L.get("dbg", False), pool_ow=L.get("pool_ow", 496))
        bld = Builder(cfg)
        nc = bld.build()
        ffns = [s for s in stages if s[0] == "ffn"]
        shared = {}
        if ffns:
            shared["wgu"] = np.stack([relayout_wgu(inp["ffn_w_gu"][l, 0 if w == 0 else 1]) for _, l, w in ffns])
            shared["wdn"] = np.stack([inp["ffn_w_down"][l, 0 if w == 0 else 1] for _, l, w in ffns])
        for st in stages:
            if st[0] == "mod":
                shared["modw"] = np.ascontiguousarray(inp["mod_w"])
            if st[0] == "pool":
                shared["pool_w"] = np.ascontiguousarray(inp["pool_w"])
            if st[0] == "conv":
                wi = inp["conv_w_in"]
                shared["conv_win"] = np.stack([blockify(np.concatenate(
                    [wi[:, dc * 128:(dc + 1) * 128], wi[:, 1024 + dc * 128:1024 + (dc + 1) * 128],
                     wi[:, 2048 + dc * 128:2048 + (dc + 1) * 128]], 1)) for dc in range(8)])
                shared["conv_wout"] = np.ascontiguousarray(inp["conv_w_out"])
            if st[0] == "qkv":
                shared["wqkv"] = qkv_blocks(st[1], inp)
            if st[0] == "attn":
                shared["wo"] = np.ascontiguousarray(inp["diff_w_o"] if st[1] == "diff" else inp["gqa_w_o"])
                if st[1] == "diff":
                    shared["lam"] = np.ascontiguousarray(
                        np.broadcast_to(inp["diff_lambda"].reshape(1, 256), (128, 256)))
        maps = []
        for c in range(ncore):
            b, q = c // CPBT, c % CPBT
            m = dict(shared)
            cs = build_consts(inp, b)
            sg = sigma_gqa()
            cs[:, COFF["gqa_qg_sw"]] = inp["gqa_q_norm_g"][sg]
            cs[:, COFF["gqa_kg_sw"]] = inp["gqa_k_norm_g"][sg]
            cs[:, COFF["maskL"]] = 0.0 if q == 0 else 1.0
            cs[:, COFF["maskR"]] = 0.0 if q == CPBT - 1 else 1.0
            m["consts"] = cs
            xe = np.zeros((D, NT + 2 * H), np.float32)
            xe[:, H:H + NT] = xcur[c]
            if H:
                if q > 0:
                    xe[:, :H] = xcur[c - 1][:, NT - H:]
                if q < CPBT - 1:
                    xe[:, H + NT:] = xcur[c + 1][:, :H]
            m["xT"] = xe
            if modtab is not None:
                m["modtab_in"] = modtab[c]
            for st in stages:
                if st[0] == "pool":
                    t = q * NT + np.arange(-H, NT + H)
                    inv = np.ones((4, NT + 2 * H), np.float32)
                    for g, win in enumerate((2, 4, 8, 16)):
                        lo = np.clip(t - win // 2, 0, S)
                        hi = np.clip(t + win // 2, 0, S)
                        cnt = (hi - lo).astype(np.float32)
                        inv[g] = np.where(cnt > 0, np.float32(1.0) / np.maximum(cnt, 1), 1.0)
                    m["pool_inv"] = np.ascontiguousarray(np.broadcast_to(inv[None], (128, 4, NT + 2 * H)))
                if st[0] == "qkv":
                    pos = q * NT + np.arange(NT)
                    C, S_ = rope_tabs(st[1], pos)
                    m["ropeC"], m["ropeS"] = C, S_
                if st[0] == "attn":
                    m["q_in"] = qkv["q"][c]
                    m["k_in"] = qkv["kfull"][b]
                    m["v_in"] = qkv["vfull"][b]
            maps.append(m)
        res = run_bass_kernel_spmd(nc, maps, core_ids=list(range(ncore)), trace=trace)
        times.append(res.exec_time_ns)
        R = res.results
        xcur = [np.asarray(R[c]["xout"]) for c in range(ncore)]
        if any(s[0] == "mod" for s in stages):
            modtab = [np.asarray(R[c]["modtab_out"]) for c in range(ncore)]
        qs = [s for s in stages if s[0] == "qkv"]
        if qs:
            KB = min(2048, S)
            NBLK, CPB = S // KB, KB // 128
            kfull, vfull = [], []
            for b in range(B):
                kf = np.concatenate([np.asarray(R[b * CPBT + q]["k_out"]) for q in range(CPBT)], axis=1)
                vf = np.concatenate([np.asarray(R[b * CPBT + q]["v_out"]) for q in range(CPBT)], axis=0)
                VH = vf.shape[1] // 128
                vf = vf.reshape(NBLK, CPB, 128, VH, 128).transpose(3, 0, 2, 1, 4).reshape(VH, NBLK, 128, CPB * 128)
                kfull.append(np.ascontiguousarray(kf))
                vfull.append(np.ascontiguousarray(vf))
            qkv = dict(q=[np.asarray(R[c]["q_out"]) for c in range(ncore)], kfull=kfull, vfull=vfull)
        if stop_after == L["name"]:
            break
    out = np.zeros((B, S, D), np.float32)
    for c in range(ncore):
        b, q = c // CPBT, c % CPBT
        out[b, q * NT:(q + 1) * NT, :] = xcur[c].T
    return out, times


FUSED_STAGES = [("mod",), ("ffn", 0, 0), ("halo",), ("pool", 0), ("ffn", 0, 2), ("ffn", 1, 0), ("qkv", "diff", 1),
                ("gather", "diff"), ("attn", "diff", 1), ("ffn", 1, 2), ("ffn", 2, 0), ("qkv", "gqa", 2),
                ("gather", "gqa"), ("attn", "gqa", 2), ("ffn", 2, 2), ("ffn", 3, 0), ("halo",), ("conv", 3),
                ("ffn", 3, 2), ("final",)]


def run_fused(inp, B, S, CPBT, same=True, trace=False, stages=FUSED_STAGES):
    inp = {k: np.asarray(v) for k, v in inp.items()}
    ncore = B * CPBT
    NT = S // CPBT
    H = 8
    groups = [[b * CPBT + q for q in range(CPBT)] for b in range(B)]
    cfg = dict(ntok=NT, halo=H, stages=stages, nkeys=S, same=same, fused=True, cpbt=CPBT, groups=groups)
    bld = Builder(cfg)
    nc = bld.build()
    ffns = [s for s in stages if s[0] == "ffn"]
    shared = {}
    shared["wgu"] = np.stack([relayout_wgu(inp["ffn_w_gu"][l, 0 if w == 0 else 1]) for _, l, w in ffns])
    shared["wdn"] = np.stack([inp["ffn_w_down"][l, 0 if w == 0 else 1] for _, l, w in ffns])
    shared["modw"] = np.ascontiguousarray(inp["mod_w"])
    shared["pool_w"] = np.ascontiguousarray(inp["pool_w"])
    wi = inp["conv_w_in"]
    shared["conv_win"] = np.stack([blockify(np.concatenate(
        [wi[:, dc * 128:(dc + 1) * 128], wi[:, 1024 + dc * 128:1024 + (dc + 1) * 128],
         wi[:, 2048 + dc * 128:2048 + (dc + 1) * 128]], 1)) for dc in range(8)])
    shared["conv_wout"] = np.ascontiguousarray(inp["conv_w_out"])
    shared["wqkv_diff"] = qkv_blocks("diff", inp)
    shared["wqkv_gqa"] = qkv_blocks("gqa", inp)
    shared["wo_diff"] = np.ascontiguousarray(inp["diff_w_o"])
    shared["wo_gqa"] = np.ascontiguousarray(inp["gqa_w_o"])
    shared["lam"] = np.ascontiguousarray(np.broadcast_to(inp["diff_lambda"].reshape(1, 256), (128, 256)))
    maps = []
    sg = sigma_gqa()
    for c in range(ncore):
        b, q = c // CPBT, c % CPBT
        m = dict(shared)
        cs = build_consts(inp, b)
        cs[:, COFF["gqa_qg_sw"]] = inp["gqa_q_norm_g"][sg]
        cs[:, COFF["gqa_kg_sw"]] = inp["gqa_k_norm_g"][sg]
        cs[:, COFF["maskL"]] = 0.0 if q == 0 else 1.0
        cs[:, COFF["maskR"]] = 0.0 if q == CPBT - 1 else 1.0
        for r in range(min(4, CPBT)):
            cs[:, COFF["selL"] + r] = 1.0 if r == q - 1 else 0.0
            cs[:, COFF["selR"] + r] = 1.0 if r == q + 1 else 0.0
        m["consts"] = cs
        xe = np.zeros((D, NT + 2 * H), np.float32)
        xe[:, H:H + NT] = inp["x"][b, q * NT:(q + 1) * NT, :].T
        m["xT"] = xe
        t = q * NT + np.arange(-H, NT + H)
        inv = np.ones((4, NT + 2 * H), np.float32)
        for g, win in enumerate((2, 4, 8, 16)):
            lo = np.clip(t - win // 2, 0, S)
            hi = np.clip(t + win // 2, 0, S)
            cnt = (hi - lo).astype(np.float32)
            inv[g] = np.where(cnt > 0, np.float32(1.0) / np.maximum(cnt, 1), 1.0)
        m["pool_inv"] = np.ascontiguousarray(np.broadcast_to(inv[None], (128, 4, NT + 2 * H)))
        pos = q * NT + np.arange(NT)
        for kind in ("diff", "gqa"):
            C, S_ = rope_tabs(kind, pos)
            m["ropeC_" + kind], m["ropeS_" + kind] = C, S_
        maps.append(m)
    need = set()
    for st in stages:
        need.add(st[0] + ("_" + st[1] if st[0] in ("qkv", "attn") else ""))
    res = run_bass_kernel_spmd(nc, maps, core_ids=list(range(ncore)), trace=trace)
    out = np.zeros((B, S, D), np.float32)
    for c in range(ncore):
        b, q = c // CPBT, c % CPBT
        out[b, q * NT:(q + 1) * NT, :] = np.asarray(res.results[c]["xout"]).T
    return out, res.exec_time_ns


def kernel(**inputs):
    out, _ = run_fused(inputs, 2, SEQ, 4)
    return out
```

```python
import numpy as np
from contextlib import ExitStack
import concourse.bass as bass
import concourse.mybir as mybir
from concourse.bass_utils import run_bass_kernel_spmd

F32 = mybir.dt.float32
BF16 = mybir.dt.bfloat16
ALU = mybir.AluOpType
AF = mybir.ActivationFunctionType

D = 1024
DFF = 2816
NFC = DFF // 128
NL = 4
SEQ = 16384
NCORE = 8
TT = 512
EPS = 1e-6


class Buf:
    __slots__ = ("w", "r", "name")

    def __init__(self, name=""):
        self.w = None
        self.r = {}
        self.name = name


class Prog:
    CENG = ["pe", "dve", "act", "pool"]
    DMAQ = ["sp", "pool"]
    R = 6

    def __init__(self, nc, stack, same_engine_sync=True):
        self.nc = nc
        self.same = same_engine_sync
        self.streams = {e: [] for e in ["pe", "dve", "act", "pool", "sp"]}
        self.ncomp = {e: 0 for e in self.CENG}
        self.comp_ops = {e: [] for e in self.CENG}
        self.ndma = {q: 0 for q in self.DMAQ}
        self.sem = {e: stack.enter_context(nc.semaphore("s_" + e)) for e in self.CENG}
        self.dsem = {q: [stack.enter_context(nc.semaphore(f"d_{q}{i}")) for i in range(self.R)]
                     for q in self.DMAQ}
        self.all_dma = []
        self.base_dep = None
        self.csem = []
        self.stack = stack

    def _deps(self, reads, writes):
        deps = set()
        if self.base_dep is not None:
            deps.add(self.base_dep)
        for b in reads:
            if b.w is not None:
                deps.add(b.w)
        for b in writes:
            if b.w is not None:
                deps.add(b.w)
            for k, v in b.r.items():
                if isinstance(k, tuple):
                    deps.add(k)
                else:
                    deps.add(("c", k, v))
        return deps

    def _mark(self, tok, reads, writes):
        for b in reads:
            if tok[0] == "c":
                b.r[tok[1]] = tok[2]
            else:
                b.r[tok] = True
        for b in writes:
            b.w = tok
            b.r = {}

    def op(self, eng, fn, reads=(), writes=()):
        self.total_ops = getattr(self, "total_ops", 0) + 1
        if self.total_ops > getattr(self, "limit", 10 ** 9):
            return None
        idx = self.ncomp[eng]
        self.ncomp[eng] += 1
        deps = self._deps(reads, writes)
        tok = ("c", eng, idx)
        o = dict(kind="c", fn=fn, deps=deps, inc=False, idx=idx)
        self.comp_ops[eng].append(o)
        self.streams[eng].append(o)
        self._mark(tok, reads, writes)
        return tok

    def dma(self, q, fn, reads=(), writes=()):
        j = self.ndma[q]
        self.ndma[q] += 1
        deps = self._deps(reads, writes)
        tok = ("d", q, j)
        o = dict(kind="d", fn=fn, deps=deps, q=q, j=j)
        self.streams[q].append(o)
        self.all_dma.append(tok)
        self._mark(tok, reads, writes)
        return tok

    def cc(self, fn, reads=(), writes=()):
        k = len(self.csem)
        self.csem.append(self.stack.enter_context(self.nc.semaphore(f"cc{k}")))
        deps = self._deps(reads, writes)
        tok = ("s", k)
        o = dict(kind="s", fn=fn, deps=deps, k=k)
        self.streams["pool"].append(o)
        self.all_dma.append(tok)
        self._mark(tok, reads, writes)
        return tok

    def finalize(self):
        for e, ops in self.streams.items():
            for o in ops:
                for d in o["deps"]:
                    if d[0] == "c":
                        if d[1] == e and o["kind"] == "c" and (not self.same or e == "pe"):
                            continue
                        self.comp_ops[d[1]][d[2]]["inc"] = True
        self.val = {}
        for e in self.CENG:
            c = 0
            vals = []
            for o in self.comp_ops[e]:
                if o["inc"]:
                    c += 1
                vals.append(c)
            self.val[e] = vals

    def emit(self, eng, h):
        seen = {}
        R = self.R
        nwait = 0
        for o in self.streams[eng]:
            needs = {}
            for d in o["deps"]:
                if d[0] == "c":
                    if d[1] == eng and o["kind"] == "c" and (not self.same or eng == "pe"):
                        continue
                    s = self.sem[d[1]]
                    v = self.val[d[1]][d[2]]
                    key = ("c", d[1])
                elif d[0] == "s":
                    s = self.csem[d[1]]
                    v = 1
                    key = ("s", d[1])
                else:
                    s = self.dsem[d[1]][d[2] % R]
                    v = 16 * (d[2] // R + 1)
                    key = ("d", d[1], d[2] % R)
                if seen.get(key, 0) >= v:
                    continue
                if key not in needs or needs[key][1] < v:
                    needs[key] = (s, v)
            if o["kind"] == "d":
                j = o["j"]
                if j >= R:
                    key = ("d", o["q"], j % R)
                    v = 16 * (j // R)
                    if seen.get(key, 0) < v and (key not in needs or needs[key][1] < v):
                        needs[key] = (self.dsem[o["q"]][j % R], v)
            for key, (s, v) in needs.items():
                h.wait_ge(s, v)
                seen[key] = v
                nwait += 1
            ins = o["fn"](h)
            if o["kind"] == "c":
                if o["inc"]:
                    ins.then_inc(self.sem[eng], 1)
            elif o["kind"] == "s":
                ins.then_inc(self.csem[o["k"]], 1)
            else:
                ins.then_inc(self.dsem[o["q"]][o["j"] % R], 16)
        return nwait

    def final_waits(self, eng, h, toks, q):
        R = self.R
        for d in toks:
            if d[0] == "d" and d[1] == q:
                h.wait_ge(self.dsem[d[1]][d[2] % R], 16 * (d[2] // R + 1))


def const_layout():
    off = {}
    n = 0

    def add(name, w):
        nonlocal n
        off[name] = n
        n += w
    add("c", 8)
    for l in range(NL):
        add(f"modb{l}", 72)
        add(f"ng{l}", 24)
    add("final_g", 8)
    add("pool_scale", 8)
    add("conv_w", 24)
    add("gqa_qg", 1)
    add("gqa_kg", 1)
    add("subln_g", 1)
    add("gqa_qg_sw", 1)
    add("gqa_kg_sw", 1)
    add("maskL", 1)
    add("maskR", 1)
    add("selL", 4)
    add("selR", 4)
    return off, n


COFF, NCONST = const_layout()


def col128(v):
    v = np.asarray(v, np.float32).reshape(-1, 128)
    return np.ascontiguousarray(v.T)


def build_consts(inp, b):
    cs = np.zeros((128, NCONST), np.float32)
    cs[:, COFF["c"]:COFF["c"] + 8] = col128(inp["c"][b])
    for l in range(NL):
        cs[:, COFF[f"modb{l}"]:COFF[f"modb{l}"] + 72] = col128(inp["mod_b"][l])
        cs[:, COFF[f"ng{l}"]:COFF[f"ng{l}"] + 24] = col128(inp["norm_g"][l].reshape(-1))
    cs[:, COFF["final_g"]:COFF["final_g"] + 8] = col128(inp["final_g"])
    cs[:, COFF["pool_scale"]:COFF["pool_scale"] + 8] = col128(inp["pool_scale"])
    cs[:, COFF["conv_w"]:COFF["conv_w"] + 24] = col128(inp["conv_w"].reshape(-1))
    cs[:, COFF["gqa_qg"]] = inp["gqa_q_norm_g"]
    cs[:, COFF["gqa_kg"]] = inp["gqa_k_norm_g"]
    cs[:, COFF["subln_g"]] = inp["diff_subln_g"]
    return cs


def relayout_wgu(w):
    w = np.asarray(w, np.float32)
    g = w[:, :DFF].reshape(8, 128, 11, 256)
    u = w[:, DFF:].reshape(8, 128, 11, 256)
    r = np.concatenate([g, u], axis=3)
    r = r.transpose(2, 1, 0, 3)
    return np.ascontiguousarray(r).reshape(11, 128, 8 * 512)


class Region:
    def __init__(self, t, n):
        self.t, self.n, self.o = t, n, 0

    def reset(self):
        self.o = 0

    def take(self, *shape):
        sz = int(np.prod(shape))
        assert self.o + sz <= self.n, ("region overflow", self.o, sz, self.n)
        ap = self.t[:, self.o:self.o + sz]
        self.o += sz
        if len(shape) == 2:
            ap = ap.rearrange("p (a b) -> p a b", a=shape[0])
        return ap


FREG = 5120
BREG = 27648


class Builder:
    def __init__(self, cfg):
        self.cfg = cfg
        self.NT = cfg["ntok"]
        self.H = cfg.get("halo", 0)
        self.NTE = self.NT + 2 * self.H
        self.NK = cfg.get("nkeys", 0)
        self.nc = bass.Bass("TRN2", target_bir_lowering=False)
        self.stack = ExitStack()
        self.P = Prog(self.nc, self.stack, same_engine_sync=cfg.get("same", True))
        self.rr = {}
        self.P.limit = cfg.get("limit", 10 ** 9)
        self.outs = []
        self.fused = cfg.get("fused", False)
        self.cpbt = cfg.get("cpbt", 1)
        self.groups = cfg.get("groups", [[0]])
        self.scr = {}
        self.nhalo = 0
        self.tiles = [(self.H + t0, min(TT, self.NT - t0)) for t0 in range(0, self.NT, TT)]

    def din(self, name, shape, dt=F32):
        return self.nc.dram_tensor(name, list(shape), dt, kind="ExternalInput").ap()

    def dout(self, name, shape, dt=F32):
        return self.nc.dram_tensor(name, list(shape), dt, kind="ExternalOutput").ap()

    def sb(self, name, shape, dt=F32):
        return self.stack.enter_context(self.nc.sbuf_tensor(name, list(shape), dt))

    def dbg(self, name, ap, shape, dt, reads):
        if not self.cfg.get("dbg"):
            return
        o = self.dout("dbg_" + name, shape, dt)
        self.outs.append(self.P.dma("sp", lambda h: h.dma_start(out=o, in_=ap), reads=reads))

    def ring(self, key, n):
        i = self.rr.get(key, 0)
        self.rr[key] = i + 1
        return i % n

    def new_stage(self):
        self.barrier()
        self.FR.reset()
        self.BR.reset()
        self.pb = [Buf(f"psum{i}") for i in range(8)]

    def barrier(self):
        P = self.P
        b = Buf("bar")
        for e in Prog.CENG:
            if P.ncomp[e] > 0:
                b.r[e] = P.ncomp[e] - 1
        for t in P.all_dma:
            b.r[t] = True
        P.all_dma = []
        bt = self.bar_t
        tok = P.op("dve", lambda h: h.memset(bt[:, 0:1], 0.0), writes=[b])
        P.base_dep = tok

    def xbuf_of(self, c0, w):
        bs = []
        for i, (t0, tw) in enumerate(self.tiles):
            if c0 < t0 + tw and c0 + w > t0:
                bs.append(self.xbuf[i])
        if c0 < self.H or c0 + w > self.H + self.NT:
            bs.append(self.xhalo)
        return bs

    def build(self):
        cfg = self.cfg
        nc, P, NT, H, NTE = self.nc, self.P, self.NT, self.H, self.NTE
        stages = cfg["stages"]
        xT_d = self.din("xT", [D, NTE])
        consts_d = self.din("consts", [128, NCONST])

        self.xT = self.sb("xT_sb", [128, 8, NTE])
        self.xbuf = [Buf(f"x{t}") for t in range(len(self.tiles))]
        self.xhalo = Buf("xhalo")
        self.consts = self.sb("consts_sb", [128, NCONST])
        self.cbuf = Buf("consts")
        self.modtab = self.sb("modtab", [128, 480])
        self.modv = self.modtab[:, 0:288]
        self.modA = self.modtab[:, 288:384]
        self.modG = self.modtab[:, 384:480]
        self.mbuf = Buf("mod")
        self.ones = self.sb("ones", [128, 128], BF16)
        self.onesb = Buf("ones")
        self.bar_t = self.sb("bar", [128, 2])
        self.epsb = self.sb("epsb", [128, 1])
        self.FR = Region(self.sb("freg", [128, FREG], F32), FREG)
        self.BR = Region(self.sb("breg", [128, BREG], BF16), BREG)
        self.psum = [self.stack.enter_context(nc.psum_tensor(f"pb{i}", [128, 512], F32))
                     for i in range(8)]
        self.pb = [Buf(f"psum{i}") for i in range(8)]

        P.dma("sp", lambda h: h.dma_start(out=self.consts[:], in_=consts_d[:, :]), writes=[self.cbuf])
        xv = xT_d.rearrange("(c p) n -> p c n", p=128)
        for i, (c0, w) in enumerate(self.tiles):
            a, b = c0, c0 + w
            wr = [self.xbuf[i]]
            if i == 0:
                a = 0
                wr.append(self.xhalo)
            if i == len(self.tiles) - 1:
                b = NTE
                wr.append(self.xhalo)
            P.dma("sp", lambda h, a=a, b=b: h.dma_start(out=self.xT[:, :, a:b], in_=xv[:, :, a:b]),
                  writes=wr)
        P.op("dve", lambda h: h.memset(self.ones[:], 1.0), writes=[self.onesb])
        P.op("dve", lambda h: h.memset(self.epsb[:], EPS), writes=[self.onesb])

        if any(s[0] == "mod" for s in stages):
            modw_d = self.din("modw", [NL, D, 9 * D])
            self.stage_mod(modw_d)
            mt_o = self.dout("modtab_out", [128, 480])
            self.outs.append(P.dma("sp", lambda h: h.dma_start(out=mt_o[:, :], in_=self.modtab[:]),
                                   reads=[self.mbuf]))
        else:
            mt_i = self.din("modtab_in", [128, 480])
            P.dma("sp", lambda h: h.dma_start(out=self.modtab[:], in_=mt_i[:, :]), writes=[self.mbuf])

        dr = {}
        for st in stages:
            kind = st[0]
            if kind == "ffn":
                if "wgu" not in dr:
                    nf = sum(1 for s in stages if s[0] == "ffn")
                    dr["wgu"] = self.din("wgu", [nf, 11, 128, 8 * 512])
                    dr["wdn"] = self.din("wdn", [nf, DFF, D])
                    dr["nf"] = 0
                    self.nffn = nf
                    self.wgu_f32, self.wdn_f32 = dr["wgu"], dr["wdn"]
                    self.wgu_bf = self.nc.dram_tensor("wgu_bf", [nf, 11, 128, 8 * 512], BF16).ap()
                    self.wdn_bf = self.nc.dram_tensor("wdn_bf", [nf, DFF, D], BF16).ap()
                    self.wsb = [Buf(f"wsb{i}") for i in range(nf)]
                    self.precast(0)
                self.stage_ffn(st[1], st[2], dr["wgu"], dr["wdn"], dr["nf"])
                dr["nf"] += 1
            elif kind == "final":
                self.stage_final()
            elif kind == "pool":
                self.stage_pool(st[1])
            elif kind == "conv":
                self.stage_conv(st[1])
            elif kind == "qkv":
                self.stage_qkv(st[1], st[2])
            elif kind == "attn":
                self.stage_attn(st[1], st[2])
            elif kind == "halo":
                self.stage_halo()
            elif kind == "gather":
                self.stage_gather(st[1])

        xo_d = self.dout("xout", [D, NT])
        ov = xo_d.rearrange("(c p) n -> p c n", p=128)
        for i, (c0, w) in enumerate(self.tiles):
            self.outs.append(P.dma("sp", lambda h, c0=c0, w=w: h.dma_start(
                out=ov[:, :, c0 - H:c0 - H + w], in_=self.xT[:, :, c0:c0 + w]), reads=[self.xbuf[i]]))
        self.emit_all(self.outs)
        return nc

    def emit_all(self, outs):
        P, nc = self.P, self.nc
        P.finalize()
        self.nwaits = {}
        with nc.Block() as block:
            @block.tensor
            def _(e):
                self.nwaits["pe"] = P.emit("pe", e)

            @block.vector
            def _(e):
                self.nwaits["dve"] = P.emit("dve", e)

            @block.scalar
            def _(e):
                self.nwaits["act"] = P.emit("act", e)

            @block.gpsimd
            def _(e):
                self.nwaits["pool"] = P.emit("pool", e)
                P.final_waits("pool", e, outs, "pool")

            @block.sync
            def _(e):
                self.nwaits["sp"] = P.emit("sp", e)
                P.final_waits("sp", e, outs, "sp")
        self.stack.close()

    def stage_mod(self, modw_d):
        P, nc = self.P, self.nc
        self.new_stage()
        cact = self.sb("cact", [128, 8], BF16)[:]
        cb = Buf("cact")
        wm = [self.BR.take(8, 512) for i in range(2)]
        wmb = [Buf(f"wm{i}") for i in range(2)]
        mps, mpb = self.psum[0], self.pb[0]
        co = COFF["c"]
        P.op("act", lambda h: h.activation(out=cact, in_=self.consts[:, co:co + 8], func=AF.Silu),
             reads=[self.cbuf], writes=[cb])
        for l in range(NL):
            mv = modw_d[l].rearrange("(c p) n -> p c n", p=128)
            for piece in range(18):
                s = self.ring("wm", 2)
                P.dma("pool", lambda h, s=s, piece=piece, mv=mv: h.dma_start(
                    out=wm[s], in_=mv[:, :, piece * 512:(piece + 1) * 512]), writes=[wmb[s]])
                for jj in range(4):
                    j = piece * 4 + jj
                    for c in range(8):
                        P.op("pe", lambda h, s=s, jj=jj, c=c, j=j: h.matmul(
                            mps[:, j:j + 1], wm[s][:, c, jj * 128:(jj + 1) * 128], cact[:, c:c + 1],
                            start=(c == 0), stop=(c == 7)),
                            reads=[wmb[s], cb], writes=[mpb])
            bo = COFF[f"modb{l}"]
            P.op("dve", lambda h, l=l, bo=bo: h.tensor_tensor(
                out=self.modv[:, l * 72:(l + 1) * 72], in0=mps[:, 0:72],
                in1=self.consts[:, bo:bo + 72], op=ALU.add),
                reads=[mpb, self.cbuf], writes=[self.mbuf])
            go = COFF[f"ng{l}"]
            for k in range(3):
                sc = self.modv[:, l * 72 + (3 * k + 1) * 8: l * 72 + (3 * k + 2) * 8]
                P.op("dve", lambda h, l=l, k=k, sc=sc, go=go: h.scalar_tensor_tensor(
                    out=self.modA[:, l * 24 + k * 8: l * 24 + (k + 1) * 8], in0=sc, scalar=1.0,
                    in1=self.consts[:, go + k * 8: go + (k + 1) * 8], op0=ALU.add, op1=ALU.mult),
                    reads=[self.mbuf, self.cbuf], writes=[self.mbuf])
                gt = self.modv[:, l * 72 + (3 * k + 2) * 8: l * 72 + (3 * k + 3) * 8]
                P.op("dve", lambda h, l=l, k=k, gt=gt: h.tensor_scalar(
                    out=self.modG[:, l * 24 + k * 8: l * 24 + (k + 1) * 8], in0=gt,
                    scalar1=(1.0 if k == 1 else 0.5), scalar2=None, op0=ALU.mult),
                    reads=[self.mbuf], writes=[self.mbuf])

    def alloc_norm_scratch(self, ssbank, tmp=True):
        S = {}
        S["sq"] = [self.BR.take(TT) for i in range(2)]
        S["sqb"] = [Buf() for _ in range(2)]
        S["ssp"] = self.psum[ssbank]
        S["sspb"] = self.pb[ssbank]
        S["rstd"] = self.FR.take(TT)
        S["rstdb"] = Buf()
        if tmp:
            S["tmp"] = [self.FR.take(TT) for i in range(2)]
            S["tmpb"] = [Buf() for _ in range(2)]
        return S

    def rstd_cols(self, c0, w, S):
        P = self.P
        xb = self.xbuf_of(c0, w)
        for c in range(8):
            s = self.ring("sq", 2)
            P.op("act", lambda h, c=c, s=s: h.activation(out=S["sq"][s][:, :w], in_=self.xT[:, c, c0:c0 + w],
                                                         func=AF.Square),
                 reads=xb, writes=[S["sqb"][s]])
            P.op("pe", lambda h, c=c, s=s: h.matmul(S["ssp"][:, :w], self.ones[:], S["sq"][s][:, :w],
                                                    start=(c == 0), stop=(c == 7)),
                 reads=[S["sqb"][s], self.onesb], writes=[S["sspb"]])
        P.op("act", lambda h: h.activation(out=S["rstd"][:, :w], in_=S["ssp"][:, :w], func=AF.Sqrt,
                                           bias=self.epsb[:], scale=1.0 / D),
             reads=[S["sspb"], self.onesb], writes=[S["rstdb"]])
        P.op("dve", lambda h: h.reciprocal(out=S["rstd"][:, :w], in_=S["rstd"][:, :w]),
             reads=[S["rstdb"]], writes=[S["rstdb"]])

    def modulate_cols(self, l, k, c0, w, S, dst, dstb):
        P = self.P
        A = lambda c: self.modA[:, l * 24 + k * 8 + c: l * 24 + k * 8 + c + 1]
        SH = lambda c: self.modv[:, l * 72 + 3 * k * 8 + c: l * 72 + 3 * k * 8 + c + 1]
        xb = self.xbuf_of(c0, w)
        for c in range(8):
            s = self.ring("tmp", 2)
            P.op("dve", lambda h, c=c, s=s: h.scalar_tensor_tensor(
                out=S["tmp"][s][:, :w], in0=self.xT[:, c, c0:c0 + w], scalar=A(c), in1=S["rstd"][:, :w],
                op0=ALU.mult, op1=ALU.mult),
                reads=xb + [S["rstdb"], self.mbuf], writes=[S["tmpb"][s]])
            P.op("act", lambda h, c=c, s=s: h.activation(
                out=dst(c), in_=S["tmp"][s][:, :w], func=AF.Identity, bias=SH(c), scale=1.0),
                reads=[S["tmpb"][s], self.mbuf], writes=[dstb])

    def precast(self, fi):
        P = self.P
        for G2 in range(11):
            P.dma("pool", lambda h, G2=G2: h.dma_start(out=self.wgu_bf[fi, G2], in_=self.wgu_f32[fi, G2]),
                  writes=[self.wsb[fi]])
        for j in range(11):
            P.dma("pool", lambda h, j=j: h.dma_start(out=self.wdn_bf[fi, j * 256:(j + 1) * 256, :],
                                                     in_=self.wdn_f32[fi, j * 256:(j + 1) * 256, :]),
                  writes=[self.wsb[fi]])

    def stage_ffn(self, l, which, wgu_d, wdn_d, fi):
        P, nc = self.P, self.nc
        self.new_stage()
        if fi + 1 < self.nffn:
            self.precast(fi + 1)
        wgu_d, wdn_d = self.wgu_bf, self.wdn_bf
        wrd = [self.wsb[fi]]
        k = which
        S = self.alloc_norm_scratch(4)
        hT = self.BR.take(8, TT)
        hb = Buf("hT")
        aT = self.BR.take(NFC, TT)
        ab = [Buf(f"a{f}") for f in range(NFC)]
        wg = [self.BR.take(8, 512) for i in range(2)]
        wgb = [Buf(f"wg{i}") for i in range(2)]
        wd = [self.BR.take(2, 512) for i in range(3)]
        wdb = [Buf(f"wd{i}") for i in range(3)]
        sg = [self.FR.take(TT) for i in range(2)]
        sgb = [Buf(f"sg{i}") for i in range(2)]
        G = lambda c: self.modG[:, l * 24 + k * 8 + c: l * 24 + k * 8 + c + 1]
        wdv = wdn_d[fi].rearrange("(f p) n -> p f n", p=128)
        for ti, (c0, w) in enumerate(self.tiles):
            xs = lambda c, c0=c0, w=w: self.xT[:, c, c0:c0 + w]
            self.rstd_cols(c0, w, S)
            self.modulate_cols(l, k, c0, w, S, lambda c, w=w: hT[:, c, :w], hb)
            for G2 in range(11):
                ws = self.ring("wg", 2)
                P.dma("sp", lambda h, ws=ws, G2=G2: h.dma_start(
                    out=wg[ws].rearrange("p c n -> p (c n)"), in_=wgu_d[fi, G2]), reads=wrd, writes=[wgb[ws]])
                for ff in range(2):
                    f = G2 * 2 + ff
                    pi = self.ring("gu", 2)
                    gp, gb = self.psum[pi], self.pb[pi]
                    up, ub = self.psum[2 + pi], self.pb[2 + pi]
                    for c in range(8):
                        P.op("pe", lambda h, c=c, ws=ws, ff=ff, gp=gp, w=w: h.matmul(
                            gp[:, :w], wg[ws][:, c, ff * 128:(ff + 1) * 128], hT[:, c, :w],
                            start=(c == 0), stop=(c == 7)),
                            reads=[wgb[ws], hb], writes=[gb])
                    for c in range(8):
                        P.op("pe", lambda h, c=c, ws=ws, ff=ff, up=up, w=w: h.matmul(
                            up[:, :w], wg[ws][:, c, 256 + ff * 128:256 + (ff + 1) * 128], hT[:, c, :w],
                            start=(c == 0), stop=(c == 7)),
                            reads=[wgb[ws], hb], writes=[ub])
                    P.op("act", lambda h, pi=pi, gp=gp, w=w: h.activation(out=sg[pi][:, :w], in_=gp[:, :w],
                                                                          func=AF.Silu),
                         reads=[gb], writes=[sgb[pi]])
                    P.op("dve", lambda h, pi=pi, f=f, up=up, w=w: h.tensor_tensor(
                        out=aT[:, f, :w], in0=sg[pi][:, :w], in1=up[:, :w], op=ALU.mult),
                        reads=[sgb[pi], ub], writes=[ab[f]])
            for half in range(2):
                for fp in range(11):
                    ws = self.ring("wd", 3)
                    P.dma("sp", lambda h, ws=ws, fp=fp, half=half: h.dma_start(
                        out=wd[ws], in_=wdv[:, 2 * fp:2 * fp + 2, half * 512:(half + 1) * 512]),
                        reads=wrd, writes=[wdb[ws]])
                    for ff in range(2):
                        f = fp * 2 + ff
                        for dq in range(4):
                            P.op("pe", lambda h, ws=ws, ff=ff, f=f, dq=dq, w=w: h.matmul(
                                self.psum[4 + dq][:, :w], wd[ws][:, ff, dq * 128:(dq + 1) * 128], aT[:, f, :w],
                                start=(f == 0), stop=(f == NFC - 1)),
                                reads=[wdb[ws], ab[f]], writes=[self.pb[4 + dq]])
                for dq in range(4):
                    dc = half * 4 + dq
                    P.op("dve", lambda h, dc=dc, dq=dq, xs=xs, w=w: h.scalar_tensor_tensor(
                        out=xs(dc), in0=self.psum[4 + dq][:, :w], scalar=G(dc), in1=xs(dc),
                        op0=ALU.mult, op1=ALU.add),
                        reads=[self.pb[4 + dq], self.mbuf], writes=[self.xbuf[ti]])

    def stage_final(self):
        P = self.P
        self.new_stage()
        S = self.alloc_norm_scratch(0, tmp=False)
        go = COFF["final_g"]
        for ti, (c0, w) in enumerate(self.tiles):
            xs = lambda c, c0=c0, w=w: self.xT[:, c, c0:c0 + w]
            self.rstd_cols(c0, w, S)
            for c in range(8):
                P.op("dve", lambda h, c=c, xs=xs, w=w: h.scalar_tensor_tensor(
                    out=xs(c), in0=xs(c), scalar=self.consts[:, go + c:go + c + 1], in1=S["rstd"][:, :w],
                    op0=ALU.mult, op1=ALU.mult),
                    reads=[S["rstdb"], self.cbuf], writes=[self.xbuf[ti]])


    def stage_pool(self, l):
        P, H, NT = self.P, self.H, self.NT
        assert H == 8
        self.new_stage()
        k = 1
        pw_d = self.din("pool_w", [4, 256, 256])
        inv_d = self.din("pool_inv", [128, 4, self.NTE])
        S = self.alloc_norm_scratch(0, tmp=False)
        hx = self.FR.take(2, TT)
        sA = self.FR.take(2, TT)
        sB = self.FR.take(2, TT)
        invt = self.FR.take(TT)
        hxb, sAb, sBb, invb = Buf(), Buf(), Buf(), Buf()
        dg = [self.BR.take(8, TT) for _ in range(2)]
        dgb = [Buf() for _ in range(2)]
        pw = self.BR.take(8, 256)
        pwb = Buf()
        gsv = self.sb("gsv", [128, 8])
        gsb = Buf()
        P.dma("pool", lambda h: h.dma_start(out=pw, in_=pw_d.rearrange("g (cc p) e -> p (g cc) e", p=128)),
              writes=[pwb])
        po = COFF["pool_scale"]
        P.op("dve", lambda h: h.tensor_tensor(out=gsv[:], in0=self.modG[:, l * 24 + 8: l * 24 + 16],
                                              in1=self.consts[:, po:po + 8], op=ALU.mult),
             reads=[self.mbuf, self.cbuf], writes=[gsb])
        A = lambda c: self.modA[:, l * 24 + k * 8 + c: l * 24 + k * 8 + c + 1]
        SH = lambda c: self.modv[:, l * 72 + 3 * k * 8 + c: l * 72 + 3 * k * 8 + c + 1]
        mL = self.consts[:, COFF["maskL"]:COFF["maskL"] + 1]
        mR = self.consts[:, COFF["maskR"]:COFF["maskR"] + 1]
        OW = self.cfg.get("pool_ow", 496)
        otiles = [(H + o, min(OW, NT - o)) for o in range(0, NT, OW)]

        def finish(i):
            o0, ow = otiles[i]
            d = dg[i % 2]
            for g in range(4):
                for ec in range(2):
                    dc = 2 * g + ec
                    pi = 1 + self.ring("py", 4)
                    for cc in range(2):
                        P.op("pe", lambda h, g=g, ec=ec, cc=cc, pi=pi, d=d, ow=ow: h.matmul(
                            self.psum[pi][:, :ow], pw[:, g * 2 + cc, ec * 128:(ec + 1) * 128], d[:, g * 2 + cc, :ow],
                            start=(cc == 0), stop=(cc == 1)),
                            reads=[pwb, dgb[i % 2]], writes=[self.pb[pi]])
                    P.op("dve", lambda h, dc=dc, pi=pi, o0=o0, ow=ow: h.scalar_tensor_tensor(
                        out=self.xT[:, dc, o0:o0 + ow], in0=self.psum[pi][:, :ow], scalar=gsv[:, dc:dc + 1],
                        in1=self.xT[:, dc, o0:o0 + ow], op0=ALU.mult, op1=ALU.add),
                        reads=[self.pb[pi], gsb], writes=self.xbuf_of(o0, ow))

        for i, (o0, ow) in enumerate(otiles):
            e0, ew = o0 - 8, ow + 16
            xb = self.xbuf_of(e0, ew)
            self.rstd_cols(e0, ew, S)
            d = dg[i % 2]
            for g in range(4):
                P.dma("sp", lambda h, g=g, e0=e0, ew=ew: h.dma_start(out=invt[:, :ew], in_=inv_d[:, g, e0:e0 + ew]),
                      writes=[invb])
                for cc in range(2):
                    c = 2 * g + cc
                    P.op("dve", lambda h, c=c, cc=cc, e0=e0, ew=ew: h.scalar_tensor_tensor(
                        out=hx[:, cc, :ew], in0=self.xT[:, c, e0:e0 + ew], scalar=A(c), in1=S["rstd"][:, :ew],
                        op0=ALU.mult, op1=ALU.mult), reads=xb + [S["rstdb"], self.mbuf], writes=[hxb])
                    P.op("act", lambda h, c=c, cc=cc, ew=ew: h.activation(
                        out=hx[:, cc, :ew], in_=hx[:, cc, :ew], func=AF.Identity, bias=SH(c), scale=1.0),
                        reads=[hxb, self.mbuf], writes=[hxb])
                if i == 0:
                    P.op("dve", lambda h: h.tensor_scalar(out=hx[:, :, 0:8], in0=hx[:, :, 0:8], scalar1=mL,
                                                          scalar2=None, op0=ALU.mult),
                         reads=[hxb, self.cbuf], writes=[hxb])
                if i == len(otiles) - 1:
                    P.op("dve", lambda h, ew=ew: h.tensor_scalar(out=hx[:, :, ew - 8:ew], in0=hx[:, :, ew - 8:ew],
                                                                 scalar1=mR, scalar2=None, op0=ALU.mult),
                         reads=[hxb, self.cbuf], writes=[hxb])
                P.op("dve", lambda h, ew=ew: h.tensor_tensor(out=sA[:, :, 1:ew], in0=hx[:, :, 0:ew - 1],
                                                             in1=hx[:, :, 1:ew], op=ALU.add),
                     reads=[hxb], writes=[sAb])
                fin, finb = sA, sAb
                if g >= 1:
                    P.op("dve", lambda h, ew=ew: h.tensor_tensor(out=sB[:, :, 2:ew - 1], in0=sA[:, :, 1:ew - 2],
                                                                 in1=sA[:, :, 3:ew], op=ALU.add),
                         reads=[sAb], writes=[sBb])
                    fin, finb = sB, sBb
                if g >= 2:
                    P.op("dve", lambda h, ew=ew: h.tensor_tensor(out=sA[:, :, 4:ew - 3], in0=sB[:, :, 2:ew - 5],
                                                                 in1=sB[:, :, 6:ew - 1], op=ALU.add),
                         reads=[sBb], writes=[sAb])
                    fin, finb = sA, sAb
                if g >= 3:
                    P.op("dve", lambda h, ew=ew: h.tensor_tensor(out=sB[:, :, 8:ew - 7], in0=sA[:, :, 4:ew - 11],
                                                                 in1=sA[:, :, 12:ew - 3], op=ALU.add),
                         reads=[sAb], writes=[sBb])
                    fin, finb = sB, sBb
                for cc in range(2):
                    P.op("dve", lambda h, cc=cc, fin=fin, ow=ow: h.tensor_tensor(
                        out=fin[:, cc, 8:8 + ow], in0=fin[:, cc, 8:8 + ow], in1=invt[:, 8:8 + ow], op=ALU.mult),
                        reads=[finb, invb], writes=[finb])
                    P.op("dve", lambda h, cc=cc, fin=fin, ow=ow, g=g, d=d: h.tensor_tensor(
                        out=d[:, g * 2 + cc, :ow], in0=fin[:, cc, 8:8 + ow], in1=hx[:, cc, 8:8 + ow],
                        op=ALU.subtract),
                        reads=[finb, hxb], writes=[dgb[i % 2]])
            if i >= 1:
                finish(i - 1)
        finish(len(otiles) - 1)

    def stage_conv(self, l):
        P, H, NT = self.P, self.H, self.NT
        assert H >= 1
        self.new_stage()
        k = 1
        win_d = self.din("conv_win", [8, 128, 8 * 384])
        wout_d = self.din("conv_wout", [D, D])
        S = self.alloc_norm_scratch(0)
        hT = self.BR.take(8, TT)
        hb = Buf()
        wi = [self.BR.take(8, 384) for _ in range(2)]
        wib = [Buf() for _ in range(2)]
        mT = [self.BR.take(8, TT) for _ in range(2)]
        mTb = [Buf() for _ in range(2)]
        wo = self.BR.take(8, D)
        wob = Buf()
        t1 = self.FR.take(TT)
        z = self.FR.take(TT)
        zc = self.FR.take(TT)
        t1b, zb, zcb = Buf(), Buf(), Buf()
        P.dma("pool", lambda h: h.dma_start(out=wo, in_=wout_d.rearrange("(c p) n -> p c n", p=128)),
              writes=[wob])
        G = lambda c: self.modG[:, l * 24 + k * 8 + c: l * 24 + k * 8 + c + 1]
        cw = lambda kk, dc: self.consts[:, COFF["conv_w"] + kk * 8 + dc: COFF["conv_w"] + kk * 8 + dc + 1]
        mL = self.consts[:, COFF["maskL"]:COFF["maskL"] + 1]
        mR = self.consts[:, COFF["maskR"]:COFF["maskR"] + 1]
        OW = 510
        otiles = [(H + o, min(OW, NT - o)) for o in range(0, NT, OW)]

        def finish(i):
            o0, ow = otiles[i]
            m = mT[i % 2]
            for oc in range(8):
                pi = 4 + self.ring("cy", 4)
                for dc in range(8):
                    P.op("pe", lambda h, oc=oc, dc=dc, pi=pi, m=m, ow=ow: h.matmul(
                        self.psum[pi][:, :ow], wo[:, dc, oc * 128:(oc + 1) * 128], m[:, dc, :ow],
                        start=(dc == 0), stop=(dc == 7)),
                        reads=[wob, mTb[i % 2]], writes=[self.pb[pi]])
                P.op("dve", lambda h, oc=oc, pi=pi, o0=o0, ow=ow: h.scalar_tensor_tensor(
                    out=self.xT[:, oc, o0:o0 + ow], in0=self.psum[pi][:, :ow], scalar=G(oc),
                    in1=self.xT[:, oc, o0:o0 + ow], op0=ALU.mult, op1=ALU.add),
                    reads=[self.pb[pi], self.mbuf], writes=self.xbuf_of(o0, ow))

        for i, (o0, ow) in enumerate(otiles):
            e0, ew = o0 - 1, ow + 2
            self.rstd_cols(e0, ew, S)
            self.modulate_cols(l, k, e0, ew, S, lambda c, ew=ew: hT[:, c, :ew], hb)
            if i >= 1:
                finish(i - 1)
            m = mT[i % 2]
            for dc in range(8):
                ws = self.ring("wi", 2)
                P.dma("pool", lambda h, ws=ws, dc=dc: h.dma_start(
                    out=wi[ws].rearrange("p c n -> p (c n)"), in_=win_d[dc]), writes=[wib[ws]])
                for part in range(3):
                    for c in range(8):
                        P.op("pe", lambda h, ws=ws, part=part, c=c, ew=ew: h.matmul(
                            self.psum[1 + part][:, :ew], wi[ws][:, c, part * 128:(part + 1) * 128], hT[:, c, :ew],
                            start=(c == 0), stop=(c == 7)),
                            reads=[wib[ws], hb], writes=[self.pb[1 + part]])
                P.op("act", lambda h, ew=ew: h.activation(out=t1[:, :ew], in_=self.psum[2][:, :ew], func=AF.Copy),
                     reads=[self.pb[2]], writes=[t1b])
                P.op("dve", lambda h, ew=ew: h.tensor_tensor(out=z[:, :ew], in0=t1[:, :ew], in1=self.psum[3][:, :ew],
                                                             op=ALU.mult),
                     reads=[t1b, self.pb[3]], writes=[zb])
                if i == 0:
                    P.op("dve", lambda h: h.tensor_scalar(out=z[:, 0:1], in0=z[:, 0:1], scalar1=mL, scalar2=None,
                                                          op0=ALU.mult), reads=[zb, self.cbuf], writes=[zb])
                if i == len(otiles) - 1:
                    P.op("dve", lambda h, ew=ew: h.tensor_scalar(out=z[:, ew - 1:ew], in0=z[:, ew - 1:ew], scalar1=mR,
                                                                 scalar2=None, op0=ALU.mult),
                         reads=[zb, self.cbuf], writes=[zb])
                P.op("dve", lambda h, dc=dc, ow=ow: h.tensor_scalar(out=zc[:, :ow], in0=z[:, 0:ow], scalar1=cw(0, dc),
                                                                    scalar2=None, op0=ALU.mult),
                     reads=[zb, self.cbuf], writes=[zcb])
                for kk in (1, 2):
                    P.op("dve", lambda h, dc=dc, ow=ow, kk=kk: h.scalar_tensor_tensor(
                        out=zc[:, :ow], in0=z[:, kk:kk + ow], scalar=cw(kk, dc), in1=zc[:, :ow],
                        op0=ALU.mult, op1=ALU.add), reads=[zb, zcb, self.cbuf], writes=[zcb])
                P.op("dve", lambda h, dc=dc, ow=ow, m=m: h.tensor_tensor(
                    out=m[:, dc, :ow], in0=zc[:, :ow], in1=self.psum[1][:, 1:1 + ow], op=ALU.mult),
                    reads=[zcb, self.pb[1]], writes=[mTb[i % 2]])
        finish(len(otiles) - 1)

    def stage_qkv(self, kind, l):
        P, H, NT = self.P, self.H, self.NT
        self.new_stage()
        k = 1
        gqa = (kind == "gqa")
        KF = 256 if gqa else 1024
        VF = 256 if gqa else 1024
        nblk = 6 if gqa else 10
        nqb, nkb = 4, (1 if gqa else 4)
        sfx = ("_" + kind) if self.fused else ""
        w_d = self.din("wqkv" + sfx, [nblk, 128, 8 * 512])
        rc_d = self.din("ropeC" + sfx, [128, NT])
        rs_d = self.din("ropeS" + sfx, [128, NT])
        KBL = min(2048, NT)
        NBL = NT // KBL
        CPB = KBL // 128
        VH = VF // 128
        if self.fused:
            q_o = self.nc.dram_tensor("q_" + kind, [1024, NT], BF16).ap()
            nkh = KF // 128
            k_ts = [self.nc.dram_tensor(f"kb_{kind}{j}", [128, NT], BF16) for j in range(nkh)]
            v_ts = [self.nc.dram_tensor(f"vb_{kind}{j}", [NBL * 128, CPB * 128], BF16) for j in range(VH)]
            v_o4 = [t.ap().rearrange("(b p) (c e) -> b p c e", b=NBL, p=128, c=CPB, e=128) for t in v_ts]
            k_o = None
            self.scr[kind] = dict(q=q_o, k_ts=k_ts, v_ts=v_ts, NBL=NBL, CPB=CPB)
        else:
            q_o = self.dout("q_out", [1024, NT], BF16)
            k_o = self.dout("k_out", [KF, NT], BF16)
            v_o = self.dout("v_out", [NT, VF], BF16)
        S = self.alloc_norm_scratch(0)
        hT = self.BR.take(8, TT)
        hb = Buf()
        wq = [self.BR.take(8, 512) for _ in range(2)]
        wqb = [Buf() for _ in range(2)]
        stg = [self.BR.take(TT) for _ in range(4)]
        stgb = [Buf() for _ in range(4)]
        sqq = self.BR.take(TT)
        sqqb = Buf()
        Ct = self.FR.take(TT)
        St = self.FR.take(TT)
        ctb = Buf()
        t1 = self.FR.take(TT)
        t2 = self.FR.take(TT)
        rq = self.FR.take(TT)
        t1b, t2b, rqb = Buf(), Buf(), Buf()
        gcol = {"q": (COFF["gqa_qg"], COFF["gqa_qg_sw"]), "k": (COFF["gqa_kg"], COFF["gqa_kg_sw"])}
        for ti, (c0, w) in enumerate(self.tiles):
            tok0 = c0 - H
            self.rstd_cols(c0, w, S)
            self.modulate_cols(l, k, c0, w, S, lambda c, w=w: hT[:, c, :w], hb)
            P.dma("sp", lambda h, tok0=tok0, w=w: h.dma_start(out=Ct[:, :w], in_=rc_d[:, tok0:tok0 + w]), writes=[ctb])
            P.dma("sp", lambda h, tok0=tok0, w=w: h.dma_start(out=St[:, :w], in_=rs_d[:, tok0:tok0 + w]), writes=[ctb])
            for b in range(nblk):
                ws = self.ring("wq", 2)
                P.dma("pool", lambda h, ws=ws, b=b: h.dma_start(
                    out=wq[ws].rearrange("p c n -> p (c n)"), in_=w_d[b]), writes=[wqb[ws]])
                if b < nqb + nkb:
                    which = "q" if b < nqb else "k"
                    dest = q_o if which == "q" else k_o
                    for ff in range(2):
                        fc = (b if which == "q" else b - nqb) * 2 + ff
                        pr = self.ring("qp", 2)
                        qp, qpb = self.psum[1 + pr], self.pb[1 + pr]
                        qs, qsb = self.psum[3 + pr], self.pb[3 + pr]
                        for c in range(8):
                            P.op("pe", lambda h, ws=ws, ff=ff, c=c, qp=qp, w=w: h.matmul(
                                qp[:, :w], wq[ws][:, c, ff * 128:(ff + 1) * 128], hT[:, c, :w],
                                start=(c == 0), stop=(c == 7)), reads=[wqb[ws], hb], writes=[qpb])
                        for c in range(8):
                            P.op("pe", lambda h, ws=ws, ff=ff, c=c, qs=qs, w=w: h.matmul(
                                qs[:, :w], wq[ws][:, c, 256 + ff * 128:256 + (ff + 1) * 128], hT[:, c, :w],
                                start=(c == 0), stop=(c == 7)), reads=[wqb[ws], hb], writes=[qsb])
                        si = self.ring("stg", 4)
                        if not gqa:
                            P.op("dve", lambda h, qp=qp, w=w: h.tensor_tensor(out=t1[:, :w], in0=qp[:, :w], in1=Ct[:, :w],
                                                                               op=ALU.mult),
                                 reads=[qpb, ctb], writes=[t1b])
                            P.op("dve", lambda h, qs=qs, w=w: h.tensor_tensor(out=t2[:, :w], in0=qs[:, :w], in1=St[:, :w],
                                                                               op=ALU.mult),
                                 reads=[qsb, ctb], writes=[t2b])
                            P.op("dve", lambda h, si=si, w=w: h.tensor_tensor(out=stg[si][:, :w], in0=t1[:, :w],
                                                                               in1=t2[:, :w], op=ALU.add),
                                 reads=[t1b, t2b], writes=[stgb[si]])
                        else:
                            g0, g1 = gcol[which]
                            P.op("act", lambda h, qp=qp, w=w: h.activation(out=sqq[:, :w], in_=qp[:, :w], func=AF.Square),
                                 reads=[qpb], writes=[sqqb])
                            P.op("pe", lambda h, w=w: h.matmul(self.psum[5][:, :w], self.ones[:], sqq[:, :w],
                                                               start=True, stop=True),
                                 reads=[sqqb, self.onesb], writes=[self.pb[5]])
                            P.op("act", lambda h, w=w: h.activation(out=rq[:, :w], in_=self.psum[5][:, :w], func=AF.Sqrt,
                                                                    bias=self.epsb[:], scale=1.0 / 128),
                                 reads=[self.pb[5], self.onesb], writes=[rqb])
                            P.op("dve", lambda h, w=w: h.reciprocal(out=rq[:, :w], in_=rq[:, :w]),
                                 reads=[rqb], writes=[rqb])
                            P.op("dve", lambda h, qp=qp, w=w, g0=g0: h.scalar_tensor_tensor(
                                out=t1[:, :w], in0=qp[:, :w], scalar=self.consts[:, g0:g0 + 1], in1=Ct[:, :w],
                                op0=ALU.mult, op1=ALU.mult), reads=[qpb, ctb, self.cbuf], writes=[t1b])
                            P.op("dve", lambda h, qs=qs, w=w, g1=g1: h.scalar_tensor_tensor(
                                out=t2[:, :w], in0=qs[:, :w], scalar=self.consts[:, g1:g1 + 1], in1=St[:, :w],
                                op0=ALU.mult, op1=ALU.mult), reads=[qsb, ctb, self.cbuf], writes=[t2b])
                            P.op("dve", lambda h, w=w: h.tensor_tensor(out=t1[:, :w], in0=t1[:, :w], in1=t2[:, :w],
                                                                       op=ALU.add),
                                 reads=[t1b, t2b], writes=[t1b])
                            P.op("dve", lambda h, si=si, w=w: h.tensor_tensor(out=stg[si][:, :w], in0=t1[:, :w],
                                                                               in1=rq[:, :w], op=ALU.mult),
                                 reads=[t1b, rqb], writes=[stgb[si]])
                        if self.fused and which == "k":
                            dap = k_ts[fc].ap()[:, tok0:tok0 + w]
                        else:
                            dap = dest[fc * 128:(fc + 1) * 128, tok0:tok0 + w]
                        self.outs.append(P.dma("sp", lambda h, si=si, dap=dap, w=w: h.dma_start(
                            out=dap, in_=stg[si][:, :w]), reads=[stgb[si]]))
                else:
                    vb = b - nqb - nkb
                    vw = 256 if gqa else 512
                    for tc in range(w // 128):
                        pr = 6 + self.ring("vp", 2)
                        for c in range(8):
                            P.op("pe", lambda h, ws=ws, c=c, pr=pr, tc=tc, vw=vw: h.matmul(
                                self.psum[pr][:, :vw], hT[:, c, tc * 128:(tc + 1) * 128], wq[ws][:, c, 0:vw],
                                start=(c == 0), stop=(c == 7)), reads=[wqb[ws], hb], writes=[self.pb[pr]])
                        si = self.ring("stg", 4)
                        P.op("act", lambda h, si=si, pr=pr, vw=vw: h.activation(out=stg[si][:, :vw], in_=self.psum[pr][:, :vw],
                                                                                func=AF.Copy),
                             reads=[self.pb[pr]], writes=[stgb[si]])
                        if self.fused:
                            tk = tok0 + tc * 128
                            bl, cl = tk // KBL, (tk % KBL) // 128
                            nh = vw // 128
                            h0 = vb * 4
                            for hh in range(nh):
                                self.outs.append(P.dma("sp", lambda h, si=si, bl=bl, cl=cl, hh=hh, h0=h0: h.dma_start(
                                    out=v_o4[h0 + hh][bl, :, cl, :], in_=stg[si][:, hh * 128:(hh + 1) * 128]),
                                    reads=[stgb[si]]))
                        else:
                            self.outs.append(P.dma("sp", lambda h, si=si, tok0=tok0, tc=tc, vb=vb, vw=vw: h.dma_start(
                                out=v_o[tok0 + tc * 128:tok0 + (tc + 1) * 128, vb * 512:vb * 512 + vw], in_=stg[si][:, :vw]),
                                reads=[stgb[si]]))

    def stage_gather(self, kind):
        P = self.P
        self.new_stage()
        sc = self.scr[kind]
        sc["kgb"], sc["vgb"] = Buf(), Buf()
        NT = self.NT
        if self.cpbt == 1:
            sc["kg"] = [t.ap() for t in sc["k_ts"]]
            sc["vg"] = [t.ap() for t in sc["v_ts"]]
            return
        R4 = self.cpbt
        vr, vc = sc["NBL"] * 128, sc["CPB"] * 128
        sc["kg"], sc["vg"] = [], []
        for j, t in enumerate(sc["k_ts"]):
            g = self.nc.dram_tensor(f"kg_{kind}{j}", [R4 * 128, NT], BF16)
            P.cc(lambda h, t=t, g=g: h.collective_compute("AllGather", ALU.bypass, replica_groups=self.groups,
                                                          ins=[t.ap().opt()], outs=[g.ap().opt()]),
                 writes=[sc["kgb"]])
            sc["kg"].append(g.ap())
        for j, t in enumerate(sc["v_ts"]):
            g = self.nc.dram_tensor(f"vg_{kind}{j}", [R4 * vr, vc], BF16)
            P.cc(lambda h, t=t, g=g: h.collective_compute("AllGather", ALU.bypass, replica_groups=self.groups,
                                                          ins=[t.ap().opt()], outs=[g.ap().opt()]),
                 writes=[sc["vgb"]])
            sc["vg"].append(g.ap())

    def stage_halo(self):
        P, H, NT, NTE = self.P, self.H, self.NT, self.NTE
        self.new_stage()
        i = self.nhalo
        self.nhalo += 1
        allx = self.xbuf + [self.xhalo]
        if self.cpbt == 1:
            P.op("dve", lambda h: h.memset(self.xT[:, :, 0:H], 0.0), writes=[self.xhalo])
            P.op("dve", lambda h: h.memset(self.xT[:, :, H + NT:NTE], 0.0), writes=[self.xhalo])
            return
        R4 = self.cpbt
        hb_t = self.nc.dram_tensor(f"hb{i}", [D, 2 * H], F32)
        hg_t = self.nc.dram_tensor(f"hg{i}", [R4 * D, 2 * H], F32)
        hbv = hb_t.ap().rearrange("(c p) n -> p c n", p=128)
        hbuf = Buf()
        P.dma("sp", lambda h: h.dma_start(out=hbv[:, :, 0:H], in_=self.xT[:, :, H:2 * H]), reads=allx, writes=[])
        P.dma("sp", lambda h: h.dma_start(out=hbv[:, :, H:2 * H], in_=self.xT[:, :, NT:NT + H]), reads=allx, writes=[])
        self.barrier()
        P.cc(lambda h: h.collective_compute("AllGather", ALU.bypass, replica_groups=self.groups,
                                            ins=[hb_t.ap().opt()], outs=[hg_t.ap().opt()]), writes=[hbuf])
        hs = self.FR.take(R4 * 8, 2 * H)
        hsb = Buf()
        P.dma("sp", lambda h: h.dma_start(out=hs, in_=hg_t.ap().rearrange("(rc p) n -> p rc n", p=128)),
              reads=[hbuf], writes=[hsb])
        sl, sr = COFF["selL"], COFF["selR"]
        for r in range(R4):
            src_l = hs[:, r * 8:(r + 1) * 8, H:2 * H]
            src_r = hs[:, r * 8:(r + 1) * 8, 0:H]
            dl = self.xT[:, :, 0:H]
            drr = self.xT[:, :, H + NT:NTE]
            if r == 0:
                P.op("dve", lambda h, src_l=src_l, dl=dl: h.tensor_scalar(
                    out=dl, in0=src_l, scalar1=self.consts[:, sl:sl + 1], scalar2=None, op0=ALU.mult),
                    reads=[hsb, self.cbuf], writes=[self.xhalo])
                P.op("dve", lambda h, src_r=src_r, drr=drr: h.tensor_scalar(
                    out=drr, in0=src_r, scalar1=self.consts[:, sr:sr + 1], scalar2=None, op0=ALU.mult),
                    reads=[hsb, self.cbuf], writes=[self.xhalo])
            else:
                P.op("dve", lambda h, src_l=src_l, dl=dl, r=r: h.scalar_tensor_tensor(
                    out=dl, in0=src_l, scalar=self.consts[:, sl + r:sl + r + 1], in1=dl, op0=ALU.mult, op1=ALU.add),
                    reads=[hsb, self.cbuf], writes=[self.xhalo])
                P.op("dve", lambda h, src_r=src_r, drr=drr, r=r: h.scalar_tensor_tensor(
                    out=drr, in0=src_r, scalar=self.consts[:, sr + r:sr + r + 1], in1=drr, op0=ALU.mult, op1=ALU.add),
                    reads=[hsb, self.cbuf], writes=[self.xhalo])

    def stage_attn(self, kind, l):
        P, H, NT, NK = self.P, self.H, self.NT, self.NK
        self.new_stage()
        gqa = (kind == "gqa")
        KB = min(2048, NT if self.fused else NK)
        NBLK = NK // KB
        CPB = KB // 128
        KF = 256 if gqa else 1024
        VH = 2 if gqa else 8
        if self.fused:
            sc = self.scr[kind]
            q_i = sc["q"]
            kg, vg = sc["kg"], sc["vg"]
            kgb, vgb = sc["kgb"], sc["vgb"]
            NBL = NT // KB
            wo_d = self.din("wo_" + kind, [D, D])

            def ksrc(kv, blk):
                r, hf = blk // NBL, blk % NBL
                return kg[kv][r * 128:(r + 1) * 128, hf * KB:(hf + 1) * KB]

            def vsrc(kv, blk):
                return vg[kv][blk * 128:(blk + 1) * 128, :]
        else:
            q_i = self.din("q_in", [1024, NT], BF16)
            k_i = self.din("k_in", [KF, NK], BF16)
            v_i = self.din("v_in", [VH, NBLK, 128, CPB * 128], BF16)
            wo_d = self.din("wo", [D, D])
            kgb, vgb = Buf(), Buf()

            def ksrc(kv, blk):
                return k_i[kv * 128:(kv + 1) * 128, blk * KB:(blk + 1) * KB]

            def vsrc(kv, blk):
                return v_i[kv, blk]
        qt = [self.BR.take(TT) for _ in range(2)]
        qtb = [Buf() for _ in range(2)]
        kt = [self.BR.take(KB) for _ in range(2)]
        ktb = [Buf() for _ in range(2)]
        vt = [self.BR.take(CPB, 128) for _ in range(2)]
        vtb = [Buf() for _ in range(2)]
        pt = [self.BR.take(TT) for _ in range(4)]
        ptb = [Buf() for _ in range(4)]
        oT = self.BR.take(8, TT)
        oTb = Buf()
        wo = self.BR.take(8, D)
        wob = Buf()
        sq = self.BR.take(TT)
        sqb = Buf()
        r1 = self.FR.take(TT)
        r2 = self.FR.take(TT)
        t1 = self.FR.take(TT)
        t2 = self.FR.take(TT)
        r1b, r2b, t1b, t2b = Buf(), Buf(), Buf(), Buf()
        P.dma("pool", lambda h: h.dma_start(out=wo, in_=wo_d.rearrange("(c p) n -> p c n", p=128)), writes=[wob])
        G = lambda c: self.modG[:, l * 24 + 8 + c: l * 24 + 8 + c + 1]
        scale = (128.0 if gqa else 64.0) ** -0.5
        if not gqa:
            lam_init = 0.8 - 0.6 * float(np.exp(-0.3 * l))
            lam_d = self.din("lam", [128, 256])
            self.lam_done = True
            lamt = self.FR.take(256)
            lt = self.sb("lamtmp", [128, 8])
            lb = Buf()
            P.dma("sp", lambda h: h.dma_start(out=lamt, in_=lam_d[:, :]), writes=[lb])
            P.op("dve", lambda h: h.tensor_tensor(out=lamt[:, 0:64], in0=lamt[:, 0:64], in1=lamt[:, 64:128], op=ALU.mult),
                 reads=[lb], writes=[lb])
            P.op("dve", lambda h: h.tensor_tensor(out=lamt[:, 128:192], in0=lamt[:, 128:192], in1=lamt[:, 192:256],
                                                  op=ALU.mult), reads=[lb], writes=[lb])
            P.op("dve", lambda h: h.reduce_sum(out=lt[:, 0:1], in_=lamt[:, 0:64], axis=mybir.AxisListType.X),
                 reads=[lb], writes=[lb])
            P.op("dve", lambda h: h.reduce_sum(out=lt[:, 1:2], in_=lamt[:, 128:192], axis=mybir.AxisListType.X),
                 reads=[lb], writes=[lb])
            P.op("act", lambda h: h.activation(out=lt[:, 2:4], in_=lt[:, 0:2], func=AF.Exp), reads=[lb], writes=[lb])
            P.op("dve", lambda h: h.tensor_tensor(out=lt[:, 4:5], in0=lt[:, 3:4], in1=lt[:, 2:3], op=ALU.subtract),
                 reads=[lb], writes=[lb])
            P.op("dve", lambda h: h.tensor_scalar(out=lt[:, 4:5], in0=lt[:, 4:5], scalar1=-lam_init, scalar2=None,
                                                  op0=ALU.add), reads=[lb], writes=[lb])
            so = COFF["subln_g"]
            P.op("dve", lambda h: h.tensor_scalar(out=lt[:, 5:6], in0=self.consts[:, so:so + 1],
                                                  scalar1=(1.0 - lam_init), scalar2=None, op0=ALU.mult),
                 reads=[lb, self.cbuf], writes=[lb])
            neglam = lt[:, 4:5]
            gsub = lt[:, 5:6]
        ncomp = 1 if gqa else 2
        for ti, (c0, w) in enumerate(self.tiles):
            tok0 = c0 - H
            for u in range(8):
                qi = self.ring("qt", 2)
                P.dma("sp", lambda h, qi=qi, u=u, tok0=tok0, w=w: h.dma_start(
                    out=qt[qi][:, :w], in_=q_i[u * 128:(u + 1) * 128, tok0:tok0 + w]), writes=[qtb[qi]])
                kv = (u // 4) if gqa else u
                if gqa:
                    ob = 2 + 2 * self.ring("ob", 2)
                    Ob = [ob]
                    Zb = [ob + 1]
                else:
                    Ob = [2, 4]
                    Zb = [3, 5]
                steps = [(blk, kc, comp) for blk in range(NBLK) for kc in range(CPB) for comp in range(ncomp)]
                nst = len(steps)
                kis = {}
                info = {}

                def emit_S(idx):
                    blk, kc, comp = steps[idx]
                    if blk not in kis:
                        ki = self.ring("kt", 2)
                        kis[blk] = ki
                        P.dma("sp", lambda h, ki=ki, kv=kv, blk=blk: h.dma_start(
                            out=kt[ki], in_=ksrc(kv, blk)), reads=[kgb], writes=[ktb[ki]])
                        P.dma("sp", lambda h, ki=ki, kv=kv, blk=blk: h.dma_start(
                            out=vt[ki].rearrange("p c e -> p (c e)"), in_=vsrc(kv, blk)), reads=[vgb], writes=[vtb[ki]])
                    ki = kis[blk]
                    r0, r1_ = (0, 128) if gqa else (comp * 64, comp * 64 + 64)
                    si = self.ring("S", 2)
                    P.op("pe", lambda h, ki=ki, qi=qi, kc=kc, r0=r0, r1_=r1_, si=si, w=w: h.matmul(
                        self.psum[si][:, :w], kt[ki][r0:r1_, kc * 128:(kc + 1) * 128], qt[qi][r0:r1_, :w],
                        start=True, stop=True), reads=[ktb[ki], qtb[qi]], writes=[self.pb[si]])
                    pi = self.ring("pt", 4)
                    P.op("act", lambda h, si=si, pi=pi, w=w: h.activation(
                        out=pt[pi][:, :w], in_=self.psum[si][:, :w], func=AF.Exp, scale=scale),
                        reads=[self.pb[si]], writes=[ptb[pi]])
                    info[idx] = (ki, pi)

                def emit_PV(idx):
                    blk, kc, comp = steps[idx]
                    ki, pi = info[idx]
                    first = (blk == 0 and kc == 0)
                    last = (blk == NBLK - 1 and kc == CPB - 1)
                    P.op("pe", lambda h, ki=ki, kc=kc, pi=pi, comp=comp, w=w, first=first, last=last, Ob=Ob: h.matmul(
                        self.psum[Ob[comp]][:, :w], vt[ki][:, kc, :], pt[pi][:, :w], start=first, stop=last),
                        reads=[vtb[ki], ptb[pi]], writes=[self.pb[Ob[comp]]])
                    P.op("pe", lambda h, pi=pi, comp=comp, w=w, first=first, last=last, Zb=Zb: h.matmul(
                        self.psum[Zb[comp]][:, :w], self.ones[:], pt[pi][:, :w], start=first, stop=last),
                        reads=[self.onesb, ptb[pi]], writes=[self.pb[Zb[comp]]])

                LA = 2
                for idx in range(min(LA, nst)):
                    emit_S(idx)
                for idx in range(nst):
                    if idx + LA < nst:
                        emit_S(idx + LA)
                    emit_PV(idx)
                P.op("dve", lambda h, w=w, Zb=Zb: h.reciprocal(out=r1[:, :w], in_=self.psum[Zb[0]][:, :w]),
                     reads=[self.pb[Zb[0]]], writes=[r1b])
                if gqa:
                    P.op("dve", lambda h, w=w, u=u, Ob=Ob: h.tensor_tensor(out=oT[:, u, :w], in0=self.psum[Ob[0]][:, :w],
                                                                           in1=r1[:, :w], op=ALU.mult),
                         reads=[self.pb[Ob[0]], r1b], writes=[oTb])
                else:
                    P.op("dve", lambda h, w=w: h.reciprocal(out=r2[:, :w], in_=self.psum[5][:, :w]),
                         reads=[self.pb[5]], writes=[r2b])
                    P.op("dve", lambda h, w=w: h.tensor_tensor(out=t1[:, :w], in0=self.psum[2][:, :w], in1=r1[:, :w],
                                                               op=ALU.mult), reads=[self.pb[2], r1b], writes=[t1b])
                    P.op("dve", lambda h, w=w: h.tensor_tensor(out=t2[:, :w], in0=self.psum[4][:, :w], in1=r2[:, :w],
                                                               op=ALU.mult), reads=[self.pb[4], r2b], writes=[t2b])
                    P.op("dve", lambda h, w=w: h.scalar_tensor_tensor(out=t1[:, :w], in0=t2[:, :w], scalar=neglam,
                                                                      in1=t1[:, :w], op0=ALU.mult, op1=ALU.add),
                         reads=[t1b, t2b, lb], writes=[t1b])
                    P.op("act", lambda h, w=w: h.activation(out=sq[:, :w], in_=t1[:, :w], func=AF.Square),
                         reads=[t1b], writes=[sqb])
                    P.op("pe", lambda h, w=w: h.matmul(self.psum[6][:, :w], self.ones[:], sq[:, :w], start=True, stop=True),
                         reads=[sqb, self.onesb], writes=[self.pb[6]])
                    P.op("act", lambda h, w=w: h.activation(out=r2[:, :w], in_=self.psum[6][:, :w], func=AF.Sqrt,
                                                            bias=self.epsb[:], scale=1.0 / 128),
                         reads=[self.pb[6], self.onesb], writes=[r2b])
                    P.op("dve", lambda h, w=w: h.reciprocal(out=r2[:, :w], in_=r2[:, :w]), reads=[r2b], writes=[r2b])
                    P.op("dve", lambda h, w=w, u=u: h.scalar_tensor_tensor(out=oT[:, u, :w], in0=t1[:, :w], scalar=gsub,
                                                                           in1=r2[:, :w], op0=ALU.mult, op1=ALU.mult),
                         reads=[t1b, r2b, lb], writes=[oTb])
            for dc in range(8):
                pi = 6 + self.ring("oy", 2)
                for u in range(8):
                    P.op("pe", lambda h, dc=dc, u=u, pi=pi, w=w: h.matmul(
                        self.psum[pi][:, :w], wo[:, u, dc * 128:(dc + 1) * 128], oT[:, u, :w],
                        start=(u == 0), stop=(u == 7)), reads=[wob, oTb], writes=[self.pb[pi]])
                P.op("dve", lambda h, dc=dc, pi=pi, c0=c0, w=w: h.scalar_tensor_tensor(
                    out=self.xT[:, dc, c0:c0 + w], in0=self.psum[pi][:, :w], scalar=G(dc),
                    in1=self.xT[:, dc, c0:c0 + w], op0=ALU.mult, op1=ALU.add),
                    reads=[self.pb[pi], self.mbuf], writes=[self.xbuf[ti]])


def blockify(w):
    n = w.shape[1]
    return np.ascontiguousarray(w.reshape(8, 128, n).transpose(1, 0, 2)).reshape(128, 8 * n)


def rope_np(pos, dim, theta):
    inv = (1.0 / (np.float32(theta) ** (np.arange(0, dim, 2, dtype=np.float32) / np.float32(dim)))).astype(np.float32)
    ang = pos.astype(np.float32)[:, None] * inv[None, :]
    return np.cos(ang).astype(np.float32), np.sin(ang).astype(np.float32)


def sigma_diff():
    s = np.arange(64)
    s[0:8] = np.arange(8, 16)
    s[8:16] = np.arange(0, 8)
    return s


def sigma_gqa():
    s = np.arange(128)
    s[0:32] = np.arange(32, 64)
    s[32:64] = np.arange(0, 32)
    s[64:96] = np.arange(96, 128)
    s[96:128] = np.arange(64, 96)
    return s


def rope_tabs(kind, pos):
    n = len(pos)
    if kind == "diff":
        cos, sin = rope_np(pos, 16, 500000.0)
        C = np.ones((64, n), np.float32)
        S = np.zeros((64, n), np.float32)
        C[0:8] = cos.T
        C[8:16] = cos.T
        S[0:8] = -sin.T
        S[8:16] = sin.T
        return np.ascontiguousarray(np.tile(C, (2, 1))), np.ascontiguousarray(np.tile(S, (2, 1)))
    cr, sr = rope_np(pos // 64, 64, 10000.0)
    cc, sc = rope_np(pos % 64, 64, 10000.0)
    C = np.concatenate([cr.T, cr.T, cc.T, cc.T], 0)
    S = np.concatenate([-sr.T, sr.T, -sc.T, sc.T], 0)
    return np.ascontiguousarray(C), np.ascontiguousarray(S)


def qkv_blocks(kind, inp):
    if kind == "diff":
        w = inp["diff_w_qkv"]
        q, k, v = w[:, :1024], w[:, 1024:2048], w[:, 2048:3072]
        sg = sigma_diff()
        hd = 64
    else:
        w = inp["gqa_w_qkv"]
        q, k, v = w[:, :1024], w[:, 1024:1280], w[:, 1280:1536]
        sg = sigma_gqa()
        hd = 128

    def sw(m):
        n = m.shape[1]
        idx = (np.arange(n) // hd) * hd + sg[np.arange(n) % hd]
        return m[:, idx]
    blocks = []
    for m in (q, k):
        ms = sw(m)
        for b in range(m.shape[1] // 256):
            blocks.append(blockify(np.concatenate([m[:, b * 256:(b + 1) * 256], ms[:, b * 256:(b + 1) * 256]], 1)))
    if kind == "diff":
        for b in range(2):
            blocks.append(blockify(np.ascontiguousarray(v[:, b * 512:(b + 1) * 512])))
    else:
        blocks.append(blockify(np.concatenate([v, np.zeros((1024, 256), np.float32)], 1)))
    return np.stack(blocks)


LAUNCHES = [
    dict(name="A", halo=0, stages=[("mod",), ("ffn", 0, 0)]),
    dict(name="B", halo=8, stages=[("pool", 0), ("ffn", 0, 2), ("ffn", 1, 0), ("qkv", "diff", 1)]),
    dict(name="C", halo=0, stages=[("attn", "diff", 1), ("ffn", 1, 2), ("ffn", 2, 0), ("qkv", "gqa", 2)]),
    dict(name="D", halo=0, stages=[("attn", "gqa", 2), ("ffn", 2, 2), ("ffn", 3, 0)]),
    dict(name="E", halo=1, stages=[("conv", 3), ("ffn", 3, 2), ("final",)]),
]


def run_pipeline(inp, B, S, CPBT, launches=LAUNCHES, same=True, trace=False, stop_after=None):
    inp = {k: np.asarray(v) for k, v in inp.items()}
    ncore = B * CPBT
    NT = S // CPBT
    xcur = [np.ascontiguousarray(inp["x"][c // CPBT, (c % CPBT) * NT:((c % CPBT) + 1) * NT, :].T)
            for c in range(ncore)]
    modtab = None
    qkv = None
    times = []
    for L in launches:
        H = L["halo"]
        stages = L["stages"]
        cfg = dict(ntok=NT, halo=H, stages=stages, nkeys=S, same=same, dbg=L.get("dbg", False), pool_ow=L.get("pool_ow", 496))
        bld = Builder(cfg)
        nc = bld.build()
        ffns = [s for s in stages if s[0] == "ffn"]
        shared = {}
        if ffns:
            shared["wgu"] = np.stack([relayout_wgu(inp["ffn_w_gu"][l, 0 if w == 0 else 1]) for _, l, w in ffns])
            shared["wdn"] = np.stack([inp["ffn_w_down"][l, 0 if w == 0 else 1] for _, l, w in ffns])
        for st in stages:
            if st[0] == "mod":
                shared["modw"] = np.ascontiguousarray(inp["mod_w"])
            if st[0] == "pool":
                shared["pool_w"] = np.ascontiguousarray(inp["pool_w"])
            if st[0] == "conv":
                wi = inp["conv_w_in"]
                shared["conv_win"] = np.stack([blockify(np.concatenate(
                    [wi[:, dc * 128:(dc + 1) * 128], wi[:, 1024 + dc * 128:1024 + (dc + 1) * 128],
                     wi[:, 2048 + dc * 128:2048 + (dc + 1) * 128]], 1)) for dc in range(8)])
                shared["conv_wout"] = np.ascontiguousarray(inp["conv_w_out"])
            if st[0] == "qkv":
                shared["wqkv"] = qkv_blocks(st[1], inp)
            if st[0] == "attn":
                shared["wo"] = np.ascontiguousarray(inp["diff_w_o"] if st[1] == "diff" else inp["gqa_w_o"])
                if st[1] == "diff":
                    shared["lam"] = np.ascontiguousarray(
                        np.broadcast_to(inp["diff_lambda"].reshape(1, 256), (128, 256)))
        maps = []
        for c in range(ncore):
            b, q = c // CPBT, c % CPBT
            m = dict(shared)
            cs = build_consts(inp, b)
            sg = sigma_gqa()
            cs[:, COFF["gqa_qg_sw"]] = inp["gqa_q_norm_g"][sg]
            cs[:, COFF["gqa_kg_sw"]] = inp["gqa_k_norm_g"][sg]
            cs[:, COFF["maskL"]] = 0.0 if q == 0 else 1.0
            cs[:, COFF["maskR"]] = 0.0 if q == CPBT - 1 else 1.0
            m["consts"] = cs
            xe = np.zeros((D, NT + 2 * H), np.float32)
            xe[:, H:H + NT] = xcur[c]
            if H:
                if q > 0:
                    xe[:, :H] = xcur[c - 1][:, NT - H:]
                if q < CPBT - 1:
                    xe[:, H + NT:] = xcur[c + 1][:, :H]
            m["xT"] = xe
            if modtab is not None:
                m["modtab_in"] = modtab[c]
            for st in stages:
                if st[0] == "pool":
                    t = q * NT + np.arange(-H, NT + H)
                    inv = np.ones((4, NT + 2 * H), np.float32)
                    for g, win in enumerate((2, 4, 8, 16)):
                        lo = np.clip(t - win // 2, 0, S)
                        hi = np.clip(t + win // 2, 0, S)
                        cnt = (hi - lo).astype(np.float32)
                        inv[g] = np.where(cnt > 0, np.float32(1.0) / np.maximum(cnt, 1), 1.0)
                    m["pool_inv"] = np.ascontiguousarray(np.broadcast_to(inv[None], (128, 4, NT + 2 * H)))
                if st[0] == "qkv":
                    pos = q * NT + np.arange(NT)
                    C, S_ = rope_tabs(st[1], pos)
                    m["ropeC"], m["ropeS"] = C, S_
                if st[0] == "attn":
                    m["q_in"] = qkv["q"][c]
                    m["k_in"] = qkv["kfull"][b]
                    m["v_in"] = qkv["vfull"][b]
            maps.append(m)
        res = run_bass_kernel_spmd(nc, maps, core_ids=list(range(ncore)), trace=trace)
        times.append(res.exec_time_ns)
        R = res.results
        xcur = [np.asarray(R[c]["xout"]) for c in range(ncore)]
        if any(s[0] == "mod" for s in stages):
            modtab = [np.asarray(R[c]["modtab_out"]) for c in range(ncore)]
        qs = [s for s in stages if s[0] == "qkv"]
        if qs:
            KB = min(2048, S)
            NBLK, CPB = S // KB, KB // 128
            kfull, vfull = [], []
            for b in range(B):
                kf = np.concatenate([np.asarray(R[b * CPBT + q]["k_out"]) for q in range(CPBT)], axis=1)
                vf = np.concatenate([np.asarray(R[b * CPBT + q]["v_out"]) for q in range(CPBT)], axis=0)
                VH = vf.shape[1] // 128
                vf = vf.reshape(NBLK, CPB, 128, VH, 128).transpose(3, 0, 2, 1, 4).reshape(VH, NBLK, 128, CPB * 128)
                kfull.append(np.ascontiguousarray(kf))
                vfull.append(np.ascontiguousarray(vf))
            qkv = dict(q=[np.asarray(R[c]["q_out"]) for c in range(ncore)], kfull=kfull, vfull=vfull)
        if stop_after == L["name"]:
            break
    out = np.zeros((B, S, D), np.float32)
    for c in range(ncore):
        b, q = c // CPBT, c % CPBT
        out[b, q * NT:(q + 1) * NT, :] = xcur[c].T
    return out, times


FUSED_STAGES = [("mod",), ("ffn", 0, 0), ("halo",), ("pool", 0), ("ffn", 0, 2), ("ffn", 1, 0), ("qkv", "diff", 1),
                ("gather", "diff"), ("attn", "diff", 1), ("ffn", 1, 2), ("ffn", 2, 0), ("qkv", "gqa", 2),
                ("gather", "gqa"), ("attn", "gqa", 2), ("ffn", 2, 2), ("ffn", 3, 0), ("halo",), ("conv", 3),
                ("ffn", 3, 2), ("final",)]


def run_fused(inp, B, S, CPBT, same=True, trace=False, stages=FUSED_STAGES):
    inp = {k: np.asarray(v) for k, v in inp.items()}
    ncore = B * CPBT
    NT = S // CPBT
    H = 8
    groups = [[b * CPBT + q for q in range(CPBT)] for b in range(B)]
    cfg = dict(ntok=NT, halo=H, stages=stages, nkeys=S, same=same, fused=True, cpbt=CPBT, groups=groups)
    bld = Builder(cfg)
    nc = bld.build()
    ffns = [s for s in stages if s[0] == "ffn"]
    shared = {}
    shared["wgu"] = np.stack([relayout_wgu(inp["ffn_w_gu"][l, 0 if w == 0 else 1]) for _, l, w in ffns])
    shared["wdn"] = np.stack([inp["ffn_w_down"][l, 0 if w == 0 else 1] for _, l, w in ffns])
    shared["modw"] = np.ascontiguousarray(inp["mod_w"])
    shared["pool_w"] = np.ascontiguousarray(inp["pool_w"])
    wi = inp["conv_w_in"]
    shared["conv_win"] = np.stack([blockify(np.concatenate(
        [wi[:, dc * 128:(dc + 1) * 128], wi[:, 1024 + dc * 128:1024 + (dc + 1) * 128],
         wi[:, 2048 + dc * 128:2048 + (dc + 1) * 128]], 1)) for dc in range(8)])
    shared["conv_wout"] = np.ascontiguousarray(inp["conv_w_out"])
    shared["wqkv_diff"] = qkv_blocks("diff", inp)
    shared["wqkv_gqa"] = qkv_blocks("gqa", inp)
    shared["wo_diff"] = np.ascontiguousarray(inp["diff_w_o"])
    shared["wo_gqa"] = np.ascontiguousarray(inp["gqa_w_o"])
    shared["lam"] = np.ascontiguousarray(np.broadcast_to(inp["diff_lambda"].reshape(1, 256), (128, 256)))
    maps = []
    sg = sigma_gqa()
    for c in range(ncore):
        b, q = c // CPBT, c % CPBT
        m = dict(shared)
        cs = build_consts(inp, b)
        cs[:, COFF["gqa_qg_sw"]] = inp["gqa_q_norm_g"][sg]
        cs[:, COFF["gqa_kg_sw"]] = inp["gqa_k_norm_g"][sg]
        cs[:, COFF["maskL"]] = 0.0 if q == 0 else 1.0
        cs[:, COFF["maskR"]] = 0.0 if q == CPBT - 1 else 1.0
        for r in range(min(4, CPBT)):
            cs[:, COFF["selL"] + r] = 1.0 if r == q - 1 else 0.0
            cs[:, COFF["selR"] + r] = 1.0 if r == q + 1 else 0.0
        m["consts"] = cs
        xe = np.zeros((D, NT + 2 * H), np.float32)
        xe[:, H:H + NT] = inp["x"][b, q * NT:(q + 1) * NT, :].T
        m["xT"] = xe
        t = q * NT + np.arange(-H, NT + H)
        inv = np.ones((4, NT + 2 * H), np.float32)
        for g, win in enumerate((2, 4, 8, 16)):
            lo = np.clip(t - win // 2, 0, S)
            hi = np.clip(t + win // 2, 0, S)
            cnt = (hi - lo).astype(np.float32)
            inv[g] = np.where(cnt > 0, np.float32(1.0) / np.maximum(cnt, 1), 1.0)
        m["pool_inv"] = np.ascontiguousarray(np.broadcast_to(inv[None], (128, 4, NT + 2 * H)))
        pos = q * NT + np.arange(NT)
        for kind in ("diff", "gqa"):
            C, S_ = rope_tabs(kind, pos)
            m["ropeC_" + kind], m["ropeS_" + kind] = C, S_
        maps.append(m)
    need = set()
    for st in stages:
        need.add(st[0] + ("_" + st[1] if st[0] in ("qkv", "attn") else ""))
    res = run_bass_kernel_spmd(nc, maps, core_ids=list(range(ncore)), trace=trace)
    out = np.zeros((B, S, D), np.float32)
    for c in range(ncore):
        b, q = c // CPBT, c % CPBT
        out[b, q * NT:(q + 1) * NT, :] = np.asarray(res.results[c]["xout"]).T
    return out, res.exec_time_ns


def kernel(**inputs):
    out, _ = run_fused(inputs, 2, SEQ, 4)
    return out
```

```python
import numpy as np
from contextlib import ExitStack
import concourse.bass as bass
import concourse.mybir as mybir
from concourse.bass_utils import run_bass_kernel_spmd

F32 = mybir.dt.float32
BF16 = mybir.dt.bfloat16
ALU = mybir.AluOpType
AF = mybir.ActivationFunctionType

D = 1024
DFF = 2816
NFC = DFF // 128
NL = 4
SEQ = 16384
NCORE = 8
TT = 512
EPS = 1e-6


class Buf:
    __slots__ = ("w", "r", "name")

    def __init__(self, name=""):
        self.w = None
        self.r = {}
        self.name = name


class Prog:
    CENG = ["pe", "dve", "act", "pool"]
    DMAQ = ["sp", "pool"]
    R = 6

    def __init__(self, nc, stack, same_engine_sync=True):
        self.nc = nc
        self.same = same_engine_sync
        self.streams = {e: [] for e in ["pe", "dve", "act", "pool", "sp"]}
        self.ncomp = {e: 0 for e in self.CENG}
        self.comp_ops = {e: [] for e in self.CENG}
        self.ndma = {q: 0 for q in self.DMAQ}
        self.sem = {e: stack.enter_context(nc.semaphore("s_" + e)) for e in self.CENG}
        self.dsem = {q: [stack.enter_context(nc.semaphore(f"d_{q}{i}")) for i in range(self.R)]
                     for q in self.DMAQ}
        self.all_dma = []
        self.base_dep = None
        self.csem = []
        self.stack = stack

    def _deps(self, reads, writes):
        deps = set()
        if self.base_dep is not None:
            deps.add(self.base_dep)
        for b in reads:
            if b.w is not None:
                deps.add(b.w)
        for b in writes:
            if b.w is not None:
                deps.add(b.w)
            for k, v in b.r.items():
                if isinstance(k, tuple):
                    deps.add(k)
                else:
                    deps.add(("c", k, v))
        return deps

    def _mark(self, tok, reads, writes):
        for b in reads:
            if tok[0] == "c":
                b.r[tok[1]] = tok[2]
            else:
                b.r[tok] = True
        for b in writes:
            b.w = tok
            b.r = {}

    def op(self, eng, fn, reads=(), writes=()):
        self.total_ops = getattr(self, "total_ops", 0) + 1
        if self.total_ops > getattr(self, "limit", 10 ** 9):
            return None
        idx = self.ncomp[eng]
        self.ncomp[eng] += 1
        deps = self._deps(reads, writes)
        tok = ("c", eng, idx)
        o = dict(kind="c", fn=fn, deps=deps, inc=False, idx=idx)
        self.comp_ops[eng].append(o)
        self.streams[eng].append(o)
        self._mark(tok, reads, writes)
        return tok

    def dma(self, q, fn, reads=(), writes=()):
        j = self.ndma[q]
        self.ndma[q] += 1
        deps = self._deps(reads, writes)
        tok = ("d", q, j)
        o = dict(kind="d", fn=fn, deps=deps, q=q, j=j)
        self.streams[q].append(o)
        self.all_dma.append(tok)
        self._mark(tok, reads, writes)
        return tok

    def cc(self, fn, reads=(), writes=()):
        k = len(self.csem)
        self.csem.append(self.stack.enter_context(self.nc.semaphore(f"cc{k}")))
        deps = self._deps(reads, writes)
        tok = ("s", k)
        o = dict(kind="s", fn=fn, deps=deps, k=k)
        self.streams["pool"].append(o)
        self.all_dma.append(tok)
        self._mark(tok, reads, writes)
        return tok

    def finalize(self):
        for e, ops in self.streams.items():
            for o in ops:
                for d in o["deps"]:
                    if d[0] == "c":
                        if d[1] == e and o["kind"] == "c" and (not self.same or e == "pe"):
                            continue
                        self.comp_ops[d[1]][d[2]]["inc"] = True
        self.val = {}
        for e in self.CENG:
            c = 0
            vals = []
            for o in self.comp_ops[e]:
                if o["inc"]:
                    c += 1
                vals.append(c)
            self.val[e] = vals

    def emit(self, eng, h):
        seen = {}
        R = self.R
        nwait = 0
        for o in self.streams[eng]:
            needs = {}
            for d in o["deps"]:
                if d[0] == "c":
                    if d[1] == eng and o["kind"] == "c" and (not self.same or eng == "pe"):
                        continue
                    s = self.sem[d[1]]
                    v = self.val[d[1]][d[2]]
                    key = ("c", d[1])
                elif d[0] == "s":
                    s = self.csem[d[1]]
                    v = 1
                    key = ("s", d[1])
                else:
                    s = self.dsem[d[1]][d[2] % R]
                    v = 16 * (d[2] // R + 1)
                    key = ("d", d[1], d[2] % R)
                if seen.get(key, 0) >= v:
                    continue
                if key not in needs or needs[key][1] < v:
                    needs[key] = (s, v)
            if o["kind"] == "d":
                j = o["j"]
                if j >= R:
                    key = ("d", o["q"], j % R)
                    v = 16 * (j // R)
                    if seen.get(key, 0) < v and (key not in needs or needs[key][1] < v):
                        needs[key] = (self.dsem[o["q"]][j % R], v)
            for key, (s, v) in needs.items():
                h.wait_ge(s, v)
                seen[key] = v
                nwait += 1
            ins = o["fn"](h)
            if o["kind"] == "c":
                if o["inc"]:
                    ins.then_inc(self.sem[eng], 1)
            elif o["kind"] == "s":
                ins.then_inc(self.csem[o["k"]], 1)
            else:
                ins.then_inc(self.dsem[o["q"]][o["j"] % R], 16)
        return nwait

    def final_waits(self, eng, h, toks, q):
        R = self.R
        for d in toks:
            if d[0] == "d" and d[1] == q:
                h.wait_ge(self.dsem[d[1]][d[2] % R], 16 * (d[2] // R + 1))


def const_layout():
    off = {}
    n = 0

    def add(name, w):
        nonlocal n
        off[name] = n
        n += w
    add("c", 8)
    for l in range(NL):
        add(f"modb{l}", 72)
        add(f"ng{l}", 24)
    add("final_g", 8)
    add("pool_scale", 8)
    add("conv_w", 24)
    add("gqa_qg", 1)
    add("gqa_kg", 1)
    add("subln_g", 1)
    add("gqa_qg_sw", 1)
    add("gqa_kg_sw", 1)
    add("maskL", 1)
    add("maskR", 1)
    add("selL", 4)
    add("selR", 4)
    return off, n


COFF, NCONST = const_layout()


def col128(v):
    v = np.asarray(v, np.float32).reshape(-1, 128)
    return np.ascontiguousarray(v.T)


def build_consts(inp, b):
    cs = np.zeros((128, NCONST), np.float32)
    cs[:, COFF["c"]:COFF["c"] + 8] = col128(inp["c"][b])
    for l in range(NL):
        cs[:, COFF[f"modb{l}"]:COFF[f"modb{l}"] + 72] = col128(inp["mod_b"][l])
        cs[:, COFF[f"ng{l}"]:COFF[f"ng{l}"] + 24] = col128(inp["norm_g"][l].reshape(-1))
    cs[:, COFF["final_g"]:COFF["final_g"] + 8] = col128(inp["final_g"])
    cs[:, COFF["pool_scale"]:COFF["pool_scale"] + 8] = col128(inp["pool_scale"])
    cs[:, COFF["conv_w"]:COFF["conv_w"] + 24] = col128(inp["conv_w"].reshape(-1))
    cs[:, COFF["gqa_qg"]] = inp["gqa_q_norm_g"]
    cs[:, COFF["gqa_kg"]] = inp["gqa_k_norm_g"]
    cs[:, COFF["subln_g"]] = inp["diff_subln_g"]
    return cs


def relayout_wgu(w):
    w = np.asarray(w, np.float32)
    g = w[:, :DFF].reshape(8, 128, 11, 256)
    u = w[:, DFF:].reshape(8, 128, 11, 256)
    r = np.concatenate([g, u], axis=3)
    r = r.transpose(2, 1, 0, 3)
    return np.ascontiguousarray(r).reshape(11, 128, 8 * 512)


class Region:
    def __init__(self, t, n):
        self.t, self.n, self.o = t, n, 0

    def reset(self):
        self.o = 0

    def take(self, *shape):
        sz = int(np.prod(shape))
        assert self.o + sz <= self.n, ("region overflow", self.o, sz, self.n)
        ap = self.t[:, self.o:self.o + sz]
        self.o += sz
        if len(shape) == 2:
            ap = ap.rearrange("p (a b) -> p a b", a=shape[0])
        return ap


FREG = 5120
BREG = 27648


class Builder:
    def __init__(self, cfg):
        self.cfg = cfg
        self.NT = cfg["ntok"]
        self.H = cfg.get("halo", 0)
        self.NTE = self.NT + 2 * self.H
        self.NK = cfg.get("nkeys", 0)
        self.nc = bass.Bass("TRN2", target_bir_lowering=False)
        self.stack = ExitStack()
        self.P = Prog(self.nc, self.stack, same_engine_sync=cfg.get("same", True))
        self.rr = {}
        self.P.limit = cfg.get("limit", 10 ** 9)
        self.outs = []
        self.fused = cfg.get("fused", False)
        self.cpbt = cfg.get("cpbt", 1)
        self.groups = cfg.get("groups", [[0]])
        self.scr = {}
        self.nhalo = 0
        self.tiles = [(self.H + t0, min(TT, self.NT - t0)) for t0 in range(0, self.NT, TT)]

    def din(self, name, shape, dt=F32):
        return self.nc.dram_tensor(name, list(shape), dt, kind="ExternalInput").ap()

    def dout(self, name, shape, dt=F32):
        return self.nc.dram_tensor(name, list(shape), dt, kind="ExternalOutput").ap()

    def sb(self, name, shape, dt=F32):
        return self.stack.enter_context(self.nc.sbuf_tensor(name, list(shape), dt))

    def dbg(self, name, ap, shape, dt, reads):
        if not self.cfg.get("dbg"):
            return
        o = self.dout("dbg_" + name, shape, dt)
        self.outs.append(self.P.dma("sp", lambda h: h.dma_start(out=o, in_=ap), reads=reads))

    def ring(self, key, n):
        i = self.rr.get(key, 0)
        self.rr[key] = i + 1
        return i % n

    def new_stage(self):
        self.barrier()
        self.FR.reset()
        self.BR.reset()
        self.pb = [Buf(f"psum{i}") for i in range(8)]

    def barrier(self):
        P = self.P
        b = Buf("bar")
        for e in Prog.CENG:
            if P.ncomp[e] > 0:
                b.r[e] = P.ncomp[e] - 1
        for t in P.all_dma:
            b.r[t] = True
        P.all_dma = []
        bt = self.bar_t
        tok = P.op("dve", lambda h: h.memset(bt[:, 0:1], 0.0), writes=[b])
        P.base_dep = tok

    def xbuf_of(self, c0, w):
        bs = []
        for i, (t0, tw) in enumerate(self.tiles):
            if c0 < t0 + tw and c0 + w > t0:
                bs.append(self.xbuf[i])
        if c0 < self.H or c0 + w > self.H + self.NT:
            bs.append(self.xhalo)
        return bs

    def build(self):
        cfg = self.cfg
        nc, P, NT, H, NTE = self.nc, self.P, self.NT, self.H, self.NTE
        stages = cfg["stages"]
        xT_d = self.din("xT", [D, NTE])
        consts_d = self.din("consts", [128, NCONST])

        self.xT = self.sb("xT_sb", [128, 8, NTE])
        self.xbuf = [Buf(f"x{t}") for t in range(len(self.tiles))]
        self.xhalo = Buf("xhalo")
        self.consts = self.sb("consts_sb", [128, NCONST])
        self.cbuf = Buf("consts")
        self.modtab = self.sb("modtab", [128, 480])
        self.modv = self.modtab[:, 0:288]
        self.modA = self.modtab[:, 288:384]
        self.modG = self.modtab[:, 384:480]
        self.mbuf = Buf("mod")
        self.ones = self.sb("ones", [128, 128], BF16)
        self.onesb = Buf("ones")
        self.bar_t = self.sb("bar", [128, 2])
        self.epsb = self.sb("epsb", [128, 1])
        self.FR = Region(self.sb("freg", [128, FREG], F32), FREG)
        self.BR = Region(self.sb("breg", [128, BREG], BF16), BREG)
        self.psum = [self.stack.enter_context(nc.psum_tensor(f"pb{i}", [128, 512], F32))
                     for i in range(8)]
        self.pb = [Buf(f"psum{i}") for i in range(8)]

        P.dma("sp", lambda h: h.dma_start(out=self.consts[:], in_=consts_d[:, :]), writes=[self.cbuf])
        xv = xT_d.rearrange("(c p) n -> p c n", p=128)
        for i, (c0, w) in enumerate(self.tiles):
            a, b = c0, c0 + w
            wr = [self.xbuf[i]]
            if i == 0:
                a = 0
                wr.append(self.xhalo)
            if i == len(self.tiles) - 1:
                b = NTE
                wr.append(self.xhalo)
            P.dma("sp", lambda h, a=a, b=b: h.dma_start(out=self.xT[:, :, a:b], in_=xv[:, :, a:b]),
                  writes=wr)
        P.op("dve", lambda h: h.memset(self.ones[:], 1.0), writes=[self.onesb])
        P.op("dve", lambda h: h.memset(self.epsb[:], EPS), writes=[self.onesb])

        if any(s[0] == "mod" for s in stages):
            modw_d = self.din("modw", [NL, D, 9 * D])
            self.stage_mod(modw_d)
            mt_o = self.dout("modtab_out", [128, 480])
            self.outs.append(P.dma("sp", lambda h: h.dma_start(out=mt_o[:, :], in_=self.modtab[:]),
                                   reads=[self.mbuf]))
        else:
            mt_i = self.din("modtab_in", [128, 480])
            P.dma("sp", lambda h: h.dma_start(out=self.modtab[:], in_=mt_i[:, :]), writes=[self.mbuf])

        dr = {}
        for st in stages:
            kind = st[0]
            if kind == "ffn":
                if "wgu" not in dr:
                    nf = sum(1 for s in stages if s[0] == "ffn")
                    dr["wgu"] = self.din("wgu", [nf, 11, 128, 8 * 512])
                    dr["wdn"] = self.din("wdn", [nf, DFF, D])
                    dr["nf"] = 0
                    self.nffn = nf
                    self.wgu_f32, self.wdn_f32 = dr["wgu"], dr["wdn"]
                    self.wgu_bf = self.nc.dram_tensor("wgu_bf", [nf, 11, 128, 8 * 512], BF16).ap()
                    self.wdn_bf = self.nc.dram_tensor("wdn_bf", [nf, DFF, D], BF16).ap()
                    self.wsb = [Buf(f"wsb{i}") for i in range(nf)]
                    self.precast(0)
                self.stage_ffn(st[1], st[2], dr["wgu"], dr["wdn"], dr["nf"])
                dr["nf"] += 1
            elif kind == "final":
                self.stage_final()
            elif kind == "pool":
                self.stage_pool(st[1])
            elif kind == "conv":
                self.stage_conv(st[1])
            elif kind == "qkv":
                self.stage_qkv(st[1], st[2])
            elif kind == "attn":
                self.stage_attn(st[1], st[2])
            elif kind == "halo":
                self.stage_halo()
            elif kind == "gather":
                self.stage_gather(st[1])

        xo_d = self.dout("xout", [D, NT])
        ov = xo_d.rearrange("(c p) n -> p c n", p=128)
        for i, (c0, w) in enumerate(self.tiles):
            self.outs.append(P.dma("sp", lambda h, c0=c0, w=w: h.dma_start(
                out=ov[:, :, c0 - H:c0 - H + w], in_=self.xT[:, :, c0:c0 + w]), reads=[self.xbuf[i]]))
        self.emit_all(self.outs)
        return nc

    def emit_all(self, outs):
        P, nc = self.P, self.nc
        P.finalize()
        self.nwaits = {}
        with nc.Block() as block:
            @block.tensor
            def _(e):
                self.nwaits["pe"] = P.emit("pe", e)

            @block.vector
            def _(e):
                self.nwaits["dve"] = P.emit("dve", e)

            @block.scalar
            def _(e):
                self.nwaits["act"] = P.emit("act", e)

            @block.gpsimd
            def _(e):
                self.nwaits["pool"] = P.emit("pool", e)
                P.final_waits("pool", e, outs, "pool")

            @block.sync
            def _(e):
                self.nwaits["sp"] = P.emit("sp", e)
                P.final_waits("sp", e, outs, "sp")
        self.stack.close()

    def stage_mod(self, modw_d):
        P, nc = self.P, self.nc
        self.new_stage()
        cact = self.sb("cact", [128, 8], BF16)[:]
        cb = Buf("cact")
        wm = [self.BR.take(8, 512) for i in range(2)]
        wmb = [Buf(f"wm{i}") for i in range(2)]
        mps, mpb = self.psum[0], self.pb[0]
        co = COFF["c"]
        P.op("act", lambda h: h.activation(out=cact, in_=self.consts[:, co:co + 8], func=AF.Silu),
             reads=[self.cbuf], writes=[cb])
        for l in range(NL):
            mv = modw_d[l].rearrange("(c p) n -> p c n", p=128)
            for piece in range(18):
                s = self.ring("wm", 2)
                P.dma("pool", lambda h, s=s, piece=piece, mv=mv: h.dma_start(
                    out=wm[s], in_=mv[:, :, piece * 512:(piece + 1) * 512]), writes=[wmb[s]])
                for jj in range(4):
                    j = piece * 4 + jj
                    for c in range(8):
                        P.op("pe", lambda h, s=s, jj=jj, c=c, j=j: h.matmul(
                            mps[:, j:j + 1], wm[s][:, c, jj * 128:(jj + 1) * 128], cact[:, c:c + 1],
                            start=(c == 0), stop=(c == 7)),
                            reads=[wmb[s], cb], writes=[mpb])
            bo = COFF[f"modb{l}"]
            P.op("dve", lambda h, l=l, bo=bo: h.tensor_tensor(
                out=self.modv[:, l * 72:(l + 1) * 72], in0=mps[:, 0:72],
                in1=self.consts[:, bo:bo + 72], op=ALU.add),
                reads=[mpb, self.cbuf], writes=[self.mbuf])
            go = COFF[f"ng{l}"]
            for k in range(3):
                sc = self.modv[:, l * 72 + (3 * k + 1) * 8: l * 72 + (3 * k + 2) * 8]
                P.op("dve", lambda h, l=l, k=k, sc=sc, go=go: h.scalar_tensor_tensor(
                    out=self.modA[:, l * 24 + k * 8: l * 24 + (k + 1) * 8], in0=sc, scalar=1.0,
                    in1=self.consts[:, go + k * 8: go + (k + 1) * 8], op0=ALU.add, op1=ALU.mult),
                    reads=[self.mbuf, self.cbuf], writes=[self.mbuf])
                gt = self.modv[:, l * 72 + (3 * k + 2) * 8: l * 72 + (3 * k + 3) * 8]
                P.op("dve", lambda h, l=l, k=k, gt=gt: h.tensor_scalar(
                    out=self.modG[:, l * 24 + k * 8: l * 24 + (k + 1) * 8], in0=gt,
                    scalar1=(1.0 if k == 1 else 0.5), scalar2=None, op0=ALU.mult),
                    reads=[self.mbuf], writes=[self.mbuf])

    def alloc_norm_scratch(self, ssbank, tmp=True):
        S = {}
        S["sq"] = [self.BR.take(TT) for i in range(2)]
        S["sqb"] = [Buf() for _ in range(2)]
        S["ssp"] = self.psum[ssbank]
        S["sspb"] = self.pb[ssbank]
        S["rstd"] = self.FR.take(TT)
        S["rstdb"] = Buf()
        if tmp:
            S["tmp"] = [self.FR.take(TT) for i in range(2)]
            S["tmpb"] = [Buf() for _ in range(2)]
        return S

    def rstd_cols(self, c0, w, S):
        P = self.P
        xb = self.xbuf_of(c0, w)
        for c in range(8):
            s = self.ring("sq", 2)
            P.op("act", lambda h, c=c, s=s: h.activation(out=S["sq"][s][:, :w], in_=self.xT[:, c, c0:c0 + w],
                                                         func=AF.Square),
                 reads=xb, writes=[S["sqb"][s]])
            P.op("pe", lambda h, c=c, s=s: h.matmul(S["ssp"][:, :w], self.ones[:], S["sq"][s][:, :w],
                                                    start=(c == 0), stop=(c == 7)),
                 reads=[S["sqb"][s], self.onesb], writes=[S["sspb"]])
        P.op("act", lambda h: h.activation(out=S["rstd"][:, :w], in_=S["ssp"][:, :w], func=AF.Sqrt,
                                           bias=self.epsb[:], scale=1.0 / D),
             reads=[S["sspb"], self.onesb], writes=[S["rstdb"]])
        P.op("dve", lambda h: h.reciprocal(out=S["rstd"][:, :w], in_=S["rstd"][:, :w]),
             reads=[S["rstdb"]], writes=[S["rstdb"]])

    def modulate_cols(self, l, k, c0, w, S, dst, dstb):
        P = self.P
        A = lambda c: self.modA[:, l * 24 + k * 8 + c: l * 24 + k * 8 + c + 1]
        SH = lambda c: self.modv[:, l * 72 + 3 * k * 8 + c: l * 72 + 3 * k * 8 + c + 1]
        xb = self.xbuf_of(c0, w)
        for c in range(8):
            s = self.ring("tmp", 2)
            P.op("dve", lambda h, c=c, s=s: h.scalar_tensor_tensor(
                out=S["tmp"][s][:, :w], in0=self.xT[:, c, c0:c0 + w], scalar=A(c), in1=S["rstd"][:, :w],
                op0=ALU.mult, op1=ALU.mult),
                reads=xb + [S["rstdb"], self.mbuf], writes=[S["tmpb"][s]])
            P.op("act", lambda h, c=c, s=s: h.activation(
                out=dst(c), in_=S["tmp"][s][:, :w], func=AF.Identity, bias=SH(c), scale=1.0),
                reads=[S["tmpb"][s], self.mbuf], writes=[dstb])

    def precast(self, fi):
        P = self.P
        for G2 in range(11):
            P.dma("pool", lambda h, G2=G2: h.dma_start(out=self.wgu_bf[fi, G2], in_=self.wgu_f32[fi, G2]),
                  writes=[self.wsb[fi]])
        for j in range(11):
            P.dma("pool", lambda h, j=j: h.dma_start(out=self.wdn_bf[fi, j * 256:(j + 1) * 256, :],
                                                     in_=self.wdn_f32[fi, j * 256:(j + 1) * 256, :]),
                  writes=[self.wsb[fi]])

    def stage_ffn(self, l, which, wgu_d, wdn_d, fi):
        P, nc = self.P, self.nc
        self.new_stage()
        if fi + 1 < self.nffn:
            self.precast(fi + 1)
        wgu_d, wdn_d = self.wgu_bf, self.wdn_bf
        wrd = [self.wsb[fi]]
        k = which
        S = self.alloc_norm_scratch(4)
        hT = self.BR.take(8, TT)
        hb = Buf("hT")
        aT = self.BR.take(NFC, TT)
        ab = [Buf(f"a{f}") for f in range(NFC)]
        wg = [self.BR.take(8, 512) for i in range(2)]
        wgb = [Buf(f"wg{i}") for i in range(2)]
        wd = [self.BR.take(2, 512) for i in range(3)]
        wdb = [Buf(f"wd{i}") for i in range(3)]
        sg = [self.FR.take(TT) for i in range(2)]
        sgb = [Buf(f"sg{i}") for i in range(2)]
        G = lambda c: self.modG[:, l * 24 + k * 8 + c: l * 24 + k * 8 + c + 1]
        wdv = wdn_d[fi].rearrange("(f p) n -> p f n", p=128)
        for ti, (c0, w) in enumerate(self.tiles):
            xs = lambda c, c0=c0, w=w: self.xT[:, c, c0:c0 + w]
            self.rstd_cols(c0, w, S)
            self.modulate_cols(l, k, c0, w, S, lambda c, w=w: hT[:, c, :w], hb)
            for G2 in range(11):
                ws = self.ring("wg", 2)
                P.dma("sp", lambda h, ws=ws, G2=G2: h.dma_start(
                    out=wg[ws].rearrange("p c n -> p (c n)"), in_=wgu_d[fi, G2]), reads=wrd, writes=[wgb[ws]])
                for ff in range(2):
                    f = G2 * 2 + ff
                    pi = self.ring("gu", 2)
                    gp, gb = self.psum[pi], self.pb[pi]
                    up, ub = self.psum[2 + pi], self.pb[2 + pi]
                    for c in range(8):
                        P.op("pe", lambda h, c=c, ws=ws, ff=ff, gp=gp, w=w: h.matmul(
                            gp[:, :w], wg[ws][:, c, ff * 128:(ff + 1) * 128], hT[:, c, :w],
                            start=(c == 0), stop=(c == 7)),
                            reads=[wgb[ws], hb], writes=[gb])
                    for c in range(8):
                        P.op("pe", lambda h, c=c, ws=ws, ff=ff, up=up, w=w: h.matmul(
                            up[:, :w], wg[ws][:, c, 256 + ff * 128:256 + (ff + 1) * 128], hT[:, c, :w],
                            start=(c == 0), stop=(c == 7)),
                            reads=[wgb[ws], hb], writes=[ub])
                    P.op("act", lambda h, pi=pi, gp=gp, w=w: h.activation(out=sg[pi][:, :w], in_=gp[:, :w],
                                                                          func=AF.Silu),
                         reads=[gb], writes=[sgb[pi]])
                    P.op("dve", lambda h, pi=pi, f=f, up=up, w=w: h.tensor_tensor(
                        out=aT[:, f, :w], in0=sg[pi][:, :w], in1=up[:, :w], op=ALU.mult),
                        reads=[sgb[pi], ub], writes=[ab[f]])
            for half in range(2):
                for fp in range(11):
                    ws = self.ring("wd", 3)
                    P.dma("sp", lambda h, ws=ws, fp=fp, half=half: h.dma_start(
                        out=wd[ws], in_=wdv[:, 2 * fp:2 * fp + 2, half * 512:(half + 1) * 512]),
                        reads=wrd, writes=[wdb[ws]])
                    for ff in range(2):
                        f = fp * 2 + ff
                        for dq in range(4):
                            P.op("pe", lambda h, ws=ws, ff=ff, f=f, dq=dq, w=w: h.matmul(
                                self.psum[4 + dq][:, :w], wd[ws][:, ff, dq * 128:(dq + 1) * 128], aT[:, f, :w],
                                start=(f == 0), stop=(f == NFC - 1)),
                                reads=[wdb[ws], ab[f]], writes=[self.pb[4 + dq]])
                for dq in range(4):
                    dc = half * 4 + dq
                    P.op("dve", lambda h, dc=dc, dq=dq, xs=xs, w=w: h.scalar_tensor_tensor(
                        out=xs(dc), in0=self.psum[4 + dq][:, :w], scalar=G(dc), in1=xs(dc),
                        op0=ALU.mult, op1=ALU.add),
                        reads=[self.pb[4 + dq], self.mbuf], writes=[self.xbuf[ti]])

    def stage_final(self):
        P = self.P
        self.new_stage()
        S = self.alloc_norm_scratch(0, tmp=False)
        go = COFF["final_g"]
        for ti, (c0, w) in enumerate(self.tiles):
            xs = lambda c, c0=c0, w=w: self.xT[:, c, c0:c0 + w]
            self.rstd_cols(c0, w, S)
            for c in range(8):
                P.op("dve", lambda h, c=c, xs=xs, w=w: h.scalar_tensor_tensor(
                    out=xs(c), in0=xs(c), scalar=self.consts[:, go + c:go + c + 1], in1=S["rstd"][:, :w],
                    op0=ALU.mult, op1=ALU.mult),
                    reads=[S["rstdb"], self.cbuf], writes=[self.xbuf[ti]])


    def stage_pool(self, l):
        P, H, NT = self.P, self.H, self.NT
        assert H == 8
        self.new_stage()
        k = 1
        pw_d = self.din("pool_w", [4, 256, 256])
        inv_d = self.din("pool_inv", [128, 4, self.NTE])
        S = self.alloc_norm_scratch(0, tmp=False)
        hx = self.FR.take(2, TT)
        sA = self.FR.take(2, TT)
        sB = self.FR.take(2, TT)
        invt = self.FR.take(TT)
        hxb, sAb, sBb, invb = Buf(), Buf(), Buf(), Buf()
        dg = [self.BR.take(8, TT) for _ in range(2)]
        dgb = [Buf() for _ in range(2)]
        pw = self.BR.take(8, 256)
        pwb = Buf()
        gsv = self.sb("gsv", [128, 8])
        gsb = Buf()
        P.dma("pool", lambda h: h.dma_start(out=pw, in_=pw_d.rearrange("g (cc p) e -> p (g cc) e", p=128)),
              writes=[pwb])
        po = COFF["pool_scale"]
        P.op("dve", lambda h: h.tensor_tensor(out=gsv[:], in0=self.modG[:, l * 24 + 8: l * 24 + 16],
                                              in1=self.consts[:, po:po + 8], op=ALU.mult),
             reads=[self.mbuf, self.cbuf], writes=[gsb])
        A = lambda c: self.modA[:, l * 24 + k * 8 + c: l * 24 + k * 8 + c + 1]
        SH = lambda c: self.modv[:, l * 72 + 3 * k * 8 + c: l * 72 + 3 * k * 8 + c + 1]
        mL = self.consts[:, COFF["maskL"]:COFF["maskL"] + 1]
        mR = self.consts[:, COFF["maskR"]:COFF["maskR"] + 1]
        OW = self.cfg.get("pool_ow", 496)
        otiles = [(H + o, min(OW, NT - o)) for o in range(0, NT, OW)]

        def finish(i):
            o0, ow = otiles[i]
            d = dg[i % 2]
            for g in range(4):
                for ec in range(2):
                    dc = 2 * g + ec
                    pi = 1 + self.ring("py", 4)
                    for cc in range(2):
                        P.op("pe", lambda h, g=g, ec=ec, cc=cc, pi=pi, d=d, ow=ow: h.matmul(
                            self.psum[pi][:, :ow], pw[:, g * 2 + cc, ec * 128:(ec + 1) * 128], d[:, g * 2 + cc, :ow],
                            start=(cc == 0), stop=(cc == 1)),
                            reads=[pwb, dgb[i % 2]], writes=[self.pb[pi]])
                    P.op("dve", lambda h, dc=dc, pi=pi, o0=o0, ow=ow: h.scalar_tensor_tensor(
                        out=self.xT[:, dc, o0:o0 + ow], in0=self.psum[pi][:, :ow], scalar=gsv[:, dc:dc + 1],
                        in1=self.xT[:, dc, o0:o0 + ow], op0=ALU.mult, op1=ALU.add),
                        reads=[self.pb[pi], gsb], writes=self.xbuf_of(o0, ow))

        for i, (o0, ow) in enumerate(otiles):
            e0, ew = o0 - 8, ow + 16
            xb = self.xbuf_of(e0, ew)
            self.rstd_cols(e0, ew, S)
            d = dg[i % 2]
            for g in range(4):
                P.dma("sp", lambda h, g=g, e0=e0, ew=ew: h.dma_start(out=invt[:, :ew], in_=inv_d[:, g, e0:e0 + ew]),
                      writes=[invb])
                for cc in range(2):
                    c = 2 * g + cc
                    P.op("dve", lambda h, c=c, cc=cc, e0=e0, ew=ew: h.scalar_tensor_tensor(
                        out=hx[:, cc, :ew], in0=self.xT[:, c, e0:e0 + ew], scalar=A(c), in1=S["rstd"][:, :ew],
                        op0=ALU.mult, op1=ALU.mult), reads=xb + [S["rstdb"], self.mbuf], writes=[hxb])
                    P.op("act", lambda h, c=c, cc=cc, ew=ew: h.activation(
                        out=hx[:, cc, :ew], in_=hx[:, cc, :ew], func=AF.Identity, bias=SH(c), scale=1.0),
                        reads=[hxb, self.mbuf], writes=[hxb])
                if i == 0:
                    P.op("dve", lambda h: h.tensor_scalar(out=hx[:, :, 0:8], in0=hx[:, :, 0:8], scalar1=mL,
                                                          scalar2=None, op0=ALU.mult),
                         reads=[hxb, self.cbuf], writes=[hxb])
                if i == len(otiles) - 1:
                    P.op("dve", lambda h, ew=ew: h.tensor_scalar(out=hx[:, :, ew - 8:ew], in0=hx[:, :, ew - 8:ew],
                                                                 scalar1=mR, scalar2=None, op0=ALU.mult),
                         reads=[hxb, self.cbuf], writes=[hxb])
                P.op("dve", lambda h, ew=ew: h.tensor_tensor(out=sA[:, :, 1:ew], in0=hx[:, :, 0:ew - 1],
                                                             in1=hx[:, :, 1:ew], op=ALU.add),
                     reads=[hxb], writes=[sAb])
                fin, finb = sA, sAb
                if g >= 1:
                    P.op("dve", lambda h, ew=ew: h.tensor_tensor(out=sB[:, :, 2:ew - 1], in0=sA[:, :, 1:ew - 2],
                                                                 in1=sA[:, :, 3:ew], op=ALU.add),
                         reads=[sAb], writes=[sBb])
                    fin, finb = sB, sBb
                if g >= 2:
                    P.op("dve", lambda h, ew=ew: h.tensor_tensor(out=sA[:, :, 4:ew - 3], in0=sB[:, :, 2:ew - 5],
                                                                 in1=sB[:, :, 6:ew - 1], op=ALU.add),
                         reads=[sBb], writes=[sAb])
                    fin, finb = sA, sAb
                if g >= 3:
                    P.op("dve", lambda h, ew=ew: h.tensor_tensor(out=sB[:, :, 8:ew - 7], in0=sA[:, :, 4:ew - 11],
                                                                 in1=sA[:, :, 12:ew - 3], op=ALU.add),
                         reads=[sAb], writes=[sBb])
                    fin, finb = sB, sBb
                for cc in range(2):
                    P.op("dve", lambda h, cc=cc, fin=fin, ow=ow: h.tensor_tensor(
                        out=fin[:, cc, 8:8 + ow], in0=fin[:, cc, 8:8 + ow], in1=invt[:, 8:8 + ow], op=ALU.mult),
                        reads=[finb, invb], writes=[finb])
                    P.op("dve", lambda h, cc=cc, fin=fin, ow=ow, g=g, d=d: h.tensor_tensor(
                        out=d[:, g * 2 + cc, :ow], in0=fin[:, cc, 8:8 + ow], in1=hx[:, cc, 8:8 + ow],
                        op=ALU.subtract),
                        reads=[finb, hxb], writes=[dgb[i % 2]])
            if i >= 1:
                finish(i - 1)
        finish(len(otiles) - 1)

    def stage_conv(self, l):
        P, H, NT = self.P, self.H, self.NT
        assert H >= 1
        self.new_stage()
        k = 1
        win_d = self.din("conv_win", [8, 128, 8 * 384])
        wout_d = self.din("conv_wout", [D, D])
        S = self.alloc_norm_scratch(0)
        hT = self.BR.take(8, TT)
        hb = Buf()
        wi = [self.BR.take(8, 384) for _ in range(2)]
        wib = [Buf() for _ in range(2)]
        mT = [self.BR.take(8, TT) for _ in range(2)]
        mTb = [Buf() for _ in range(2)]
        wo = self.BR.take(8, D)
        wob = Buf()
        t1 = self.FR.take(TT)
        z = self.FR.take(TT)
        zc = self.FR.take(TT)
        t1b, zb, zcb = Buf(), Buf(), Buf()
        P.dma("pool", lambda h: h.dma_start(out=wo, in_=wout_d.rearrange("(c p) n -> p c n", p=128)),
              writes=[wob])
        G = lambda c: self.modG[:, l * 24 + k * 8 + c: l * 24 + k * 8 + c + 1]
        cw = lambda kk, dc: self.consts[:, COFF["conv_w"] + kk * 8 + dc: COFF["conv_w"] + kk * 8 + dc + 1]
        mL = self.consts[:, COFF["maskL"]:COFF["maskL"] + 1]
        mR = self.consts[:, COFF["maskR"]:COFF["maskR"] + 1]
        OW = 510
        otiles = [(H + o, min(OW, NT - o)) for o in range(0, NT, OW)]

        def finish(i):
            o0, ow = otiles[i]
            m = mT[i % 2]
            for oc in range(8):
                pi = 4 + self.ring("cy", 4)
                for dc in range(8):
                    P.op("pe", lambda h, oc=oc, dc=dc, pi=pi, m=m, ow=ow: h.matmul(
                        self.psum[pi][:, :ow], wo[:, dc, oc * 128:(oc + 1) * 128], m[:, dc, :ow],
                        start=(dc == 0), stop=(dc == 7)),
                        reads=[wob, mTb[i % 2]], writes=[self.pb[pi]])
                P.op("dve", lambda h, oc=oc, pi=pi, o0=o0, ow=ow: h.scalar_tensor_tensor(
                    out=self.xT[:, oc, o0:o0 + ow], in0=self.psum[pi][:, :ow], scalar=G(oc),
                    in1=self.xT[:, oc, o0:o0 + ow], op0=ALU.mult, op1=ALU.add),
                    reads=[self.pb[pi], self.mbuf], writes=self.xbuf_of(o0, ow))

        for i, (o0, ow) in enumerate(otiles):
            e0, ew = o0 - 1, ow + 2
            self.rstd_cols(e0, ew, S)
            self.modulate_cols(l, k, e0, ew, S, lambda c, ew=ew: hT[:, c, :ew], hb)
            if i >= 1:
                finish(i - 1)
            m = mT[i % 2]
            for dc in range(8):
                ws = self.ring("wi", 2)
                P.dma("pool", lambda h, ws=ws, dc=dc: h.dma_start(
                    out=wi[ws].rearrange("p c n -> p (c n)"), in_=win_d[dc]), writes=[wib[ws]])
                for part in range(3):
                    for c in range(8):
                        P.op("pe", lambda h, ws=ws, part=part, c=c, ew=ew: h.matmul(
                            self.psum[1 + part][:, :ew], wi[ws][:, c, part * 128:(part + 1) * 128], hT[:, c, :ew],
                            start=(c == 0), stop=(c == 7)),
                            reads=[wib[ws], hb], writes=[self.pb[1 + part]])
                P.op("act", lambda h, ew=ew: h.activation(out=t1[:, :ew], in_=self.psum[2][:, :ew], func=AF.Copy),
                     reads=[self.pb[2]], writes=[t1b])
                P.op("dve", lambda h, ew=ew: h.tensor_tensor(out=z[:, :ew], in0=t1[:, :ew], in1=self.psum[3][:, :ew],
                                                             op=ALU.mult),
                     reads=[t1b, self.pb[3]], writes=[zb])
                if i == 0:
                    P.op("dve", lambda h: h.tensor_scalar(out=z[:, 0:1], in0=z[:, 0:1], scalar1=mL, scalar2=None,
                                                          op0=ALU.mult), reads=[zb, self.cbuf], writes=[zb])
                if i == len(otiles) - 1:
                    P.op("dve", lambda h, ew=ew: h.tensor_scalar(out=z[:, ew - 1:ew], in0=z[:, ew - 1:ew], scalar1=mR,
                                                                 scalar2=None, op0=ALU.mult),
                         reads=[zb, self.cbuf], writes=[zb])
                P.op("dve", lambda h, dc=dc, ow=ow: h.tensor_scalar(out=zc[:, :ow], in0=z[:, 0:ow], scalar1=cw(0, dc),
                                                                    scalar2=None, op0=ALU.mult),
                     reads=[zb, self.cbuf], writes=[zcb])
                for kk in (1, 2):
                    P.op("dve", lambda h, dc=dc, ow=ow, kk=kk: h.scalar_tensor_tensor(
                        out=zc[:, :ow], in0=z[:, kk:kk + ow], scalar=cw(kk, dc), in1=zc[:, :ow],
                        op0=ALU.mult, op1=ALU.add), reads=[zb, zcb, self.cbuf], writes=[zcb])
                P.op("dve", lambda h, dc=dc, ow=ow, m=m: h.tensor_tensor(
                    out=m[:, dc, :ow], in0=zc[:, :ow], in1=self.psum[1][:, 1:1 + ow], op=ALU.mult),
                    reads=[zcb, self.pb[1]], writes=[mTb[i % 2]])
        finish(len(otiles) - 1)

    def stage_qkv(self, kind, l):
        P, H, NT = self.P, self.H, self.NT
        self.new_stage()
        k = 1
        gqa = (kind == "gqa")
        KF = 256 if gqa else 1024
        VF = 256 if gqa else 1024
        nblk = 6 if gqa else 10
        nqb, nkb = 4, (1 if gqa else 4)
        sfx = ("_" + kind) if self.fused else ""
        w_d = self.din("wqkv" + sfx, [nblk, 128, 8 * 512])
        rc_d = self.din("ropeC" + sfx, [128, NT])
        rs_d = self.din("ropeS" + sfx, [128, NT])
        KBL = min(2048, NT)
        NBL = NT // KBL
        CPB = KBL // 128
        VH = VF // 128
        if self.fused:
            q_o = self.nc.dram_tensor("q_" + kind, [1024, NT], BF16).ap()
            nkh = KF // 128
            k_ts = [self.nc.dram_tensor(f"kb_{kind}{j}", [128, NT], BF16) for j in range(nkh)]
            v_ts = [self.nc.dram_tensor(f"vb_{kind}{j}", [NBL * 128, CPB * 128], BF16) for j in range(VH)]
            v_o4 = [t.ap().rearrange("(b p) (c e) -> b p c e", b=NBL, p=128, c=CPB, e=128) for t in v_ts]
            k_o = None
            self.scr[kind] = dict(q=q_o, k_ts=k_ts, v_ts=v_ts, NBL=NBL, CPB=CPB)
        else:
            q_o = self.dout("q_out", [1024, NT], BF16)
            k_o = self.dout("k_out", [KF, NT], BF16)
            v_o = self.dout("v_out", [NT, VF], BF16)
        S = self.alloc_norm_scratch(0)
        hT = self.BR.take(8, TT)
        hb = Buf()
        wq = [self.BR.take(8, 512) for _ in range(2)]
        wqb = [Buf() for _ in range(2)]
        stg = [self.BR.take(TT) for _ in range(4)]
        stgb = [Buf() for _ in range(4)]
        sqq = self.BR.take(TT)
        sqqb = Buf()
        Ct = self.FR.take(TT)
        St = self.FR.take(TT)
        ctb = Buf()
        t1 = self.FR.take(TT)
        t2 = self.FR.take(TT)
        rq = self.FR.take(TT)
        t1b, t2b, rqb = Buf(), Buf(), Buf()
        gcol = {"q": (COFF["gqa_qg"], COFF["gqa_qg_sw"]), "k": (COFF["gqa_kg"], COFF["gqa_kg_sw"])}
        for ti, (c0, w) in enumerate(self.tiles):
            tok0 = c0 - H
            self.rstd_cols(c0, w, S)
            self.modulate_cols(l, k, c0, w, S, lambda c, w=w: hT[:, c, :w], hb)
            P.dma("sp", lambda h, tok0=tok0, w=w: h.dma_start(out=Ct[:, :w], in_=rc_d[:, tok0:tok0 + w]), writes=[ctb])
            P.dma("sp", lambda h, tok0=tok0, w=w: h.dma_start(out=St[:, :w], in_=rs_d[:, tok0:tok0 + w]), writes=[ctb])
            for b in range(nblk):
                ws = self.ring("wq", 2)
                P.dma("pool", lambda h, ws=ws, b=b: h.dma_start(
                    out=wq[ws].rearrange("p c n -> p (c n)"), in_=w_d[b]), writes=[wqb[ws]])
                if b < nqb + nkb:
                    which = "q" if b < nqb else "k"
                    dest = q_o if which == "q" else k_o
                    for ff in range(2):
                        fc = (b if which == "q" else b - nqb) * 2 + ff
                        pr = self.ring("qp", 2)
                        qp, qpb = self.psum[1 + pr], self.pb[1 + pr]
                        qs, qsb = self.psum[3 + pr], self.pb[3 + pr]
                        for c in range(8):
                            P.op("pe", lambda h, ws=ws, ff=ff, c=c, qp=qp, w=w: h.matmul(
                                qp[:, :w], wq[ws][:, c, ff * 128:(ff + 1) * 128], hT[:, c, :w],
                                start=(c == 0), stop=(c == 7)), reads=[wqb[ws], hb], writes=[qpb])
                        for c in range(8):
                            P.op("pe", lambda h, ws=ws, ff=ff, c=c, qs=qs, w=w: h.matmul(
                                qs[:, :w], wq[ws][:, c, 256 + ff * 128:256 + (ff + 1) * 128], hT[:, c, :w],
                                start=(c == 0), stop=(c == 7)), reads=[wqb[ws], hb], writes=[qsb])
                        si = self.ring("stg", 4)
                        if not gqa:
                            P.op("dve", lambda h, qp=qp, w=w: h.tensor_tensor(out=t1[:, :w], in0=qp[:, :w], in1=Ct[:, :w],
                                                                               op=ALU.mult),
                                 reads=[qpb, ctb], writes=[t1b])
                            P.op("dve", lambda h, qs=qs, w=w: h.tensor_tensor(out=t2[:, :w], in0=qs[:, :w], in1=St[:, :w],
                                                                               op=ALU.mult),
                                 reads=[qsb, ctb], writes=[t2b])
                            P.op("dve", lambda h, si=si, w=w: h.tensor_tensor(out=stg[si][:, :w], in0=t1[:, :w],
                                                                               in1=t2[:, :w], op=ALU.add),
                                 reads=[t1b, t2b], writes=[stgb[si]])
                        else:
                            g0, g1 = gcol[which]
                            P.op("act", lambda h, qp=qp, w=w: h.activation(out=sqq[:, :w], in_=qp[:, :w], func=AF.Square),
                                 reads=[qpb], writes=[sqqb])
                            P.op("pe", lambda h, w=w: h.matmul(self.psum[5][:, :w], self.ones[:], sqq[:, :w],
                                                               start=True, stop=True),
                                 reads=[sqqb, self.onesb], writes=[self.pb[5]])
                            P.op("act", lambda h, w=w: h.activation(out=rq[:, :w], in_=self.psum[5][:, :w], func=AF.Sqrt,
                                                                    bias=self.epsb[:], scale=1.0 / 128),
                                 reads=[self.pb[5], self.onesb], writes=[rqb])
                            P.op("dve", lambda h, w=w: h.reciprocal(out=rq[:, :w], in_=rq[:, :w]),
                                 reads=[rqb], writes=[rqb])
                            P.op("dve", lambda h, qp=qp, w=w, g0=g0: h.scalar_tensor_tensor(
                                out=t1[:, :w], in0=qp[:, :w], scalar=self.consts[:, g0:g0 + 1], in1=Ct[:, :w],
                                op0=ALU.mult, op1=ALU.mult), reads=[qpb, ctb, self.cbuf], writes=[t1b])
                            P.op("dve", lambda h, qs=qs, w=w, g1=g1: h.scalar_tensor_tensor(
                                out=t2[:, :w], in0=qs[:, :w], scalar=self.consts[:, g1:g1 + 1], in1=St[:, :w],
                                op0=ALU.mult, op1=ALU.mult), reads=[qsb, ctb, self.cbuf], writes=[t2b])
                            P.op("dve", lambda h, w=w: h.tensor_tensor(out=t1[:, :w], in0=t1[:, :w], in1=t2[:, :w],
                                                                       op=ALU.add),
                                 reads=[t1b, t2b], writes=[t1b])
                            P.op("dve", lambda h, si=si, w=w: h.tensor_tensor(out=stg[si][:, :w], in0=t1[:, :w],
                                                                               in1=rq[:, :w], op=ALU.mult),
                                 reads=[t1b, rqb], writes=[stgb[si]])
                        if self.fused and which == "k":
                            dap = k_ts[fc].ap()[:, tok0:tok0 + w]
                        else:
                            dap = dest[fc * 128:(fc + 1) * 128, tok0:tok0 + w]
                        self.outs.append(P.dma("sp", lambda h, si=si, dap=dap, w=w: h.dma_start(
                            out=dap, in_=stg[si][:, :w]), reads=[stgb[si]]))
                else:
                    vb = b - nqb - nkb
                    vw = 256 if gqa else 512
                    for tc in range(w // 128):
                        pr = 6 + self.ring("vp", 2)
                        for c in range(8):
                            P.op("pe", lambda h, ws=ws, c=c, pr=pr, tc=tc, vw=vw: h.matmul(
                                self.psum[pr][:, :vw], hT[:, c, tc * 128:(tc + 1) * 128], wq[ws][:, c, 0:vw],
                                start=(c == 0), stop=(c == 7)), reads=[wqb[ws], hb], writes=[self.pb[pr]])
                        si = self.ring("stg", 4)
                        P.op("act", lambda h, si=si, pr=pr, vw=vw: h.activation(out=stg[si][:, :vw], in_=self.psum[pr][:, :vw],
                                                                                func=AF.Copy),
                             reads=[self.pb[pr]], writes=[stgb[si]])
                        if self.fused:
                            tk = tok0 + tc * 128
                            bl, cl = tk // KBL, (tk % KBL) // 128
                            nh = vw // 128
                            h0 = vb * 4
                            for hh in range(nh):
                                self.outs.append(P.dma("sp", lambda h, si=si, bl=bl, cl=cl, hh=hh, h0=h0: h.dma_start(
                                    out=v_o4[h0 + hh][bl, :, cl, :], in_=stg[si][:, hh * 128:(hh + 1) * 128]),
                                    reads=[stgb[si]]))
                        else:
                            self.outs.append(P.dma("sp", lambda h, si=si, tok0=tok0, tc=tc, vb=vb, vw=vw: h.dma_start(
                                out=v_o[tok0 + tc * 128:tok0 + (tc + 1) * 128, vb * 512:vb * 512 + vw], in_=stg[si][:, :vw]),
                                reads=[stgb[si]]))

    def stage_gather(self, kind):
        P = self.P
        self.new_stage()
        sc = self.scr[kind]
        sc["kgb"], sc["vgb"] = Buf(), Buf()
        NT = self.NT
        if self.cpbt == 1:
            sc["kg"] = [t.ap() for t in sc["k_ts"]]
            sc["vg"] = [t.ap() for t in sc["v_ts"]]
            return
        R4 = self.cpbt
        vr, vc = sc["NBL"] * 128, sc["CPB"] * 128
        sc["kg"], sc["vg"] = [], []
        for j, t in enumerate(sc["k_ts"]):
            g = self.nc.dram_tensor(f"kg_{kind}{j}", [R4 * 128, NT], BF16)
            P.cc(lambda h, t=t, g=g: h.collective_compute("AllGather", ALU.bypass, replica_groups=self.groups,
                                                          ins=[t.ap().opt()], outs=[g.ap().opt()]),
                 writes=[sc["kgb"]])
            sc["kg"].append(g.ap())
        for j, t in enumerate(sc["v_ts"]):
            g = self.nc.dram_tensor(f"vg_{kind}{j}", [R4 * vr, vc], BF16)
            P.cc(lambda h, t=t, g=g: h.collective_compute("AllGather", ALU.bypass, replica_groups=self.groups,
                                                          ins=[t.ap().opt()], outs=[g.ap().opt()]),
                 writes=[sc["vgb"]])
            sc["vg"].append(g.ap())

    def stage_halo(self):
        P, H, NT, NTE = self.P, self.H, self.NT, self.NTE
        self.new_stage()
        i = self.nhalo
        self.nhalo += 1
        allx = self.xbuf + [self.xhalo]
        if self.cpbt == 1:
            P.op("dve", lambda h: h.memset(self.xT[:, :, 0:H], 0.0), writes=[self.xhalo])
            P.op("dve", lambda h: h.memset(self.xT[:, :, H + NT:NTE], 0.0), writes=[self.xhalo])
            return
        R4 = self.cpbt
        hb_t = self.nc.dram_tensor(f"hb{i}", [D, 2 * H], F32)
        hg_t = self.nc.dram_tensor(f"hg{i}", [R4 * D, 2 * H], F32)
        hbv = hb_t.ap().rearrange("(c p) n -> p c n", p=128)
        hbuf = Buf()
        P.dma("sp", lambda h: h.dma_start(out=hbv[:, :, 0:H], in_=self.xT[:, :, H:2 * H]), reads=allx, writes=[])
        P.dma("sp", lambda h: h.dma_start(out=hbv[:, :, H:2 * H], in_=self.xT[:, :, NT:NT + H]), reads=allx, writes=[])
        self.barrier()
        P.cc(lambda h: h.collective_compute("AllGather", ALU.bypass, replica_groups=self.groups,
                                            ins=[hb_t.ap().opt()], outs=[hg_t.ap().opt()]), writes=[hbuf])
        hs = self.FR.take(R4 * 8, 2 * H)
        hsb = Buf()
        P.dma("sp", lambda h: h.dma_start(out=hs, in_=hg_t.ap().rearrange("(rc p) n -> p rc n", p=128)),
              reads=[hbuf], writes=[hsb])
        sl, sr = COFF["selL"], COFF["selR"]
        for r in range(R4):
            src_l = hs[:, r * 8:(r + 1) * 8, H:2 * H]
            src_r = hs[:, r * 8:(r + 1) * 8, 0:H]
            dl = self.xT[:, :, 0:H]
            drr = self.xT[:, :, H + NT:NTE]
            if r == 0:
                P.op("dve", lambda h, src_l=src_l, dl=dl: h.tensor_scalar(
                    out=dl, in0=src_l, scalar1=self.consts[:, sl:sl + 1], scalar2=None, op0=ALU.mult),
                    reads=[hsb, self.cbuf], writes=[self.xhalo])
                P.op("dve", lambda h, src_r=src_r, drr=drr: h.tensor_scalar(
                    out=drr, in0=src_r, scalar1=self.consts[:, sr:sr + 1], scalar2=None, op0=ALU.mult),
                    reads=[hsb, self.cbuf], writes=[self.xhalo])
            else:
                P.op("dve", lambda h, src_l=src_l, dl=dl, r=r: h.scalar_tensor_tensor(
                    out=dl, in0=src_l, scalar=self.consts[:, sl + r:sl + r + 1], in1=dl, op0=ALU.mult, op1=ALU.add),
                    reads=[hsb, self.cbuf], writes=[self.xhalo])
                P.op("dve", lambda h, src_r=src_r, drr=drr, r=r: h.scalar_tensor_tensor(
                    out=drr, in0=src_r, scalar=self.consts[:, sr + r:sr + r + 1], in1=drr, op0=ALU.mult, op1=ALU.add),
                    reads=[hsb, self.cbuf], writes=[self.xhalo])

    def stage_attn(self, kind, l):
        P, H, NT, NK = self.P, self.H, self.NT, self.NK
        self.new_stage()
        gqa = (kind == "gqa")
        KB = min(2048, NT if self.fused else NK)
        NBLK = NK // KB
        CPB = KB // 128
        KF = 256 if gqa else 1024
        VH = 2 if gqa else 8
        if self.fused:
            sc = self.scr[kind]
            q_i = sc["q"]
            kg, vg = sc["kg"], sc["vg"]
            kgb, vgb = sc["kgb"], sc["vgb"]
            NBL = NT // KB
            wo_d = self.din("wo_" + kind, [D, D])

            def ksrc(kv, blk):
                r, hf = blk // NBL, blk % NBL
                return kg[kv][r * 128:(r + 1) * 128, hf * KB:(hf + 1) * KB]

            def vsrc(kv, blk):
                return vg[kv][blk * 128:(blk + 1) * 128, :]
        else:
            q_i = self.din("q_in", [1024, NT], BF16)
            k_i = self.din("k_in", [KF, NK], BF16)
            v_i = self.din("v_in", [VH, NBLK, 128, CPB * 128], BF16)
            wo_d = self.din("wo", [D, D])
            kgb, vgb = Buf(), Buf()

            def ksrc(kv, blk):
                return k_i[kv * 128:(kv + 1) * 128, blk * KB:(blk + 1) * KB]

            def vsrc(kv, blk):
                return v_i[kv, blk]
        qt = [self.BR.take(TT) for _ in range(2)]
        qtb = [Buf() for _ in range(2)]
        kt = [self.BR.take(KB) for _ in range(2)]
        ktb = [Buf() for _ in range(2)]
        vt = [self.BR.take(CPB, 128) for _ in range(2)]
        vtb = [Buf() for _ in range(2)]
        pt = [self.BR.take(TT) for _ in range(4)]
        ptb = [Buf() for _ in range(4)]
        oT = self.BR.take(8, TT)
        oTb = Buf()
        wo = self.BR.take(8, D)
        wob = Buf()
        sq = self.BR.take(TT)
        sqb = Buf()
        r1 = self.FR.take(TT)
        r2 = self.FR.take(TT)
        t1 = self.FR.take(TT)
        t2 = self.FR.take(TT)
        r1b, r2b, t1b, t2b = Buf(), Buf(), Buf(), Buf()
        zacc = [self.FR.take(TT) for _ in range(2)]
        zab = [Buf() for _ in range(2)]
        zhi = self.BR.take(TT)
        zlo = self.BR.take(TT)
        zhb = Buf()
        P.dma("pool", lambda h: h.dma_start(out=wo, in_=wo_d.rearrange("(c p) n -> p c n", p=128)), writes=[wob])
        G = lambda c: self.modG[:, l * 24 + 8 + c: l * 24 + 8 + c + 1]
        scale = (128.0 if gqa else 64.0) ** -0.5
        if not gqa:
            lam_init = 0.8 - 0.6 * float(np.exp(-0.3 * l))
            lam_d = self.din("lam", [128, 256])
            self.lam_done = True
            lamt = self.FR.take(256)
            lt = self.sb("lamtmp", [128, 8])
            lb = Buf()
            P.dma("sp", lambda h: h.dma_start(out=lamt, in_=lam_d[:, :]), writes=[lb])
            P.op("dve", lambda h: h.tensor_tensor(out=lamt[:, 0:64], in0=lamt[:, 0:64], in1=lamt[:, 64:128], op=ALU.mult),
                 reads=[lb], writes=[lb])
            P.op("dve", lambda h: h.tensor_tensor(out=lamt[:, 128:192], in0=lamt[:, 128:192], in1=lamt[:, 192:256],
                                                  op=ALU.mult), reads=[lb], writes=[lb])
            P.op("dve", lambda h: h.reduce_sum(out=lt[:, 0:1], in_=lamt[:, 0:64], axis=mybir.AxisListType.X),
                 reads=[lb], writes=[lb])
            P.op("dve", lambda h: h.reduce_sum(out=lt[:, 1:2], in_=lamt[:, 128:192], axis=mybir.AxisListType.X),
                 reads=[lb], writes=[lb])
            P.op("act", lambda h: h.activation(out=lt[:, 2:4], in_=lt[:, 0:2], func=AF.Exp), reads=[lb], writes=[lb])
            P.op("dve", lambda h: h.tensor_tensor(out=lt[:, 4:5], in0=lt[:, 3:4], in1=lt[:, 2:3], op=ALU.subtract),
                 reads=[lb], writes=[lb])
            P.op("dve", lambda h: h.tensor_scalar(out=lt[:, 4:5], in0=lt[:, 4:5], scalar1=-lam_init, scalar2=None,
                                                  op0=ALU.add), reads=[lb], writes=[lb])
            so = COFF["subln_g"]
            P.op("dve", lambda h: h.tensor_scalar(out=lt[:, 5:6], in0=self.consts[:, so:so + 1],
                                                  scalar1=(1.0 - lam_init), scalar2=None, op0=ALU.mult),
                 reads=[lb, self.cbuf], writes=[lb])
            neglam = lt[:, 4:5]
            gsub = lt[:, 5:6]
        ncomp = 1 if gqa else 2
        for ti, (c0, w) in enumerate(self.tiles):
            tok0 = c0 - H
            for u in range(8):
                qi = self.ring("qt", 2)
                P.dma("sp", lambda h, qi=qi, u=u, tok0=tok0, w=w: h.dma_start(
                    out=qt[qi][:, :w], in_=q_i[u * 128:(u + 1) * 128, tok0:tok0 + w]), writes=[qtb[qi]])
                kv = (u // 4) if gqa else u
                if gqa:
                    ob = 2 + 2 * self.ring("ob", 2)
                    Ob = [ob]
                    Zb = [ob + 1]
                else:
                    Ob = [2, 4]
                    Zb = [3, 5]
                steps = [(blk, kc, comp) for blk in range(NBLK) for kc in range(CPB) for comp in range(ncomp)]
                nst = len(steps)
                kis = {}
                info = {}

                def emit_S(idx):
                    blk, kc, comp = steps[idx]
                    if blk not in kis:
                        ki = self.ring("kt", 2)
                        kis[blk] = ki
                        P.dma("sp", lambda h, ki=ki, kv=kv, blk=blk: h.dma_start(
                            out=kt[ki], in_=ksrc(kv, blk)), reads=[kgb], writes=[ktb[ki]])
                        P.dma("sp", lambda h, ki=ki, kv=kv, blk=blk: h.dma_start(
                            out=vt[ki].rearrange("p c e -> p (c e)"), in_=vsrc(kv, blk)), reads=[vgb], writes=[vtb[ki]])
                    ki = kis[blk]
                    r0, r1_ = (0, 128) if gqa else (comp * 64, comp * 64 + 64)
                    si = self.ring("S", 2)
                    P.op("pe", lambda h, ki=ki, qi=qi, kc=kc, r0=r0, r1_=r1_, si=si, w=w: h.matmul(
                        self.psum[si][:, :w], kt[ki][r0:r1_, kc * 128:(kc + 1) * 128], qt[qi][r0:r1_, :w],
                        start=True, stop=True), reads=[ktb[ki], qtb[qi]], writes=[self.pb[si]])
                    pi = self.ring("pt", 4)
                    P.op("act", lambda h, si=si, pi=pi, w=w: h.activation(
                        out=pt[pi][:, :w], in_=self.psum[si][:, :w], func=AF.Exp, scale=scale),
                        reads=[self.pb[si]], writes=[ptb[pi]])
                    info[idx] = (ki, pi)
                    if blk == 0 and kc == 0:
                        P.op("dve", lambda h, pi=pi, comp=comp, w=w: h.tensor_copy(out=zacc[comp][:, :w], in_=pt[pi][:, :w]),
                             reads=[ptb[pi]], writes=[zab[comp]])
                    else:
                        P.op("dve", lambda h, pi=pi, comp=comp, w=w: h.tensor_tensor(
                            out=zacc[comp][:, :w], in0=zacc[comp][:, :w], in1=pt[pi][:, :w], op=ALU.add),
                            reads=[ptb[pi], zab[comp]], writes=[zab[comp]])

                def emit_PV(idx):
                    blk, kc, comp = steps[idx]
                    ki, pi = info[idx]
                    first = (blk == 0 and kc == 0)
                    last = (blk == NBLK - 1 and kc == CPB - 1)
                    P.op("pe", lambda h, ki=ki, kc=kc, pi=pi, comp=comp, w=w, first=first, last=last, Ob=Ob: h.matmul(
                        self.psum[Ob[comp]][:, :w], vt[ki][:, kc, :], pt[pi][:, :w], start=first, stop=last),
                        reads=[vtb[ki], ptb[pi]], writes=[self.pb[Ob[comp]]])

                LA = 2
                for idx in range(min(LA, nst)):
                    emit_S(idx)
                for idx in range(nst):
                    if idx + LA < nst:
                        emit_S(idx + LA)
                    emit_PV(idx)
                for comp in range(ncomp):
                    P.op("dve", lambda h, comp=comp, w=w: h.tensor_copy(out=zhi[:, :w], in_=zacc[comp][:, :w]),
                         reads=[zab[comp]], writes=[zhb])
                    P.op("dve", lambda h, comp=comp, w=w: h.tensor_tensor(out=zlo[:, :w], in0=zacc[comp][:, :w],
                                                                          in1=zhi[:, :w], op=ALU.subtract),
                         reads=[zab[comp], zhb], writes=[zhb])
                    P.op("pe", lambda h, comp=comp, w=w, Zb=Zb: h.matmul(self.psum[Zb[comp]][:, :w], self.ones[:], zhi[:, :w],
                                                                          start=True, stop=False),
                         reads=[self.onesb, zhb], writes=[self.pb[Zb[comp]]])
                    P.op("pe", lambda h, comp=comp, w=w, Zb=Zb: h.matmul(self.psum[Zb[comp]][:, :w], self.ones[:], zlo[:, :w],
                                                                          start=False, stop=True),
                         reads=[self.onesb, zhb], writes=[self.pb[Zb[comp]]])
                P.op("dve", lambda h, w=w, Zb=Zb: h.reciprocal(out=r1[:, :w], in_=self.psum[Zb[0]][:, :w]),
                     reads=[self.pb[Zb[0]]], writes=[r1b])
                if gqa:
                    P.op("dve", lambda h, w=w, u=u, Ob=Ob: h.tensor_tensor(out=oT[:, u, :w], in0=self.psum[Ob[0]][:, :w],
                                                                           in1=r1[:, :w], op=ALU.mult),
                         reads=[self.pb[Ob[0]], r1b], writes=[oTb])
                else:
                    P.op("dve", lambda h, w=w: h.reciprocal(out=r2[:, :w], in_=self.psum[5][:, :w]),
                         reads=[self.pb[5]], writes=[r2b])
                    P.op("dve", lambda h, w=w: h.tensor_tensor(out=t1[:, :w], in0=self.psum[2][:, :w], in1=r1[:, :w],
                                                               op=ALU.mult), reads=[self.pb[2], r1b], writes=[t1b])
                    P.op("dve", lambda h, w=w: h.tensor_tensor(out=t2[:, :w], in0=self.psum[4][:, :w], in1=r2[:, :w],
                                                               op=ALU.mult), reads=[self.pb[4], r2b], writes=[t2b])
                    P.op("dve", lambda h, w=w: h.scalar_tensor_tensor(out=t1[:, :w], in0=t2[:, :w], scalar=neglam,
                                                                      in1=t1[:, :w], op0=ALU.mult, op1=ALU.add),
                         reads=[t1b, t2b, lb], writes=[t1b])
                    P.op("act", lambda h, w=w: h.activation(out=sq[:, :w], in_=t1[:, :w], func=AF.Square),
                         reads=[t1b], writes=[sqb])
                    P.op("pe", lambda h, w=w: h.matmul(self.psum[6][:, :w], self.ones[:], sq[:, :w], start=True, stop=True),
                         reads=[sqb, self.onesb], writes=[self.pb[6]])
                    P.op("act", lambda h, w=w: h.activation(out=r2[:, :w], in_=self.psum[6][:, :w], func=AF.Sqrt,
                                                            bias=self.epsb[:], scale=1.0 / 128),
                         reads=[self.pb[6], self.onesb], writes=[r2b])
                    P.op("dve", lambda h, w=w: h.reciprocal(out=r2[:, :w], in_=r2[:, :w]), reads=[r2b], writes=[r2b])
                    P.op("dve", lambda h, w=w, u=u: h.scalar_tensor_tensor(out=oT[:, u, :w], in0=t1[:, :w], scalar=gsub,
                                                                           in1=r2[:, :w], op0=ALU.mult, op1=ALU.mult),
                         reads=[t1b, r2b, lb], writes=[oTb])
            for dc in range(8):
                pi = 6 + self.ring("oy", 2)
                for u in range(8):
                    P.op("pe", lambda h, dc=dc, u=u, pi=pi, w=w: h.matmul(
                        self.psum[pi][:, :w], wo[:, u, dc * 128:(dc + 1) * 128], oT[:, u, :w],
                        start=(u == 0), stop=(u == 7)), reads=[wob, oTb], writes=[self.pb[pi]])
                P.op("dve", lambda h, dc=dc, pi=pi, c0=c0, w=w: h.scalar_tensor_tensor(
                    out=self.xT[:, dc, c0:c0 + w], in0=self.psum[pi][:, :w], scalar=G(dc),
                    in1=self.xT[:, dc, c0:c0 + w], op0=ALU.mult, op1=ALU.add),
                    reads=[self.pb[pi], self.mbuf], writes=[self.xbuf[ti]])


def blockify(w):
    n = w.shape[1]
    return np.ascontiguousarray(w.reshape(8, 128, n).transpose(1, 0, 2)).reshape(128, 8 * n)


def rope_np(pos, dim, theta):
    inv = (1.0 / (np.float32(theta) ** (np.arange(0, dim, 2, dtype=np.float32) / np.float32(dim)))).astype(np.float32)
    ang = pos.astype(np.float32)[:, None] * inv[None, :]
    return np.cos(ang).astype(np.float32), np.sin(ang).astype(np.float32)


def sigma_diff():
    s = np.arange(64)
    s[0:8] = np.arange(8, 16)
    s[8:16] = np.arange(0, 8)
    return s


def sigma_gqa():
    s = np.arange(128)
    s[0:32] = np.arange(32, 64)
    s[32:64] = np.arange(0, 32)
    s[64:96] = np.arange(96, 128)
    s[96:128] = np.arange(64, 96)
    return s


def rope_tabs(kind, pos):
    n = len(pos)
    if kind == "diff":
        cos, sin = rope_np(pos, 16, 500000.0)
        C = np.ones((64, n), np.float32)
        S = np.zeros((64, n), np.float32)
        C[0:8] = cos.T
        C[8:16] = cos.T
        S[0:8] = -sin.T
        S[8:16] = sin.T
        return np.ascontiguousarray(np.tile(C, (2, 1))), np.ascontiguousarray(np.tile(S, (2, 1)))
    cr, sr = rope_np(pos // 64, 64, 10000.0)
    cc, sc = rope_np(pos % 64, 64, 10000.0)
    C = np.concatenate([cr.T, cr.T, cc.T, cc.T], 0)
    S = np.concatenate([-sr.T, sr.T, -sc.T, sc.T], 0)
    return np.ascontiguousarray(C), np.ascontiguousarray(S)


def qkv_blocks(kind, inp):
    if kind == "diff":
        w = inp["diff_w_qkv"]
        q, k, v = w[:, :1024], w[:, 1024:2048], w[:, 2048:3072]
        sg = sigma_diff()
        hd = 64
    else:
        w = inp["gqa_w_qkv"]
        q, k, v = w[:, :1024], w[:, 1024:1280], w[:, 1280:1536]
        sg = sigma_gqa()
        hd = 128

    def sw(m):
        n = m.shape[1]
        idx = (np.arange(n) // hd) * hd + sg[np.arange(n) % hd]
        return m[:, idx]
    blocks = []
    for m in (q, k):
        ms = sw(m)
        for b in range(m.shape[1] // 256):
            blocks.append(blockify(np.concatenate([m[:, b * 256:(b + 1) * 256], ms[:, b * 256:(b + 1) * 256]], 1)))
    if kind == "diff":
        for b in range(2):
            blocks.append(blockify(np.ascontiguousarray(v[:, b * 512:(b + 1) * 512])))
    else:
        blocks.append(blockify(np.concatenate([v, np.zeros((1024, 256), np.float32)], 1)))
    return np.stack(blocks)


LAUNCHES = [
    dict(name="A", halo=0, stages=[("mod",), ("ffn", 0, 0)]),
    dict(name="B", halo=8, stages=[("pool", 0), ("ffn", 0, 2), ("ffn", 1, 0), ("qkv", "diff", 1)]),
    dict(name="C", halo=0, stages=[("attn", "diff", 1), ("ffn", 1, 2), ("ffn", 2, 0), ("qkv", "gqa", 2)]),
    dict(name="D", halo=0, stages=[("attn", "gqa", 2), ("ffn", 2, 2), ("ffn", 3, 0)]),
    dict(name="E", halo=1, stages=[("conv", 3), ("ffn", 3, 2), ("final",)]),
]


def run_pipeline(inp, B, S, CPBT, launches=LAUNCHES, same=True, trace=False, stop_after=None):
    inp = {k: np.asarray(v) for k, v in inp.items()}
    ncore = B * CPBT
    NT = S // CPBT
    xcur = [np.ascontiguousarray(inp["x"][c // CPBT, (c % CPBT) * NT:((c % CPBT) + 1) * NT, :].T)
            for c in range(ncore)]
    modtab = None
    qkv = None
    times = []
    for L in launches:
        H = L["halo"]
        stages = L["stages"]
        cfg = dict(ntok=NT, halo=H, stages=stages, nkeys=S, same=same, dbg=L.get("dbg", False), pool_ow=L.get("pool_ow", 496))
        bld = Builder(cfg)
        nc = bld.build()
        ffns = [s for s in stages if s[0] == "ffn"]
        shared = {}
        if ffns:
            shared["wgu"] = np.stack([relayout_wgu(inp["ffn_w_gu"][l, 0 if w == 0 else 1]) for _, l, w in ffns])
            shared["wdn"] = np.stack([inp["ffn_w_down"][l, 0 if w == 0 else 1] for _, l, w in ffns])
        for st in stages:
            if st[0] == "mod":
                shared["modw"] = np.ascontiguousarray(inp["mod_w"])
            if st[0] == "pool":
                shared["pool_w"] = np.ascontiguousarray(inp["pool_w"])
            if st[0] == "conv":
                wi = inp["conv_w_in"]
                shared["conv_win"] = np.stack([blockify(np.concatenate(
                    [wi[:, dc * 128:(dc + 1) * 128], wi[:, 1024 + dc * 128:1024 + (dc + 1) * 128],
                     wi[:, 2048 + dc * 128:2048 + (dc + 1) * 128]], 1)) for dc in range(8)])
                shared["conv_wout"] = np.ascontiguousarray(inp["conv_w_out"])
            if st[0] == "qkv":
                shared["wqkv"] = qkv_blocks(st[1], inp)
            if st[0] == "attn":
                shared["wo"] = np.ascontiguousarray(inp["diff_w_o"] if st[1] == "diff" else inp["gqa_w_o"])
                if st[1] == "diff":
                    shared["lam"] = np.ascontiguousarray(
                        np.broadcast_to(inp["diff_lambda"].reshape(1, 256), (128, 256)))
        maps = []
        for c in range(ncore):
            b, q = c // CPBT, c % CPBT
            m = dict(shared)
            cs = build_consts(inp, b)
            sg = sigma_gqa()
            cs[:, COFF["gqa_qg_sw"]] = inp["gqa_q_norm_g"][sg]
            cs[:, COFF["gqa_kg_sw"]] = inp["gqa_k_norm_g"][sg]
            cs[:, COFF["maskL"]] = 0.0 if q == 0 else 1.0
            cs[:, COFF["maskR"]] = 0.0 if q == CPBT - 1 else 1.0
            m["consts"] = cs
            xe = np.zeros((D, NT + 2 * H), np.float32)
            xe[:, H:H + NT] = xcur[c]
            if H:
                if q > 0:
                    xe[:, :H] = xcur[c - 1][:, NT - H:]
                if q < CPBT - 1:
                    xe[:, H + NT:] = xcur[c + 1][:, :H]
            m["xT"] = xe
            if modtab is not None:
                m["modtab_in"] = modtab[c]
            for st in stages:
                if st[0] == "pool":
                    t = q * NT + np.arange(-H, NT + H)
                    inv = np.ones((4, NT + 2 * H), np.float32)
                    for g, win in enumerate((2, 4, 8, 16)):
                        lo = np.clip(t - win // 2, 0, S)
                        hi = np.clip(t + win // 2, 0, S)
                        cnt = (hi - lo).astype(np.float32)
                        inv[g] = np.where(cnt > 0, np.float32(1.0) / np.maximum(cnt, 1), 1.0)
                    m["pool_inv"] = np.ascontiguousarray(np.broadcast_to(inv[None], (128, 4, NT + 2 * H)))
                if st[0] == "qkv":
                    pos = q * NT + np.arange(NT)
                    C, S_ = rope_tabs(st[1], pos)
                    m["ropeC"], m["ropeS"] = C, S_
                if st[0] == "attn":
                    m["q_in"] = qkv["q"][c]
                    m["k_in"] = qkv["kfull"][b]
                    m["v_in"] = qkv["vfull"][b]
            maps.append(m)
        res = run_bass_kernel_spmd(nc, maps, core_ids=list(range(ncore)), trace=trace)
        times.append(res.exec_time_ns)
        R = res.results
        xcur = [np.asarray(R[c]["xout"]) for c in range(ncore)]
        if any(s[0] == "mod" for s in stages):
            modtab = [np.asarray(R[c]["modtab_out"]) for c in range(ncore)]
        qs = [s for s in stages if s[0] == "qkv"]
        if qs:
            KB = min(2048, S)
            NBLK, CPB = S // KB, KB // 128
            kfull, vfull = [], []
            for b in range(B):
                kf = np.concatenate([np.asarray(R[b * CPBT + q]["k_out"]) for q in range(CPBT)], axis=1)
                vf = np.concatenate([np.asarray(R[b * CPBT + q]["v_out"]) for q in range(CPBT)], axis=0)
                VH = vf.shape[1] // 128
                vf = vf.reshape(NBLK, CPB, 128, VH, 128).transpose(3, 0, 2, 1, 4).reshape(VH, NBLK, 128, CPB * 128)
                kfull.append(np.ascontiguousarray(kf))
                vfull.append(np.ascontiguousarray(vf))
            qkv = dict(q=[np.asarray(R[c]["q_out"]) for c in range(ncore)], kfull=kfull, vfull=vfull)
        if stop_after == L["name"]:
            break
    out = np.zeros((B, S, D), np.float32)
    for c in range(ncore):
        b, q = c // CPBT, c % CPBT
        out[b, q * NT:(q + 1) * NT, :] = xcur[c].T
    return out, times


FUSED_STAGES = [("mod",), ("ffn", 0, 0), ("halo",), ("pool", 0), ("ffn", 0, 2), ("ffn", 1, 0), ("qkv", "diff", 1),
                ("gather", "diff"), ("attn", "diff", 1), ("ffn", 1, 2), ("ffn", 2, 0), ("qkv", "gqa", 2),
                ("gather", "gqa"), ("attn", "gqa", 2), ("ffn", 2, 2), ("ffn", 3, 0), ("halo",), ("conv", 3),
                ("ffn", 3, 2), ("final",)]


def run_fused(inp, B, S, CPBT, same=True, trace=False, stages=FUSED_STAGES):
    inp = {k: np.asarray(v) for k, v in inp.items()}
    ncore = B * CPBT
    NT = S // CPBT
    H = 8
    groups = [[b * CPBT + q for q in range(CPBT)] for b in range(B)]
    cfg = dict(ntok=NT, halo=H, stages=stages, nkeys=S, same=same, fused=True, cpbt=CPBT, groups=groups)
    bld = Builder(cfg)
    nc = bld.build()
    ffns = [s for s in stages if s[0] == "ffn"]
    shared = {}
    shared["wgu"] = np.stack([relayout_wgu(inp["ffn_w_gu"][l, 0 if w == 0 else 1]) for _, l, w in ffns])
    shared["wdn"] = np.stack([inp["ffn_w_down"][l, 0 if w == 0 else 1] for _, l, w in ffns])
    shared["modw"] = np.ascontiguousarray(inp["mod_w"])
    shared["pool_w"] = np.ascontiguousarray(inp["pool_w"])
    wi = inp["conv_w_in"]
    shared["conv_win"] = np.stack([blockify(np.concatenate(
        [wi[:, dc * 128:(dc + 1) * 128], wi[:, 1024 + dc * 128:1024 + (dc + 1) * 128],
         wi[:, 2048 + dc * 128:2048 + (dc + 1) * 128]], 1)) for dc in range(8)])
    shared["conv_wout"] = np.ascontiguousarray(inp["conv_w_out"])
    shared["wqkv_diff"] = qkv_blocks("diff", inp)
    shared["wqkv_gqa"] = qkv_blocks("gqa", inp)
    shared["wo_diff"] = np.ascontiguousarray(inp["diff_w_o"])
    shared["wo_gqa"] = np.ascontiguousarray(inp["gqa_w_o"])
    shared["lam"] = np.ascontiguousarray(np.broadcast_to(inp["diff_lambda"].reshape(1, 256), (128, 256)))
    maps = []
    sg = sigma_gqa()
    for c in range(ncore):
        b, q = c // CPBT, c % CPBT
        m = dict(shared)
        cs = build_consts(inp, b)
        cs[:, COFF["gqa_qg_sw"]] = inp["gqa_q_norm_g"][sg]
        cs[:, COFF["gqa_kg_sw"]] = inp["gqa_k_norm_g"][sg]
        cs[:, COFF["maskL"]] = 0.0 if q == 0 else 1.0
        cs[:, COFF["maskR"]] = 0.0 if q == CPBT - 1 else 1.0
        for r in range(min(4, CPBT)):
            cs[:, COFF["selL"] + r] = 1.0 if r == q - 1 else 0.0
            cs[:, COFF["selR"] + r] = 1.0 if r == q + 1 else 0.0
        m["consts"] = cs
        xe = np.zeros((D, NT + 2 * H), np.float32)
        xe[:, H:H + NT] = inp["x"][b, q * NT:(q + 1) * NT, :].T
        m["xT"] = xe
        t = q * NT + np.arange(-H, NT + H)
        inv = np.ones((4, NT + 2 * H), np.float32)
        for g, win in enumerate((2, 4, 8, 16)):
            lo = np.clip(t - win // 2, 0, S)
            hi = np.clip(t + win // 2, 0, S)
            cnt = (hi - lo).astype(np.float32)
            inv[g] = np.where(cnt > 0, np.float32(1.0) / np.maximum(cnt, 1), 1.0)
        m["pool_inv"] = np.ascontiguousarray(np.broadcast_to(inv[None], (128, 4, NT + 2 * H)))
        pos = q * NT + np.arange(NT)
        for kind in ("diff", "gqa"):
            C, S_ = rope_tabs(kind, pos)
            m["ropeC_" + kind], m["ropeS_" + kind] = C, S_
        maps.append(m)
    need = set()
    for st in stages:
        need.add(st[0] + ("_" + st[1] if st[0] in ("qkv", "attn") else ""))
    res = run_bass_kernel_spmd(nc, maps, core_ids=list(range(ncore)), trace=trace)
    out = np.zeros((B, S, D), np.float32)
    for c in range(ncore):
        b, q = c // CPBT, c % CPBT
        out[b, q * NT:(q + 1) * NT, :] = np.asarray(res.results[c]["xout"]).T
    return out, res.exec_time_ns


def kernel(**inputs):
    out, _ = run_fused(inputs, 2, SEQ, 4)
    return out
```
